# Optimizing a Trainium2 kernel written in Bass

```python
import jax, jax.numpy as jnp
from jax import lax
import numpy as np

D_MODEL = 1024
BATCH = 2
SEQ = 8192
DEPTH = 2
DEC_BATCH = 128
DEC_SEQ = 1
PAST_LEN = 8192
PAGE_SIZE = 128

N_MIXERS = 2
N_GLA_LAYERS = (DEPTH + 1) // 2
N_SWA_LAYERS = DEPTH // 2
D_FF = 4 * D_MODEL
NORM_EPS = 1e-6

GLA_HEADS = 4
GLA_DK = D_MODEL // 2
GLA_DV = D_MODEL
GLA_DK_HEAD = GLA_DK // GLA_HEADS
GLA_DV_HEAD = GLA_DV // GLA_HEADS
GLA_GATE_RANK = 16
GLA_TAU = 16.0
GLA_CHUNK = 64
GLA_IN = 2 * GLA_DK + 2 * GLA_DV + GLA_GATE_RANK

SWA_HEAD_DIM = 64
SWA_HEADS = D_MODEL // SWA_HEAD_DIM
SWA_KV_HEADS = 4
SWA_GROUP = SWA_HEADS // SWA_KV_HEADS
SWA_WINDOW = 128
SWA_BLOCK = 128
SWA_QKV = (SWA_HEADS + 2 * SWA_KV_HEADS) * SWA_HEAD_DIM
ROPE_THETA = 500000.0
ROPE_DIM = SWA_HEAD_DIM // 4

kernel_name = 'hybrid_gla_swa_sink_decoder_step'


def rms_norm(x, g):
    xf = x.astype(jnp.float32)
    y = xf * lax.rsqrt(jnp.mean(xf * xf, axis=-1, keepdims=True) + NORM_EPS)
    return (y * g.astype(jnp.float32)).astype(x.dtype)


def partial_rope(x, pos):
    half = ROPE_DIM // 2
    inv = jnp.power(ROPE_THETA, -jnp.arange(half, dtype=jnp.float32) * 2.0 / ROPE_DIM)
    ang = pos.astype(jnp.float32)[:, None] * inv[None, :]
    cos = jnp.cos(ang)[None, :, None, :]
    sin = jnp.sin(ang)[None, :, None, :]
    xf = x.astype(jnp.float32)
    x1 = xf[..., :half]
    x2 = xf[..., half:ROPE_DIM]
    out = jnp.concatenate([x1 * cos - x2 * sin, x2 * cos + x1 * sin, xf[..., ROPE_DIM:]], axis=-1)
    return out.astype(x.dtype)


def gla_recurrence(q, k, v, log_a, s0):
    b_, L = q.shape[:2]
    c = min(GLA_CHUNK, L)
    n = -(-L // c)
    pad = n * c - L
    q, k, v, log_a = [t.astype(jnp.float32) for t in (q, k, v, log_a)]
    if pad:
        pw = ((0, 0), (0, pad), (0, 0), (0, 0))
        q, k, v, log_a = [jnp.pad(t, pw) for t in (q, k, v, log_a)]
    q, k, v, log_a = [t.reshape(b_, n, c, *t.shape[2:]) for t in (q, k, v, log_a)]
    cum = jnp.cumsum(log_a, axis=2)
    total = cum[:, :, -1]
    q_dec = q * jnp.exp(cum)
    k_inv = k * jnp.exp(-cum)
    k_end = k * jnp.exp(total[:, :, None] - cum)
    causal = jnp.tril(jnp.ones((c, c), dtype=bool))
    att = jnp.where(causal, jnp.einsum('bnthd,bnshd->bnhts', q_dec, k_inv), 0.0)
    o_intra = jnp.einsum('bnhts,bnshv->bnthv', att, v)

    def step(s, xs):
        qd, ke, vv, tot = xs
        o = jnp.einsum('bthd,bhdv->bthv', qd, s)
        s = jnp.exp(tot)[..., None] * s + jnp.einsum('bshd,bshv->bhdv', ke, vv)
        return s, o

    xs = tuple(jnp.moveaxis(t, 1, 0) for t in (q_dec, k_end, v, total))
    s_fin, o_inter = lax.scan(step, s0.astype(jnp.float32), xs)
    o = o_intra + jnp.moveaxis(o_inter, 0, 1)
    o = o.reshape(b_, n * c, *o.shape[3:])[:, :L]
    return o, s_fin


def gla_mixer(x, s0, w_in, w_gate2, b_gate, g_head, w_out):
    b_, L, _ = x.shape
    h = x @ w_in
    q, k, v, r, z = jnp.split(h, [GLA_DK, 2 * GLA_DK, 2 * GLA_DK + GLA_DV, 2 * GLA_DK + 2 * GLA_DV], axis=-1)
    log_a = jax.nn.log_sigmoid((z @ w_gate2 + b_gate).astype(jnp.float32)) / GLA_TAU
    q = q.reshape(b_, L, GLA_HEADS, GLA_DK_HEAD) * (GLA_DK_HEAD ** -0.5)
    k = k.reshape(b_, L, GLA_HEADS, GLA_DK_HEAD)
    v = v.reshape(b_, L, GLA_HEADS, GLA_DV_HEAD)
    log_a = log_a.reshape(b_, L, GLA_HEADS, GLA_DK_HEAD)
    o, s_new = gla_recurrence(q, k, v, log_a, s0)
    o = rms_norm(o, g_head).astype(x.dtype).reshape(b_, L, GLA_DV) * jax.nn.silu(r)
    return o @ w_out, s_new


def swa_qkv(x, pos, w_qkv, b_qkv):
    b_, L, _ = x.shape
    h = x @ w_qkv + b_qkv
    q, k, v = jnp.split(h, [SWA_HEADS * SWA_HEAD_DIM, (SWA_HEADS + SWA_KV_HEADS) * SWA_HEAD_DIM], axis=-1)
    q = partial_rope(q.reshape(b_, L, SWA_HEADS, SWA_HEAD_DIM), pos)
    k = partial_rope(k.reshape(b_, L, SWA_KV_HEADS, SWA_HEAD_DIM), pos)
    v = v.reshape(b_, L, SWA_KV_HEADS, SWA_HEAD_DIM)
    return q, k, v


def sink_probs(scores, mask, sinks):
    s = jnp.where(mask, scores, -jnp.inf)
    sk = sinks.astype(jnp.float32).reshape(SWA_KV_HEADS, SWA_GROUP)[:, :, None, None]
    m = jnp.maximum(jnp.max(s, axis=-1, keepdims=True), sk)
    p = jnp.exp(s - m)
    return p / (jnp.sum(p, axis=-1, keepdims=True) + jnp.exp(sk - m))


def swa_attend_prompt(q, k, v, sinks):
    b_, L = q.shape[:2]
    nb = L // SWA_BLOCK
    qb = q.reshape(b_, nb, SWA_BLOCK, SWA_KV_HEADS, SWA_GROUP, SWA_HEAD_DIM)

    def with_prev(t):
        t = t.reshape(b_, nb, SWA_BLOCK, SWA_KV_HEADS, SWA_HEAD_DIM)
        prev = jnp.concatenate([jnp.zeros_like(t[:, :1]), t[:, :-1]], axis=1)
        return jnp.concatenate([prev, t], axis=2)

    kk, vv = with_prev(k), with_prev(v)
    scores = jnp.einsum('bnqkgd,bnskd->bnkgqs', qb, kk).astype(jnp.float32) * (SWA_HEAD_DIM ** -0.5)
    i = jnp.arange(SWA_BLOCK)[:, None]
    j = jnp.arange(2 * SWA_BLOCK)[None, :]
    diff = SWA_BLOCK + i - j
    key_pos = jnp.arange(nb)[:, None, None] * SWA_BLOCK - SWA_BLOCK + j[None]
    mask = (diff >= 0) & (diff <= SWA_WINDOW) & (key_pos >= 0)
    p = sink_probs(scores, mask[None, :, None, None], sinks)
    o = jnp.einsum('bnkgqs,bnskd->bnqkgd', p.astype(vv.dtype), vv)
    return o.reshape(b_, L, SWA_HEADS * SWA_HEAD_DIM)


def swa_attend_sample(q, k_new, v_new, buf_k, buf_v, sinks):
    db, T = q.shape[:2]
    w = buf_k.shape[1]
    kk = jnp.concatenate([buf_k, k_new], axis=1)
    vv = jnp.concatenate([buf_v, v_new], axis=1)
    qg = q.reshape(db, T, SWA_KV_HEADS, SWA_GROUP, SWA_HEAD_DIM)
    scores = jnp.einsum('bqkgd,bskd->bkgqs', qg, kk).astype(jnp.float32) * (SWA_HEAD_DIM ** -0.5)
    diff = w + jnp.arange(T)[:, None] - jnp.arange(w + T)[None, :]
    mask = (diff >= 0) & (diff <= SWA_WINDOW)
    p = sink_probs(scores, mask, sinks)
    o = jnp.einsum('bkgqs,bskd->bqkgd', p.astype(vv.dtype), vv)
    return o.reshape(db, T, SWA_HEADS * SWA_HEAD_DIM), kk[:, -w:], vv[:, -w:]


def sqrelu_mlp(x, w_up, w_down):
    return jnp.square(jax.nn.relu(x @ w_up)) @ w_down


def setup_inputs(seed: int = 0) -> dict:
    key = jax.random.key(seed)
    ks = jax.random.split(key, 24)
    f32 = jnp.float32

    def nrm(k, shape, scale):
        return jax.random.normal(k, shape, f32) * scale

    def gains(k, n, d):
        return 1.0 + 0.05 * jax.random.normal(k, (n, d), f32)

    win_buf = min(SWA_WINDOW, PAST_LEN)
    return {
        'x_prompt': nrm(ks[0], (BATCH, SEQ, D_MODEL), 1.0),
        'x_sample': nrm(ks[1], (DEC_BATCH, DEC_SEQ, D_MODEL), 1.0),
        'state_gla': nrm(ks[2], (N_GLA_LAYERS, DEC_BATCH, GLA_HEADS, GLA_DK_HEAD, GLA_DV_HEAD), 0.5),
        'cache_swa_k': nrm(ks[3], (N_SWA_LAYERS, DEC_BATCH, win_buf, SWA_KV_HEADS, SWA_HEAD_DIM), 1.0),
        'cache_swa_v': nrm(ks[4], (N_SWA_LAYERS, DEC_BATCH, win_buf, SWA_KV_HEADS, SWA_HEAD_DIM), 1.0),
        'gla_w_in': nrm(ks[5], (N_GLA_LAYERS, D_MODEL, GLA_IN), D_MODEL ** -0.5),
        'gla_w_gate2': nrm(ks[6], (N_GLA_LAYERS, GLA_GATE_RANK, GLA_DK), GLA_GATE_RANK ** -0.5),
        'gla_b_gate': nrm(ks[7], (N_GLA_LAYERS, GLA_DK), 0.1),
        'gla_g_head': gains(ks[8], N_GLA_LAYERS, GLA_DV_HEAD),
        'gla_w_out': nrm(ks[9], (N_GLA_LAYERS, GLA_DV, D_MODEL), GLA_DV ** -0.5),
        'swa_w_qkv': nrm(ks[10], (N_SWA_LAYERS, D_MODEL, SWA_QKV), D_MODEL ** -0.5),
        'swa_b_qkv': nrm(ks[11], (N_SWA_LAYERS, SWA_QKV), 0.02),
        'swa_sinks': nrm(ks[12], (N_SWA_LAYERS, SWA_HEADS), 1.0),
        'swa_w_out': nrm(ks[13], (N_SWA_LAYERS, SWA_HEADS * SWA_HEAD_DIM, D_MODEL), (SWA_HEADS * SWA_HEAD_DIM) ** -0.5),
        'swa_b_out': nrm(ks[14], (N_SWA_LAYERS, D_MODEL), 0.02),
        'norm_mix_pre': gains(ks[15], DEPTH, D_MODEL),
        'norm_mix_post': gains(ks[16], DEPTH, D_MODEL),
        'norm_ffn_pre': gains(ks[17], DEPTH, D_MODEL),
        'norm_ffn_post': gains(ks[18], DEPTH, D_MODEL),
        'ffn_w_up': nrm(ks[19], (DEPTH, D_MODEL, D_FF), D_MODEL ** -0.5),
        'ffn_w_down': nrm(ks[20], (DEPTH, D_FF, D_MODEL), D_FF ** -0.5),
    }


def reference(x_prompt, x_sample, state_gla, cache_swa_k, cache_swa_v,
              gla_w_in, gla_w_gate2, gla_b_gate, gla_g_head, gla_w_out,
              swa_w_qkv, swa_b_qkv, swa_sinks, swa_w_out, swa_b_out,
              norm_mix_pre, norm_mix_post, norm_ffn_pre, norm_ffn_post,
              ffn_w_up, ffn_w_down):
    b_p, L_p = x_prompt.shape[:2]
    T_s = x_sample.shape[1]
    pos_p = jnp.arange(L_p)
    pos_s = PAST_LEN + jnp.arange(T_s)
    hp, hs = x_prompt, x_sample
    gla_p, gla_s, kp_l, vp_l, ks_l, vs_l = [], [], [], [], [], []
    for layer in range(DEPTH):
        slot = layer // N_MIXERS
        ap = rms_norm(hp, norm_mix_pre[layer])
        a_s = rms_norm(hs, norm_mix_pre[layer])
        if layer % N_MIXERS == 0:
            s0 = jnp.zeros((b_p, GLA_HEADS, GLA_DK_HEAD, GLA_DV_HEAD), jnp.float32)
            mp, sp = gla_mixer(ap, s0, gla_w_in[slot], gla_w_gate2[slot], gla_b_gate[slot], gla_g_head[slot], gla_w_out[slot])
            ms, ss = gla_mixer(a_s, state_gla[slot], gla_w_in[slot], gla_w_gate2[slot], gla_b_gate[slot], gla_g_head[slot], gla_w_out[slot])
            gla_p.append(sp)
            gla_s.append(ss)
        else:
            qp, kp, vp = swa_qkv(ap, pos_p, swa_w_qkv[slot], swa_b_qkv[slot])
            op = swa_attend_prompt(qp, kp, vp, swa_sinks[slot])
            wp = min(SWA_WINDOW, L_p)
            kp_l.append(kp[:, -wp:])
            vp_l.append(vp[:, -wp:])
            qs, k_s, v_s = swa_qkv(a_s, pos_s, swa_w_qkv[slot], swa_b_qkv[slot])
            os_, nk, nv = swa_attend_sample(qs, k_s, v_s, cache_swa_k[slot], cache_swa_v[slot], swa_sinks[slot])
            ks_l.append(nk)
            vs_l.append(nv)
            mp = op @ swa_w_out[slot] + swa_b_out[slot]
            ms = os_ @ swa_w_out[slot] + swa_b_out[slot]
        hp = hp + rms_norm(mp, norm_mix_post[layer])
        hs = hs + rms_norm(ms, norm_mix_post[layer])
        hp = hp + rms_norm(sqrelu_mlp(rms_norm(hp, norm_ffn_pre[layer]), ffn_w_up[layer], ffn_w_down[layer]), norm_ffn_post[layer])
        hs = hs + rms_norm(sqrelu_mlp(rms_norm(hs, norm_ffn_pre[layer]), ffn_w_up[layer], ffn_w_down[layer]), norm_ffn_post[layer])
    return (hp, hs, jnp.stack(gla_p), jnp.stack(gla_s), jnp.stack(kp_l), jnp.stack(vp_l), jnp.stack(ks_l), jnp.stack(vs_l))
```

```python
import contextlib
import numpy as np
import concourse.bass as bass
import concourse.mybir as mybir
from concourse.bass_utils import run_bass_kernel_spmd

F32 = mybir.dt.float32
BF16 = mybir.dt.bfloat16
AF = mybir.ActivationFunctionType
ALU = mybir.AluOpType

NCORES = 8
D = 1024
SEQ = 8192
PAST_LEN = 8192
NS = 16
EPS = 1e-6
GIN = 3088
STAGE = 99
ENGS = ("pe", "act", "dve", "pool", "sp")


class Prog:
    def __init__(self, nc):
        self.nc = nc
        self.ops = []
        self.last_w = {}
        self.readers = {}
        self.dma_count = {}
        self.bar = set()

    def capture(self):
        self._cap = []

    def end_capture(self):
        lst = self._cap
        self._cap = None
        return lst

    def commit_merged(self, A, B):
        def banks_of(o):
            return {r for r in list(o[2]) + list(o[3]) if r.startswith("pb")}
        fut = [set() for _ in range(len(A) + 1)]
        for i in range(len(A) - 1, -1, -1):
            fut[i] = fut[i + 1] | banks_of(A[i])
        i = j = 0
        while i < len(A) or j < len(B):
            takeB = False
            if j < len(B):
                if i >= len(A):
                    takeB = True
                elif not (banks_of(B[j]) & fut[i]) and j * len(A) <= i * len(B):
                    takeB = True
            if takeB:
                self.op(*B[j]); j += 1
            else:
                self.op(*A[i]); i += 1

    def op(self, eng, fn, reads=(), writes=(), dma=None, inc=16, batch=False):
        if getattr(self, "_cap", None) is not None:
            self._cap.append((eng, fn, list(reads), list(writes), dma, inc, batch))
            return None
        idx = len(self.ops)
        deps = set(self.bar)
        pbr = [r for r in reads if r.startswith("pb")]
        if pbr:
            reads = [r for r in reads if not r.startswith("pb")]
            writes = list(writes) + pbr
        for r in reads:
            w = self.last_w.get(r)
            if w is not None:
                deps.add(w)
        for r in writes:
            w = self.last_w.get(r)
            if w is not None:
                deps.add(w)
            for rd in self.readers.get(r, ()):
                deps.add(rd)
        if eng == "pe" and dma is None:
            deps = {d for d in deps if not (self.ops[d]["eng"] == "pe" and self.ops[d]["dma"] is None)}
        if dma is not None and batch:
            deps = {d for d in deps if self.ops[d]["dma"] != dma}
        o = dict(eng=eng, fn=fn, deps=deps, dma=dma, signal=False, ev=None, inc=1)
        if dma is not None:
            n = self.dma_count.get(dma, 0) + 1
            self.dma_count[dma] = n
            o["ev"] = (("dma", dma), inc * n)
            o["signal"] = True
            o["inc"] = inc
            o["batch"] = batch
        self.ops.append(o)
        for r in reads:
            self.readers.setdefault(r, []).append(idx)
        for r in writes:
            self.last_w[r] = idx
            self.readers[r] = []
        return idx

    def barrier(self):
        allres = list(set(self.last_w) | set(self.readers))
        ids = []
        for e in ENGS:
            ids.append(self.op(e, lambda en: en.nop(), writes=allres + ["bar_" + e]))
        self.bar = set(ids)
        self.last_w = {}
        self.readers = {}

    def emit(self):
        nc = self.nc
        ops = self.ops
        for o in ops:
            for d in o["deps"]:
                ops[d]["signal"] = True
        for o in ops:
            if o["dma"] is not None and o.get("batch"):
                o["ev"] = (o["ev"][0], o["inc"] * self.dma_count[o["dma"]])
        cnt = {e: 0 for e in ENGS}
        for o in ops:
            if o["dma"] is None and o["signal"]:
                cnt[o["eng"]] += 1
                o["ev"] = (("eng", o["eng"]), cnt[o["eng"]])
        sem_keys = []
        for o in ops:
            if o["ev"] is not None and o["ev"][0] not in sem_keys:
                sem_keys.append(o["ev"][0])
        with contextlib.ExitStack() as st:
            sems = {}
            for k in sem_keys:
                sems[k] = st.enter_context(nc.semaphore("s_" + "_".join(str(x) for x in k)))
            block = st.enter_context(nc.Block())
            final = {}
            for o in ops:
                if o["dma"] is not None:
                    final[o["ev"][0]] = max(final.get(o["ev"][0], 0), o["ev"][1])

            def replay(engname, e):
                waited = {}
                for o in ops:
                    if o["eng"] != engname:
                        continue
                    need = {}
                    for d in o["deps"]:
                        k, v = ops[d]["ev"]
                        if v > need.get(k, 0):
                            need[k] = v
                    for k, v in need.items():
                        if waited.get(k, 0) < v:
                            e.wait_ge(sems[k], v)
                            waited[k] = v
                    ins = o["fn"](e)
                    if o["signal"]:
                        ins.then_inc(sems[o["ev"][0]], o["inc"])
                if engname == "sp":
                    for k, v in final.items():
                        if waited.get(k, 0) < v:
                            e.wait_ge(sems[k], v)
                            waited[k] = v

            @block.tensor
            def _(e):
                replay("pe", e)

            @block.scalar
            def _(e):
                replay("act", e)

            @block.vector
            def _(e):
                replay("dve", e)

            @block.gpsimd
            def _(e):
                replay("pool", e)

            @block.sync
            def _(e):
                replay("sp", e)


class _Done(Exception):
    pass


class Arena:
    def __init__(self, t, words):
        self.t = t
        self.words = words
        self.off = 0

    def reset(self):
        self.off = 0

    def alloc(self, shape, dtype, parts=None):
        parts = parts or shape[0]
        n = int(np.prod(shape[1:]))
        esz = 2 if dtype == BF16 else 4
        words = (n * esz + 3) // 4
        words = (words + 7) // 8 * 8
        assert self.off + words <= self.words, ("arena overflow", self.off, words, self.words)
        ap = self.t[0:parts, self.off:self.off + words]
        if dtype == BF16:
            ap = ap.bitcast(BF16)
        ap = ap[:, 0:n]
        self.off += words
        fs = shape[1:]
        if len(fs) == 2:
            ap = ap.rearrange("p (a b) -> p a b", a=fs[0])
        elif len(fs) == 3:
            ap = ap.rearrange("p (a b c) -> p a b c", a=fs[0], b=fs[1])
        return ap


def build(NT):
    box = {}
    try:
        _build(NT, box)
    except _Done:
        pass
    return box['nc']


def _build(NT, box):
    nc = bass.Bass("TRN2", target_bir_lowering=False)
    box["nc"] = nc
    NTT = NT + 1
    TOK = NT * 128

    def din(name, shape):
        return nc.dram_tensor(name, list(shape), F32, kind="ExternalInput").ap()

    def dout(name, shape):
        return nc.dram_tensor(name, list(shape), F32, kind="ExternalOutput").ap()

    xp = din("xp", [TOK, D]); xs = din("xs", [NS, D])
    sgla = din("sgla", [NS, 4, 128, 256]); ck = din("ck", [NS, 128, 256]); cv = din("cv", [NS, 128, 256])
    w_in = din("w_in", [D, GIN]); wg2 = din("wg2", [16, 512]); bg = din("bg", [1, 512]); ghead = din("ghead", [256])
    w_out0 = din("w_out0", [D, D]); wqkv = din("wqkv", [D, 1536]); bqkv = din("bqkv", [1, 1536])
    sinks = din("sinks", [16]); wo1 = din("wo1", [D, D]); bo1 = din("bo1", [1, D])
    nmp = din("nmp", [2, D]); nmpost = din("nmpost", [2, D]); nfp = din("nfp", [2, D]); nfpost = din("nfpost", [2, D])
    wup = din("wup", [2, D, 4096]); wdn = din("wdn", [2, 4096, D])
    c_ident = din("c_ident", [128, 128]); c_triI = din("c_triI", [128, 128]); c_triS = din("c_triS", [128, 128])
    c_mge = din("c_mge", [128, 128]); c_pm0 = din("c_pm0", [128, 128]); c_eye16 = din("c_eye16", [128, 16, 16])
    c_cos = din("c_cos", [NTT * 128, 8]); c_sin = din("c_sin", [NTT * 128, 8])
    c_cmask = din("c_cmask", [128, 4]); c_selprev = din("c_selprev", [128, 4])

    yp = dout("yp", [TOK, D]); ys = dout("ys", [NS, D])
    gp = dout("gp", [4, 128, 256]); gs = dout("gs", [NS, 4, 128, 256])
    kp = dout("kp", [128, 256]); vp = dout("vp", [128, 256])
    ks = dout("ks", [NS, 128, 256]); vs = dout("vs", [NS, 128, 256])

    ag1_in = nc.dram_tensor("ag1_in", [128, 1032], F32)
    ag1_out = nc.dram_tensor("ag1_out", [4 * 128, 1032], F32)
    ag2_in = nc.dram_tensor("ag2_in", [128, 512], F32)
    ag2_out = nc.dram_tensor("ag2_out", [4 * 128, 512], F32)
    groups = [[0, 1, 2, 3], [4, 5, 6, 7]]

    ARW = 31200
    with contextlib.ExitStack() as ctx:
        def sb(name, shape, dt):
            return ctx.enter_context(nc.sbuf_tensor(name, list(shape), dt))

        H = sb("H", [128, NTT, D], F32)
        G = sb("G", [128, 2, D], F32)
        identf = sb("identf", [128, 128], F32); identb = sb("identb", [128, 128], BF16)
        triI = sb("triI", [128, 128], F32); triS = sb("triS", [128, 128], F32)
        mle = sb("mle", [128, 128], BF16); mge = sb("mge", [128, 128], BF16); pm0 = sb("pm0", [128, 128], BF16)
        onesf = sb("onesf", [128, 128], F32); onesb = sb("onesb", [128, 128], BF16)
        eye16 = sb("eye16", [128, 16, 16], F32)
        cosT = sb("cosT", [128, NTT, 8], F32); sinT = sb("sinT", [128, NTT, 8], F32)
        cmask = sb("cmask", [128, 4], F32); selprev = sb("selprev", [128, 4], F32)
        stat = sb("stat", [128, 16], F32)
        junk = sb("junk", [128, D], F32)
        ctmp = sb("ctmp", [128, 128], F32)
        arena_t = sb("arena", [128, ARW], F32)
        AR = Arena(arena_t, ARW)
        banks = [ctx.enter_context(nc.psum_tensor("pb%d" % i, [128, 512], F32)) for i in range(8)]

        P = Prog(nc)
        uid = [0]

        def cut(v):
            if STAGE <= v:
                P.barrier()
                for t in range(NT):
                    P.op("sp", lambda e, t=t: e.dma_start(out=yp[t * 128:(t + 1) * 128, :], in_=H[:, t, :]), [], [], dma="dbg")
                P.op("sp", lambda e: e.dma_start(out=ys, in_=H[:NS, NT, :]), [], [], dma="dbg")
                P.emit()
                raise _Done()

        def bk(i):
            return "pb%d" % i

        def bf(i):
            return banks[i][:].bitcast(BF16)

        def ld(eng, out, in_, w, key, r=(), batch=False):
            P.op(eng, lambda e: e.dma_start(out=out, in_=in_), reads=r, writes=w, dma=key, batch=batch)

        ld("sp", identf[:], c_ident, ["identf"], "c0", batch=True)
        ld("sp", triI[:], c_triI, ["triI"], "c0", batch=True)
        ld("sp", triS[:], c_triS, ["triS"], "c0", batch=True)
        ld("sp", eye16[:], c_eye16, ["eye16"], "c0", batch=True)
        ld("sp", cosT[:], c_cos.rearrange("(t p) e -> p t e", p=128), ["cos"], "c0", batch=True)
        ld("sp", sinT[:], c_sin.rearrange("(t p) e -> p t e", p=128), ["sin"], "c0", batch=True)
        ld("sp", cmask[:], c_cmask, ["cmask"], "c0", batch=True)
        ld("sp", selprev[:], c_selprev, ["selprev"], "c0", batch=True)
        P.op("dve", lambda e: e.tensor_copy(out=identb[:], in_=identf[:]), ["identf"], ["identb"])
        P.op("dve", lambda e: e.tensor_copy(out=mle[:], in_=triI[:]), ["triI"], ["mle"])
        ld("sp", ctmp[:], c_mge, ["ctmp"], "c1")
        P.op("dve", lambda e: e.tensor_copy(out=mge[:], in_=ctmp[:]), ["ctmp"], ["mge"])
        ld("sp", ctmp[:], c_pm0, ["ctmp"], "c1")
        P.op("dve", lambda e: e.tensor_copy(out=pm0[:], in_=ctmp[:]), ["ctmp"], ["pm0"])
        P.op("pool", lambda e: e.memset(onesf[:], 1.0), [], ["onesf"])
        P.op("pool", lambda e: e.memset(onesb[:], 1.0), [], ["onesb"])

        def load_gain(slot, src_row):
            ld("sp", G[:, slot, :], src_row.partition_broadcast(128), ["G%d" % slot], "g%d" % slot)

        def rstd_from_ss(tn, ss_col, out_col, n, tag):
            P.op("act", lambda e: e.activation(out=stat[:tn, out_col:out_col + 1], in_=stat[:tn, ss_col:ss_col + 1],
                                               func=AF.Ln, scale=1.0 / n, bias=EPS),
                 ["st%d" % ss_col], ["st%d" % out_col])
            P.op("act", lambda e: e.activation(out=stat[:tn, out_col:out_col + 1], in_=stat[:tn, out_col:out_col + 1],
                                               func=AF.Exp, scale=-0.5),
                 ["st%d" % out_col], ["st%d" % out_col])

        def norm_stats(src, src_res, gslot, tn, xn, xn_res):
            P.op("act", lambda e: e.activation(out=junk[:tn, :], in_=src, func=AF.Square, accum_out=stat[:tn, 0:1]),
                 [src_res], ["junk", "st0"])
            rstd_from_ss(tn, 0, 1, D, "n")
            P.op("dve", lambda e: e.scalar_tensor_tensor(out=xn[:tn, :], in0=src, scalar=stat[:tn, 1:2],
                                                         in1=G[:tn, gslot, :], op0=ALU.mult, op1=ALU.mult),
                 [src_res, "st1", "G%d" % gslot], [xn_res])

        def norm_T(src, src_res, gslot, tn, xn, dstT, dst_res, tpbank):
            P.op("act", lambda e: e.activation(out=junk[:tn, :], in_=src, func=AF.Square, accum_out=stat[:tn, 0:1]),
                 [src_res], ["junk", "st0"])
            rstd_from_ss(tn, 0, 1, D, "n")
            P.op("dve", lambda e: e.scalar_tensor_tensor(out=xn[:tn, :], in0=src, scalar=stat[:tn, 1:2],
                                                         in1=G[:tn, gslot, :], op0=ALU.mult, op1=ALU.mult),
                 [src_res, "st1", "G%d" % gslot], ["xn"])
            transpose8(xn, "xn", tn, dstT, dst_res, tpbank)

        def transpose8(src, src_res, tn, dstT, dst_res, tpbank):
            tp = bf(tpbank)
            for k in range(8):
                P.op("pe", lambda e, k=k: e.transpose(out=tp[:, k * 128:k * 128 + tn], in_=src[:tn, k * 128:(k + 1) * 128],
                                                      identity=identb[:tn, :tn]),
                     [src_res, "identb"], [bk(tpbank)])
            P.op("act", lambda e: e.activation(out=dstT, in_=tp.rearrange("p (k t) -> p k t", k=8)[:, :, :tn], func=AF.Copy),
                 [bk(tpbank)], [dst_res])

        def proj(aT, aT_res, tn, W, W_res, c0, ncols, bank, col0=0, bias_row=None, bias_res=None):
            out = banks[bank][:tn, col0:col0 + ncols]
            for k in range(8):
                last = (k == 7 and bias_row is None)
                P.op("pe", lambda e, k=k, last=last: e.matmul(out, lhsT=aT[:, k, :tn], rhs=W[:, k, c0:c0 + ncols],
                                                                start=(k == 0), stop=last),
                     [aT_res, W_res], [bk(bank)])
            if bias_row is not None:
                P.op("pe", lambda e: e.matmul(out, lhsT=onesb[0:1, :tn], rhs=bias_row[0:1, c0:c0 + ncols],
                                              start=False, stop=True),
                     ["onesb", bias_res], [bk(bank)])

        def postnorm_residual(tn, mbanks, gslot, res_in, res_in_res, dst, dst_res, tmp, tmp_res):
            for i, b in enumerate(mbanks):
                P.op("act", lambda e, i=i, b=b: e.activation(out=junk[:tn, i * 512:(i + 1) * 512], in_=banks[b][:tn, :],
                                                             func=AF.Square, accum_out=stat[:tn, 2 + i:3 + i]),
                     [bk(b)], ["junk%d" % i, "st%d" % (2 + i)])
            P.op("dve", lambda e: e.tensor_tensor(out=stat[:tn, 4:5], in0=stat[:tn, 2:3], in1=stat[:tn, 3:4], op=ALU.add),
                 ["st2", "st3"], ["st4"])
            rstd_from_ss(tn, 4, 5, D, "p")
            for i, b in enumerate(mbanks):
                P.op("dve", lambda e, i=i, b=b: e.scalar_tensor_tensor(
                    out=tmp[:tn, i * 512:(i + 1) * 512], in0=banks[b][:tn, :], scalar=stat[:tn, 5:6],
                    in1=G[:tn, gslot, i * 512:(i + 1) * 512], op0=ALU.mult, op1=ALU.mult),
                     [bk(b), "st5", "G%d" % gslot], [tmp_res])
            P.op("pool", lambda e: e.tensor_tensor(out=dst, in0=res_in, in1=tmp[:tn, :], op=ALU.add),
                 [res_in_res, tmp_res], [dst_res])

        def Hres(t):
            return "H%d" % t

        def tile_rows(t):
            return 128 if t < NT else NS

        AR.reset()
        WIN = AR.alloc([128, 8, GIN], BF16)
        WO0 = AR.alloc([128, 8, D], BF16)
        wg2a = AR.alloc([17, 512], BF16, parts=17)
        gh = AR.alloc([128, D], F32)
        S = AR.alloc([128, D], F32)
        Sbf = AR.alloc([128, D], BF16)
        ltot = AR.alloc([128, 8], F32)
        xt = [AR.alloc([128, D], F32) for _ in range(2)]
        xn = AR.alloc([128, D], BF16)
        aT = AR.alloc([128, 8, 128], BF16)
        zTa = AR.alloc([17, 128], BF16, parts=17)
        ebuf = AR.alloc([128, 512], F32)
        _o = AR.off
        lbuf = AR.alloc([128, 512], F32)
        AR.off = _o
        wgtmp = AR.alloc([17, 512], F32, parts=17)
        AR.off = _o + 512
        sr2 = [AR.alloc([128, D], BF16) for _ in range(2)]
        sr = sr2[0]
        dcol = AR.alloc([128, 4], F32)
        E3 = ebuf
        U0 = AR.off
        E1 = AR.alloc([128, 512], F32); E2 = AR.alloc([128, 512], F32)
        qd = AR.alloc([128, 512], BF16); ki = AR.alloc([128, 512], BF16); ke = AR.alloc([128, 512], BF16)
        vb = AR.alloc([128, D], BF16)
        qkT = AR.alloc([128, 8, 128], BF16)
        attT = AR.alloc([128, 4, 128], BF16)
        on = AR.alloc([128, D], BF16)
        aT2 = [aT, AR.alloc([128, 8, 128], BF16)]
        ke2 = [ke, AR.alloc([128, 512], BF16)]
        qkT2 = [qkT, AR.alloc([128, 8, 128], BF16)]
        vb2 = [vb, AR.alloc([128, D], BF16)]
        dcol2 = [dcol, AR.alloc([128, 4], F32)]
        U1 = AR.off
        AR.off = U0
        cg = AR.alloc([128, 4, 1032], F32)
        U2 = AR.off
        AR.off = U0
        kf = AR.alloc([NS, 512], F32, parts=NS); vf = AR.alloc([NS, D], BF16, parts=NS)
        _oq = AR.off
        qf = AR.alloc([NS, 512], F32, parts=NS); af = AR.alloc([NS, 512], F32, parts=NS)
        _oe = AR.off
        AR.off = _oq
        S0_third = AR.alloc([128, D], F32)
        assert AR.off == _oe
        aqT = AR.alloc([128, 8, NS], F32)
        Qm = AR.alloc([128, 4, NS, NS], BF16)
        Km = AR.alloc([NS, 512], BF16, parts=NS)
        S0 = [AR.alloc([128, D], F32) for _ in range(2)] + [S0_third]
        Sn = S0
        Snb = [AR.alloc([128, D], BF16) for _ in range(2)]
        AR.off = max(AR.off, U1, U2)
        print("GLA arena words", AR.off)

        w_in_v = w_in.rearrange("(k p) n -> p k n", p=128)
        for (c0, c1, key, res) in [(512, 2048, "w0a", "WINa"), (3072, 3088, "w0z", "WINz"), (0, 512, "w0q", "WINq"), (2048, 3072, "w0r", "WINr")]:
            P.op("pool", lambda e, c0=c0, c1=c1: e.dma_start(out=WIN[:, :, c0:c1], in_=w_in_v[:, :, c0:c1]), [], [res], dma=key)
        P.op("pool", lambda e: e.dma_start(out=WO0, in_=w_out0.rearrange("(k p) n -> p k n", p=128)), [], ["WO0"], dma="w1")
        ld("sp", wgtmp[0:16, :], wg2, ["lbuf"], "c2", batch=True)
        ld("sp", wgtmp[16:17, :], bg, ["lbuf"], "c2", batch=True)
        P.op("dve", lambda e: e.tensor_copy(out=wg2a, in_=wgtmp), ["lbuf"], ["wg2a"])
        for h in range(4):
            ld("sp", gh[:, h * 256:(h + 1) * 256], ghead.partition_broadcast(128), ["gh"], "c2", batch=True)
        P.op("pool", lambda e: e.memset(zTa, 1.0), [], ["zTa"])
        P.op("pool", lambda e: e.memset(S, 0.0), [], ["S"])
        P.op("pool", lambda e: e.memset(ltot, 0.0), [], ["ltot"])
        load_gain(0, nmp[0])
        load_gain(1, nmpost[0])

        ZC = 3072

        def gla_front(t, pre):
            tn = tile_rows(t)
            sample = t >= NT
            p = t % 2
            xa = xt[p]
            aTp = aT2[p]; kep = ke2[p]; qkTp = qkT2[p]; vbp = vb2[p]; dcp = dcol2[p]
            RA = "aT%d" % p; RK = "ke%d" % p; RQ = "qkT%d" % p; RV = "vb%d" % p; RD = "dcol%d" % p
            src = xp[t * 128:(t + 1) * 128, :] if not sample else xs
            ld("sp", xa[:tn, :], src, ["xt%d" % p], "x%d" % p)
            xres = "xt%d" % p
            if pre:
                b4 = 4 * p
                TPB = b4; BZ = b4 + 1; BK = b4 + 2; BQ = b4 + 2; BV0 = b4; BV1 = b4 + 3; BD = [b4, b4 + 3]
            else:
                TPB = 7 if sample else 2
                BZ = 2; BK = 6; BQ = 5; BV0 = 3; BV1 = 4; BD = [0, 1]
            norm_T(xa[:tn, :], xres, 0, tn, xn, aTp[:, :, :tn], RA, TPB)
            zps = banks[BZ][0:16, 0:tn]
            for k in range(8):
                P.op("pe", lambda e, k=k: e.matmul(zps, lhsT=WIN[:, k, ZC:ZC + 16], rhs=aTp[:, k, :tn], start=(k == 0), stop=(k == 7)),
                     [RA, "WINz"], [bk(BZ)])
            P.op("act", lambda e: e.activation(out=zTa[0:16, :tn], in_=zps, func=AF.Copy), [bk(BZ)], ["zTa"])
            if not pre:
                proj(aTp, RA, tn, WIN, "WINq", 0, 512, BQ)
            proj(aTp, RA, tn, WIN, "WINa", 512, 512, BK)
            P.op("pe", lambda e: e.matmul(banks[BZ][:tn, :], lhsT=zTa[:, :tn], rhs=wg2a, start=True, stop=True),
                 ["zTa", "wg2a"], [bk(BZ)])
            P.op("act", lambda e: e.activation(out=ebuf[:tn, :], in_=banks[BZ][:tn, :], func=AF.Exp, scale=-1.0), [bk(BZ)], ["ebuf"])
            P.op("act", lambda e: e.activation(out=lbuf[:tn, :], in_=ebuf[:tn, :], func=AF.Ln, bias=1.0, scale=1.0), ["ebuf"], ["lbuf"])
            proj(aTp, RA, tn, WIN, "WINa", 1024, 512, BV0)
            proj(aTp, RA, tn, WIN, "WINa", 1536, 512, BV1)
            if not sample:
                P.op("dve", lambda e: e.tensor_copy(out=vbp[:tn, 0:512], in_=banks[BV0][:tn, :]), [bk(BV0)], [RV])
                P.op("dve", lambda e: e.tensor_copy(out=vbp[:tn, 512:1024], in_=banks[BV1][:tn, :]), [bk(BV1)], [RV])
            CB = 3
            RB = 4 if not pre else BV1
            if not sample:
                if not pre:
                    P.op("pe", lambda e: e.matmul(banks[CB][:tn, :], lhsT=triI[:tn, :tn], rhs=lbuf[:tn, :], start=True, stop=True),
                         ["triI", "lbuf"], [bk(CB)])
                P.op("pe", lambda e: e.matmul(banks[RB][:tn, :], lhsT=triS[:tn, :tn], rhs=lbuf[:tn, :], start=True, stop=True),
                     ["triS", "lbuf"], [bk(RB)])
                for h in range(4):
                    P.op("pe", lambda e, h=h: e.matmul(banks[BZ][:, 256 + h:257 + h], lhsT=lbuf[:tn, h * 128:(h + 1) * 128],
                                                       rhs=onesf[:tn, 0:1], start=True, stop=True),
                         ["lbuf", "onesf"], [bk(BZ)])
                P.op("act", lambda e: e.activation(out=dcp, in_=banks[BZ][:, 256:260], func=AF.Exp, scale=-1.0 / 16), [bk(BZ)], [RD])
                if pre:
                    P.op("dve", lambda e: e.tensor_tensor(out=ltot[:, 0:4], in0=ltot[:, 0:4], in1=banks[BZ][:, 256:260], op=ALU.add),
                         ["ltot", bk(BZ)], ["ltot"])
                P.op("act", lambda e: e.activation(out=E3[:tn, :], in_=banks[RB][:tn, :], func=AF.Exp, scale=-1.0 / 16), [bk(RB)], ["ebuf"])
                P.op("dve", lambda e: e.tensor_tensor(out=kep[:tn, :], in0=banks[BK][:tn, :], in1=E3[:tn, :], op=ALU.mult),
                     [bk(BK), "ebuf"], [RK])
                if not pre:
                    P.op("act", lambda e: e.activation(out=E1[:tn, :], in_=banks[CB][:tn, :], func=AF.Exp, scale=-1.0 / 16,
                                                       bias=float(np.log(128.0 ** -0.5))), [bk(CB)], ["E1"])
                    P.op("act", lambda e: e.activation(out=E2[:tn, :], in_=banks[CB][:tn, :], func=AF.Exp, scale=1.0 / 16), [bk(CB)], ["E2"])
                    P.op("dve", lambda e: e.tensor_tensor(out=qd[:tn, :], in0=banks[5][:tn, :], in1=E1[:tn, :], op=ALU.mult),
                         [bk(5), "E1"], ["qd"])
                    P.op("dve", lambda e: e.tensor_tensor(out=ki[:tn, :], in0=banks[6][:tn, :], in1=E2[:tn, :], op=ALU.mult),
                         [bk(6), "E2"], ["ki"])
            else:
                P.op("dve", lambda e: e.tensor_copy(out=vf[:, 0:512], in_=banks[3][:tn, :]), [bk(3)], ["vf"])
                P.op("dve", lambda e: e.tensor_copy(out=vf[:, 512:1024], in_=banks[4][:tn, :]), [bk(4)], ["vf"])
                P.op("act", lambda e: e.activation(out=af, in_=lbuf[:tn, :], func=AF.Exp, scale=-1.0 / 16), ["lbuf"], ["af"])
                P.op("act", lambda e: e.activation(out=qf, in_=banks[5][:tn, :], func=AF.Copy, scale=float(128.0 ** -0.5)), [bk(5)], ["qf"])
                P.op("dve", lambda e: e.tensor_copy(out=kf, in_=banks[6][:tn, :]), [bk(6)], ["kf"])
            if not pre:
                srp = sr2[p]; RS = "sr%d" % p
                proj(aTp, RA, tn, WIN, "WINr", 2048, 512, 3)
                proj(aTp, RA, tn, WIN, "WINr", 2560, 512, 4)
                if not sample:
                    tp = bf(TPB)
                    for h in range(4):
                        P.op("pe", lambda e, h=h: e.transpose(out=tp[:, h * 128:(h + 1) * 128], in_=qd[:tn, h * 128:(h + 1) * 128],
                                                              identity=identb[:tn, :tn]), ["qd", "identb"], [bk(TPB)])
                    for h in range(4):
                        P.op("pe", lambda e, h=h: e.transpose(out=tp[:, (4 + h) * 128:(5 + h) * 128], in_=ki[:tn, h * 128:(h + 1) * 128],
                                                              identity=identb[:tn, :tn]), ["ki", "identb"], [bk(TPB)])
                    P.op("act", lambda e: e.activation(out=qkTp, in_=tp.rearrange("p (k t) -> p k t", k=8), func=AF.Copy), [bk(TPB)], [RQ])
                P.op("act", lambda e: e.activation(out=srp[:tn, 0:512], in_=banks[3][:tn, :], func=AF.Silu), [bk(3)], [RS])
                P.op("act", lambda e: e.activation(out=srp[:tn, 512:1024], in_=banks[4][:tn, :], func=AF.Silu), [bk(4)], [RS])
                P.op("pool", lambda e: e.tensor_tensor(out=srp[:tn, :], in0=srp[:tn, :], in1=gh[:tn, :], op=ALU.mult), [RS, "gh"], [RS])
            if pre:
                for h in range(4):
                    P.op("pe", lambda e, h=h: e.matmul(banks[BD[h // 2]][:, (h % 2) * 256:(h % 2 + 1) * 256],
                                                       lhsT=kep[:tn, h * 128:(h + 1) * 128], rhs=vbp[:tn, h * 256:(h + 1) * 256],
                                                       start=True, stop=True), [RK, RV], [bk(BD[h // 2])])
                for h in range(4):
                    P.op("dve", lambda e, h=h: e.scalar_tensor_tensor(
                        out=S[:, h * 256:(h + 1) * 256], in0=S[:, h * 256:(h + 1) * 256], scalar=dcp[:, h:h + 1],
                        in1=banks[BD[h // 2]][:, (h % 2) * 256:(h % 2 + 1) * 256], op0=ALU.mult, op1=ALU.add),
                         ["S", RD, bk(BD[h // 2])], ["S"])

        def gla_back(t, part):
            tn = tile_rows(t)
            sample = t >= NT
            p = t % 2
            xa = xt[p]; xres = "xt%d" % p
            aTp = aT2[p]; kep = ke2[p]; qkTp = qkT2[p]; vbp = vb2[p]; dcp = dcol2[p]
            RA = "aT%d" % p; RK = "ke%d" % p; RQ = "qkT%d" % p; RV = "vb%d" % p; RD = "dcol%d" % p
            gsr = sr2[p]; RS = "sr%d" % p
            if part == 1:
                if not sample:
                    for h in range(4):
                        P.op("pe", lambda e, h=h: e.matmul(banks[2][:, h * 128:(h + 1) * 128], lhsT=qkTp[:, 4 + h, :], rhs=qkTp[:, h, :],
                                                           start=True, stop=True), [RQ], [bk(2)])
                    P.op("dve", lambda e: e.tensor_tensor(out=attT, in0=banks[2][:, :].rearrange("p (h t) -> p h t", h=4),
                                                          in1=mle[:].unsqueeze(1).broadcast_to([128, 4, 128]), op=ALU.mult),
                         [bk(2), "mle"], ["attT"])
                    for h in range(4):
                        ob = banks[h // 2][:, (h % 2) * 256:(h % 2 + 1) * 256]
                        P.op("pe", lambda e, h=h, ob=ob: e.matmul(ob, lhsT=attT[:, h, :], rhs=vbp[:, h * 256:(h + 1) * 256], start=True, stop=False),
                             ["attT", RV], [bk(h // 2)])
                        P.op("pe", lambda e, h=h, ob=ob: e.matmul(ob, lhsT=qkTp[:, h, :], rhs=Sbf[:, h * 256:(h + 1) * 256], start=False, stop=True),
                             [RQ, "Sbf"], [bk(h // 2)])
                    for h in range(4):
                        P.op("pe", lambda e, h=h: e.matmul(banks[3 + h // 2][:, (h % 2) * 256:(h % 2 + 1) * 256],
                                                           lhsT=kep[:, h * 128:(h + 1) * 128], rhs=vbp[:, h * 256:(h + 1) * 256],
                                                           start=True, stop=True), [RK, RV], [bk(3 + h // 2)])
                    for h in range(4):
                        P.op("dve", lambda e, h=h: e.scalar_tensor_tensor(
                            out=S[:, h * 256:(h + 1) * 256], in0=S[:, h * 256:(h + 1) * 256], scalar=dcp[:, h:h + 1],
                            in1=banks[3 + h // 2][:, (h % 2) * 256:(h % 2 + 1) * 256], op0=ALU.mult, op1=ALU.add),
                             ["S", RD, bk(3 + h // 2)], ["S"])
                    P.op("pool", lambda e: e.tensor_copy(out=Sbf, in_=S), ["S"], ["Sbf"])
                else:
                    gla_sample_state()
                for h in range(4):
                    P.op("act", lambda e, h=h: e.activation(out=junk[:tn, h * 256:(h + 1) * 256],
                                                            in_=banks[h // 2][:tn, (h % 2) * 256:(h % 2 + 1) * 256],
                                                            func=AF.Square, accum_out=stat[:tn, 8 + h:9 + h]),
                         [bk(h // 2)], ["junkh%d" % h, "st%d" % (8 + h)])
                P.op("act", lambda e: e.activation(out=stat[:tn, 12:16], in_=stat[:tn, 8:12], func=AF.Ln, scale=1.0 / 256, bias=EPS),
                     ["st8", "st9", "st10", "st11"], ["st12"])
                P.op("act", lambda e: e.activation(out=stat[:tn, 12:16], in_=stat[:tn, 12:16], func=AF.Exp, scale=-0.5), ["st12"], ["st12"])
                for h in range(4):
                    P.op("dve", lambda e, h=h: e.scalar_tensor_tensor(
                        out=on[:tn, h * 256:(h + 1) * 256], in0=banks[h // 2][:tn, (h % 2) * 256:(h % 2 + 1) * 256],
                        scalar=stat[:tn, 12 + h:13 + h], in1=gsr[:tn, h * 256:(h + 1) * 256], op0=ALU.mult, op1=ALU.mult),
                         [bk(h // 2), "st12", RS], ["on"] + (["S0_0", "S0_1", "S0_2"] if sample else []))
                return
            onT = aTp
            transpose8(on, "on", tn, onT[:, :, :tn], RA, 7)
            proj(onT, RA, tn, WO0, "WO0", 0, 512, 0)
            proj(onT, RA, tn, WO0, "WO0", 512, 512, 1)
            postnorm_residual(tn, [0, 1], 1, xa[:tn, :], xres, H[:tn, t, :], Hres(t), H[:, t, :], Hres(t))

        def gla_tile(t, pre):
            gla_front(t, pre)
            if not pre:
                gla_back(t, 1)
                gla_back(t, 2)

        def gla_sample_state():
            tn = NS
            pt = banks[2][:, 0:8 * NS].rearrange("p (k n) -> p k n", k=8)
            for h in range(4):
                P.op("pe", lambda e, h=h: e.transpose(out=pt[:, h, :], in_=af[:, h * 128:(h + 1) * 128], identity=identf[:tn, :tn]),
                     ["af", "identf"], [bk(2)])
            for h in range(4):
                P.op("pe", lambda e, h=h: e.transpose(out=pt[:, 4 + h, :], in_=qf[:, h * 128:(h + 1) * 128], identity=identf[:tn, :tn]),
                     ["qf", "identf"], [bk(2)])
            P.op("dve", lambda e: e.tensor_copy(out=aqT, in_=pt), [bk(2)], ["aqT"])
            P.op("dve", lambda e: e.tensor_tensor(out=Qm, in0=aqT[:, 4:8, :].unsqueeze(3).broadcast_to([128, 4, NS, NS]),
                                                  in1=eye16[:].unsqueeze(1).broadcast_to([128, 4, NS, NS]), op=ALU.mult),
                 ["aqT", "eye16"], ["Qm"])
            for n in range(NS):
                s0 = S0[n % 3]; sn = Sn[n % 3]
                ld("sp", s0.rearrange("p (h v) -> p h v", h=4), sgla[n].rearrange("h p v -> p h v"), ["S0_%d" % (n % 3)] + (["qf", "af"] if n % 3 == 2 else []), "s0_%d" % (n % 3))
                P.op("dve", lambda e, n=n: e.tensor_scalar(out=Km, in0=kf, scalar1=identf[:NS, n:n + 1],
                                                           scalar2=None, op0=ALU.mult), ["kf", "identf"], ["Km"])
                for h in range(4):
                    P.op("pe", lambda e, h=h: e.matmul(banks[3 + h // 2][:, (h % 2) * 256:(h % 2 + 1) * 256],
                                                       lhsT=Km[:, h * 128:(h + 1) * 128], rhs=vf[:, h * 256:(h + 1) * 256],
                                                       start=True, stop=True), ["Km", "vf"], [bk(3 + h // 2)])
                for h in range(4):
                    P.op("dve", lambda e, h=h, n=n, s0=s0, sn=sn: e.scalar_tensor_tensor(
                        out=sn[:, h * 256:(h + 1) * 256], in0=s0[:, h * 256:(h + 1) * 256], scalar=aqT[:, h, n:n + 1],
                        in1=banks[3 + h // 2][:, (h % 2) * 256:(h % 2 + 1) * 256], op0=ALU.mult, op1=ALU.add),
                         ["S0_%d" % (n % 3), "aqT", bk(3 + h // 2)], ["S0_%d" % (n % 3)])
                P.op("act", lambda e, n=n, sn=sn: e.dma_start(out=gs[n].rearrange("h p v -> p h v"), in_=sn.rearrange("p (h v) -> p h v", h=4)),
                     ["S0_%d" % (n % 3)], [], dma="so_%d" % (n % 3))
                snb = Snb[n % 2]
                P.op("act", lambda e, sn=sn, snb=snb: e.activation(out=snb, in_=sn, func=AF.Copy), ["S0_%d" % (n % 3)], ["Snb%d" % (n % 2)])
                for h in range(4):
                    P.op("pe", lambda e, h=h, n=n, sn=sn: e.matmul(banks[h // 2][:tn, (h % 2) * 256:(h % 2 + 1) * 256],
                                                                   lhsT=Qm[:, h, n, :], rhs=Snb[n % 2][:, h * 256:(h + 1) * 256],
                                                                   start=(n == 0 and h % 2 == 0), stop=(n == NS - 1),
                                                                   skip_group_check=True),
                         ["Qm", "Snb%d" % (n % 2)], [bk(h // 2)])

        def eyecol(n):
            return dcolsel[:NS, n:n + 1]

        dcolsel = identf

        pre_lists = []
        for t in range(NT):
            P.capture()
            gla_front(t, True)
            pre_lists.append(P.end_capture())
        ksp = [int(len(l) * 0.55) for l in pre_lists]
        for o_ in pre_lists[0][:ksp[0]]:
            P.op(*o_)
        for t in range(NT):
            A_ = pre_lists[t][ksp[t]:]
            B_ = pre_lists[t + 1][:ksp[t + 1]] if t + 1 < NT else []
            P.commit_merged(A_, B_)
        P.barrier()
        P.op("sp", lambda e: e.dma_start(out=ag1_in.ap()[:, 0:1024], in_=S), ["S"], ["ag1_in"], dma="ag1w")
        P.op("sp", lambda e: e.dma_start(out=ag1_in.ap()[:, 1024:1032], in_=ltot), ["ltot"], ["ag1_in"], dma="ag1w")
        P.op("pool", lambda e: e.collective_compute("AllGather", ALU.bypass, replica_groups=groups,
                                                    ins=[ag1_in.ap().opt()], outs=[ag1_out.ap().opt()]),
             ["ag1_in"], ["ag1_out"], dma="cc1", inc=1)
        gla_tile(NT, False)
        P.barrier()
        P.op("sp", lambda e: e.dma_start(out=cg, in_=ag1_out.ap().rearrange("(r p) n -> p r n", p=128)), ["ag1_out"], ["cg"], dma="ag1r")
        P.op("pool", lambda e: e.memset(S, 0.0), [], ["S"])
        for j in range(4):
            P.op("dve", lambda e, j=j: e.tensor_scalar(out=stat[:, 8:12], in0=cg[:, j, 1024:1028], scalar1=cmask[:, j:j + 1], scalar2=None,
                                                       op0=ALU.mult), ["cg", "cmask"], ["st8"])
            P.op("act", lambda e: e.activation(out=stat[:, 12:16], in_=stat[:, 8:12], func=AF.Exp, scale=-1.0 / 16), ["st8"], ["st12"])
            P.op("dve", lambda e, j=j: e.tensor_scalar(out=junk[:, :], in0=cg[:, j, 0:1024], scalar1=cmask[:, j:j + 1], scalar2=None,
                                                       op0=ALU.mult), ["cg", "cmask"], ["junk"])
            for h in range(4):
                P.op("dve", lambda e, h=h: e.scalar_tensor_tensor(
                    out=S[:, h * 256:(h + 1) * 256], in0=S[:, h * 256:(h + 1) * 256], scalar=stat[:, 12 + h:13 + h],
                    in1=junk[:, h * 256:(h + 1) * 256], op0=ALU.mult, op1=ALU.add), ["S", "st12", "junk"], ["S"])
        P.op("act", lambda e: e.activation(out=Sbf, in_=S, func=AF.Copy), ["S"], ["Sbf"])
        P.barrier()
        gla_front(0, False)
        for t in range(NT):
            P.capture()
            gla_back(t, 1)
            gla_back(t, 2)
            A_ = P.end_capture()
            B_ = []
            if t + 1 < NT:
                P.capture()
                gla_front(t + 1, False)
                B_ = P.end_capture()
            P.commit_merged(A_, B_)
        P.op("sp", lambda e: e.dma_start(out=gp.rearrange("h p v -> p h v"), in_=S.rearrange("p (h v) -> p h v", h=4)), ["S"], [], dma="gpo")

        if STAGE <= 1:
            for t in range(NT):
                P.op("sp", lambda e, t=t: e.dma_start(out=yp[t * 128:(t + 1) * 128, :], in_=H[:, t, :]), [Hres(t)], [], dma="dbg")
            P.op("sp", lambda e: e.dma_start(out=ys, in_=H[:NS, NT, :]), [Hres(NT)], [], dma="dbg")
            P.emit()
            return nc


        def postnorm_sb(tn, srcs, src_res, gslot, dst, dst_res, tmp, tmp_res):
            for i, sap in enumerate(srcs):
                P.op("act", lambda e, i=i, sap=sap: e.activation(out=junk[:tn, i * 512:(i + 1) * 512], in_=sap,
                                                                 func=AF.Square, accum_out=stat[:tn, 2 + i:3 + i]),
                     [src_res], ["junk%d" % i, "st%d" % (2 + i)])
            P.op("dve", lambda e: e.tensor_tensor(out=stat[:tn, 4:5], in0=stat[:tn, 2:3], in1=stat[:tn, 3:4], op=ALU.add),
                 ["st2", "st3"], ["st4"])
            rstd_from_ss(tn, 4, 5, D, "p")
            for i, sap in enumerate(srcs):
                P.op("dve", lambda e, i=i, sap=sap: e.scalar_tensor_tensor(
                    out=tmp[:tn, i * 512:(i + 1) * 512], in0=sap, scalar=stat[:tn, 5:6],
                    in1=G[:tn, gslot, i * 512:(i + 1) * 512], op0=ALU.mult, op1=ALU.mult),
                     [src_res, "st5", "G%d" % gslot], [tmp_res])
            P.op("dve", lambda e: e.tensor_tensor(out=dst, in0=dst, in1=tmp[:tn, :], op=ALU.add),
                 [dst_res, tmp_res], [dst_res])

        def ffn_layer(l, final):
            P.barrier()
            AR.reset()
            half = NT // 2
            passes = [list(range(0, half)), list(range(half, NTT))]
            NPT = max(len(p) for p in passes)
            acc = AR.alloc([128, NPT, D], F32)
            aTf = AR.alloc([128, 8, max(1, len(passes[0])) * 128], BF16)
            aTf2 = AR.alloc([128, 8, NPT * 128], BF16)
            xn_f2 = AR.alloc([128, D], BF16)
            wu = [AR.alloc([128, 8, 512], BF16) for _ in range(2)]
            wd = [AR.alloc([128, 4, D], BF16) for _ in range(2)]
            uT = [AR.alloc([128, 4, 512], BF16) for _ in range(2)]
            rT = [AR.alloc([128, 512], BF16) for _ in range(2)]
            xn_f = AR.alloc([128, D], BF16)
            ftmp = AR.alloc([128, D], F32)
            load_gain(0, nfp[l])
            load_gain(1, nfpost[l])
            wup_v = wup[l].rearrange("(k p) n -> p k n", p=128)
            wdn_v = wdn[l].rearrange("(f p) n -> p f n", p=128)
            ucount = [0]
            dcount = [0]
            passes = [p for p in passes if p]
            aTfs = [aTf, aTf2]
            offs_p = []
            groups_p = []
            for tiles in passes:
                offs = {}
                o = 0
                for t in tiles:
                    offs[t] = o
                    o += tile_rows(t)
                offs_p.append(offs)
                tot = sum(tile_rows(t) for t in tiles)
                n_g = -(-tot // 512)
                groups_ = []
                i0 = 0
                for gi in range(n_g):
                    cnt = len(tiles) // n_g + (1 if gi < len(tiles) % n_g else 0)
                    groups_.append(tiles[i0:i0 + cnt]); i0 += cnt
                groups_ = [g for g in groups_ if g]
                assert all(sum(tile_rows(t) for t in g) <= 512 for g in groups_)
                groups_p.append(groups_)
            items = [(p, blk, g) for p in range(len(passes)) for blk in range(8) for g in groups_p[p]]

            def emit_norm(p):
                for t in passes[p]:
                    tn = tile_rows(t)
                    o = offs_p[p][t]
                    norm_T(H[:tn, t, :], Hres(t), 0, tn, xn_f, aTfs[p % 2][:, :, o:o + tn], "aTf%d" % (p % 2), 7)

            def wslot(p, blk):
                return (p * 8 + blk) % 2

            def load_w(p, blk):
                slot = wslot(p, blk)
                P.op("pool", lambda e, slot=slot, blk=blk: e.dma_start(out=wu[slot], in_=wup_v[:, :, blk * 512:(blk + 1) * 512]),
                     [], ["wu%d" % slot], dma="wu%d" % slot)
                P.op("pool", lambda e, slot=slot, blk=blk: e.dma_start(out=wd[slot], in_=wdn_v[:, blk * 4:(blk + 1) * 4, :]),
                     [], ["wd%d" % slot], dma="wd%d" % slot)

            def emit_U(it, idx):
                p, blk, g = it
                slot = wslot(p, blk)
                us = idx % 2
                g0 = offs_p[p][g[0]]
                gw = sum(tile_rows(t) for t in g)
                aT_ = aTfs[p % 2]
                for f in range(4):
                    ub = ucount[0] % 3
                    ucount[0] += 1
                    for k in range(8):
                        P.op("pe", lambda e, k=k, f=f, ub=ub, slot=slot, g0=g0, gw=gw, aT_=aT_: e.matmul(
                            banks[ub][:, :gw], lhsT=wu[slot][:, k, f * 128:(f + 1) * 128], rhs=aT_[:, k, g0:g0 + gw],
                            start=(k == 0), stop=(k == 7)), ["wu%d" % slot, "aTf%d" % (p % 2)], [bk(ub)])
                    P.op("act", lambda e, f=f, ub=ub, gw=gw: e.activation(out=rT[f % 2][:, :gw], in_=banks[ub][:, :gw], func=AF.Relu),
                         [bk(ub)], ["rT%d" % (f % 2)])
                    P.op("dve", lambda e, f=f, us=us, gw=gw: e.tensor_tensor(out=uT[us][:, f, :gw], in0=rT[f % 2][:, :gw], in1=rT[f % 2][:, :gw], op=ALU.mult),
                         ["rT%d" % (f % 2)], ["uT%d_%d" % (us, f)])

            def emit_D(it, idx):
                p, blk, g = it
                slot = wslot(p, blk)
                us = idx % 2
                g0 = offs_p[p][g[0]]
                for t in g:
                    tn = tile_rows(t)
                    o_ = offs_p[p][t] - g0
                    j = passes[p].index(t)
                    for hf in range(2):
                        db = 3 + dcount[0] % 4
                        dcount[0] += 1
                        for f in range(4):
                            P.op("pe", lambda e, f=f, db=db, us=us, o_=o_, tn=tn, slot=slot, hf=hf: e.matmul(
                                banks[db][:tn, :], lhsT=uT[us][:, f, o_:o_ + tn], rhs=wd[slot][:, f, hf * 512:(hf + 1) * 512],
                                start=(f == 0), stop=(f == 3)), ["uT%d_%d" % (us, f), "wd%d" % slot], [bk(db)])
                        if blk == 0:
                            P.op("dve", lambda e, db=db, tn=tn, j=j, hf=hf: e.tensor_copy(out=acc[:tn, j, hf * 512:(hf + 1) * 512], in_=banks[db][:tn, :]),
                                 [bk(db)], ["acc%d" % j])
                        else:
                            P.op("dve", lambda e, db=db, tn=tn, j=j, hf=hf: e.tensor_tensor(out=acc[:tn, j, hf * 512:(hf + 1) * 512],
                                                                                        in0=acc[:tn, j, hf * 512:(hf + 1) * 512], in1=banks[db][:tn, :], op=ALU.add),
                                 [bk(db), "acc%d" % j], ["acc%d" % j])
                    if blk == 7:
                        emit_post(p, [t])

            def emit_post(p, only=None):
                for j, t in enumerate(passes[p]):
                    if only is not None and t not in only:
                        continue
                    tn = tile_rows(t)
                    postnorm_sb(tn, [acc[:tn, j, 0:512], acc[:tn, j, 512:1024]], "acc%d" % j, 1, H[:tn, t, :], Hres(t), ftmp, "ftmp")
                    if final:
                        if t < NT:
                            P.op("sp", lambda e, t=t: e.dma_start(out=yp[t * 128:(t + 1) * 128, :], in_=H[:, t, :]), [Hres(t)], [], dma="yo")
                        else:
                            P.op("sp", lambda e: e.dma_start(out=ys, in_=H[:NS, NT, :]), [Hres(NT)], [], dma="yo")

            def emit_norm_group(p, g):
                for t in g:
                    tn = tile_rows(t)
                    o = offs_p[p][t]
                    norm_T(H[:tn, t, :], Hres(t), 0, tn, xn_f, aTfs[p % 2][:, :, o:o + tn], "aTf%d" % (p % 2), 7)
            emit_norm_group(0, groups_p[0][0])
            first_rest = list(groups_p[0][1:])
            loaded = set()
            normed = {0}
            n_items_p = [8 * len(groups_p[p]) for p in range(len(passes))]
            start_p = [sum(n_items_p[:p]) for p in range(len(passes))]
            for idx, it in enumerate(items):
                p = it[0]
                for la in (0, 1):
                    if idx + la < len(items):
                        key = items[idx + la][:2]
                        if key not in loaded:
                            load_w(*key); loaded.add(key)
                if idx == 0:
                    emit_U(it, idx)
                    for g_ in first_rest:
                        emit_norm_group(0, g_)
                if p + 1 < len(passes):
                    q_ = p + 1
                    i_ = idx - start_p[p] - 1
                    tl_ = passes[q_]
                    if 0 <= i_ - 1 < len(tl_):
                        t_ = tl_[i_ - 1]; tn_ = tile_rows(t_); o_ = offs_p[q_][t_]
                        xb = [xn_f, xn_f2][(i_ - 1) % 2]
                        transpose8(xb, "xnf%d" % ((i_ - 1) % 2), tn_, aTfs[q_ % 2][:, :, o_:o_ + tn_], "aTf%d" % (q_ % 2), 7)
                    if 0 <= i_ < len(tl_):
                        t_ = tl_[i_]; tn_ = tile_rows(t_)
                        norm_stats(H[:tn_, t_, :], Hres(t_), 0, tn_, [xn_f, xn_f2][i_ % 2], "xnf%d" % (i_ % 2))
                    assert len(tl_) + 2 < n_items_p[p]
                if idx + 1 < len(items):
                    emit_U(items[idx + 1], idx + 1)
                emit_D(it, idx)

        ffn_layer(0, False)
        if STAGE <= 2:
            for t in range(NT):
                P.op("sp", lambda e, t=t: e.dma_start(out=yp[t * 128:(t + 1) * 128, :], in_=H[:, t, :]), [Hres(t)], [], dma="dbg")
            P.op("sp", lambda e: e.dma_start(out=ys, in_=H[:NS, NT, :]), [Hres(NT)], [], dma="dbg")
            P.emit()
            return nc


        P.barrier()
        AR.reset()
        WQ = AR.alloc([128, 8, 1536], BF16)
        bqb = AR.alloc([1, 1536], BF16, parts=1)
        bob = AR.alloc([1, D], BF16, parts=1)
        esink = AR.alloc([128, 16], F32)
        esr = AR.alloc([1, NS, 16], F32, parts=1)
        xn_s = AR.alloc([128, D], BF16)
        aTs = AR.alloc([128, 8, 128], BF16)
        qkf = AR.alloc([128, 1280], F32)
        vfp = AR.alloc([128, 256], F32)
        qkb = AR.alloc([128, 1280], BF16)
        rt = [AR.alloc([128, 20, 8], F32) for _ in range(4)]
        stmp = AR.alloc([128, D], F32)
        U0 = AR.off
        WO1 = AR.alloc([128, 8, D], BF16)
        qT = [AR.alloc([64, 16, 128], BF16, parts=64) for _ in range(3)]
        kT = [AR.alloc([64, 4, 128], BF16, parts=64) for _ in range(4)]
        vaug = [AR.alloc([128, 4, 65], BF16) for _ in range(4)]
        PT = AR.alloc([128, 2, 4, 512], BF16)
        dn = AR.alloc([128, 16], F32)
        on_s = AR.alloc([128, D], BF16)
        onTs = AR.alloc([128, 8, 128], BF16)
        hk = AR.alloc([128, 512], F32)
        _o = AR.off
        hg = AR.alloc([128, 4, 512], F32)
        AR.off = _o
        btmp = AR.alloc([1, 1536], F32, parts=1)
        AR.off = _o + 2048
        U1 = AR.off
        AR.off = U0
        WO2 = AR.alloc([64, 16, D], BF16, parts=64)
        selmat = AR.alloc([NS, NS, 128], BF16, parts=NS)
        Kc = [AR.alloc([128, 256], F32) for _ in range(2)]
        prod = AR.alloc([128, 16, 64], F32)
        sc_all = AR.alloc([128, NS, 16], F32)
        P_all = AR.alloc([128, NS, 16], F32)
        pnew = AR.alloc([NS, 16], F32, parts=NS)
        Pnm = AR.alloc([NS, NS, 16], F32, parts=NS)
        rden = AR.alloc([64, NS * 16], F32, parts=64)
        oTs = AR.alloc([64, 16, NS], BF16, parts=64)
        AR.off = max(AR.off, U1)
        print("SWA arena words", AR.off)

        wq_v = wqkv.rearrange("(k p) n -> p k n", p=128)
        P.op("pool", lambda e: e.dma_start(out=WQ[:, :, 1024:1536], in_=wq_v[:, :, 1024:1536]), [], ["WQkv"], dma="w2a")
        P.op("pool", lambda e: e.dma_start(out=WQ[:, :, 0:1024], in_=wq_v[:, :, 0:1024]), [], ["WQq"], dma="w2b")
        P.op("pool", lambda e: e.dma_start(out=WO1, in_=wo1.rearrange("(k p) n -> p k n", p=128)), [], ["WO1"], dma="w3")
        ld("sp", btmp[0:1, :], bqkv, ["btmp"], "c3")
        P.op("dve", lambda e: e.tensor_copy(out=bqb, in_=btmp), ["btmp"], ["bqb"])
        ld("sp", btmp[0:1, 0:D], bo1, ["btmp"], "c3")
        P.op("dve", lambda e: e.tensor_copy(out=bob, in_=btmp[0:1, 0:D]), ["btmp"], ["bob"])
        ld("sp", esink, sinks.partition_broadcast(128), ["esink"], "c4")
        P.op("act", lambda e: e.activation(out=esink, in_=esink, func=AF.Exp), ["esink"], ["esink"])
        P.op("dve", lambda e: e.tensor_copy(out=esr, in_=esink[0:1, :].unsqueeze(1).broadcast_to([1, NS, 16])), ["esink"], ["esr"])
        for i in range(4):
            P.op("pool", lambda e, i=i: e.memset(vaug[i], 1.0), [], ["vaug%d" % i])
        load_gain(0, nmp[1])
        load_gain(1, nmpost[1])
        cut(2.05)

        def slot_of(t):
            return 2 if t == 0 else t % 2

        def kv_finish(slot, ksrc, ksrc_res, vsrc, vsrc_res, tn):
            P.op("act", lambda e: e.activation(out=vaug[slot][:tn, :, 0:64], in_=vsrc.rearrange("p (j d) -> p j d", j=4), func=AF.Copy),
                 [vsrc_res], ["vaug%d" % slot])
            tp = bf(2)
            for j in range(4):
                P.op("pe", lambda e, j=j: e.transpose(out=tp[0:64, j * 128:j * 128 + tn], in_=ksrc[:tn, j * 64:(j + 1) * 64],
                                                      identity=identb[:tn, :tn]), [ksrc_res, "identb"], [bk(2)])
            P.op("act", lambda e: e.activation(out=kT[slot][:, :, :tn], in_=tp[0:64, 0:512].rearrange("p (j t) -> p j t", j=4)[:, :, :tn], func=AF.Copy),
                 [bk(2)], ["kT%d" % slot])

        def swa_A(t, kv_only=False, slot=None):
            tn = tile_rows(t)
            if slot is None:
                slot = slot_of(t)
            norm_T(H[:tn, t, :], Hres(t), 0, tn, xn_s, aTs[:, :, :tn], "aTs", 2)
            if not kv_only:
                proj(aTs, "aTs", tn, WQ, "WQq", 0, 512, 0, bias_row=bqb, bias_res="bqb")
                proj(aTs, "aTs", tn, WQ, "WQq", 512, 512, 1, bias_row=bqb, bias_res="bqb")
                P.op("act", lambda e: e.activation(out=qkf[:tn, 0:512], in_=banks[0][:tn, :], func=AF.Copy), [bk(0)], ["qkf"])
                P.op("act", lambda e: e.activation(out=qkf[:tn, 512:1024], in_=banks[1][:tn, :], func=AF.Copy), [bk(1)], ["qkf"])
            proj(aTs, "aTs", tn, WQ, "WQkv", 1024, 512, 2, bias_row=bqb, bias_res="bqb")
            P.op("act", lambda e: e.activation(out=qkf[:tn, 1024:1280], in_=banks[2][:tn, 0:256], func=AF.Copy), [bk(2)], ["qkf"])
            if kv_only:
                cut(2.06)
            need_v32 = kv_only or t >= NT - 1
            if need_v32:
                P.op("dve", lambda e: e.tensor_copy(out=vfp[:tn, :], in_=banks[2][:tn, 256:512]), [bk(2), "qkf"], ["vfp"])
            if kv_only:
                cut(2.062)
            h0 = 16 if kv_only else 0
            nh = 20 - h0
            qv = qkf[:tn, :].rearrange("p (h d) -> p h d", d=64)
            x1 = qv[:, h0:20, 0:8]; x2 = qv[:, h0:20, 8:16]
            cb = cosT[:tn, t, :].unsqueeze(1).broadcast_to([tn, nh, 8])
            sb_ = sinT[:tn, t, :].unsqueeze(1).broadcast_to([tn, nh, 8])
            P.op("dve", lambda e: e.tensor_tensor(out=rt[0][:tn, :nh, :], in0=x1, in1=cb, op=ALU.mult), ["qkf", "cos"], ["rt0"])
            P.op("dve", lambda e: e.tensor_tensor(out=rt[1][:tn, :nh, :], in0=x2, in1=sb_, op=ALU.mult), ["qkf", "sin"], ["rt1"])
            P.op("dve", lambda e: e.tensor_tensor(out=rt[2][:tn, :nh, :], in0=x2, in1=cb, op=ALU.mult), ["qkf", "cos"], ["rt2"])
            P.op("dve", lambda e: e.tensor_tensor(out=rt[3][:tn, :nh, :], in0=x1, in1=sb_, op=ALU.mult), ["qkf", "sin"], ["rt3"])
            if kv_only:
                cut(2.064)
            P.op("dve", lambda e: e.tensor_tensor(out=x1, in0=rt[0][:tn, :nh, :], in1=rt[1][:tn, :nh, :], op=ALU.subtract), ["rt0", "rt1", "qkf"], ["qkf"])
            P.op("dve", lambda e: e.tensor_tensor(out=x2, in0=rt[2][:tn, :nh, :], in1=rt[3][:tn, :nh, :], op=ALU.add), ["rt2", "rt3", "qkf"], ["qkf"])
            if kv_only:
                cut(2.07)
            c0 = 1024 if kv_only else 0
            P.op("act", lambda e: e.activation(out=qkb[:tn, c0:1280], in_=qkf[:tn, c0:1280], func=AF.Copy), ["qkf"], ["qkb"])
            if kv_only:
                cut(2.08)
            if t >= NT:
                return
            if not kv_only:
                tps = [bf(0), bf(1)]
                for h in range(16):
                    b = h // 8
                    P.op("pe", lambda e, h=h, b=b: e.transpose(out=tps[b][0:64, (h % 8) * 128:(h % 8) * 128 + tn], in_=qkb[:tn, h * 64:(h + 1) * 64],
                                                               identity=identb[:tn, :tn]), ["qkb", "identb"], [bk(b)])
                for b in range(2):
                    P.op("act", lambda e, b=b: e.activation(out=qT[slot][:, b * 8:(b + 1) * 8, :tn],
                                                            in_=tps[b][0:64, :].rearrange("p (h t) -> p h t", h=8)[:, :, :tn], func=AF.Copy),
                         [bk(b)], ["qT%d" % slot])
            kv_finish(slot, qkb[:, 1024:1280], "qkb", banks[2][:tn, 256:512], bk(2), tn)

        def swa_B(t, part=None):
            tn = 128
            if part == "back":
                transpose8(on_s, "on_s", tn, onTs[:, :, :tn], "onTs", 5)
                proj(onTs, "onTs", tn, WO1, "WO1", 0, 512, 3, bias_row=bob, bias_res="bob")
                proj(onTs, "onTs", tn, WO1, "WO1", 512, 512, 4, bias_row=bob, bias_res="bob")
                postnorm_residual(tn, [3, 4], 1, H[:tn, t, :], Hres(t), H[:tn, t, :], Hres(t), stmp, "stmp")
                return
            slot = slot_of(t)
            pslot = 3 if t == 0 else slot_of(t - 1)
            pmask = pm0 if t == 0 else mge
            pmres = "pm0" if t == 0 else "mge"
            for j in range(4):
                for blk, (ks_, msk, mres) in enumerate([(pslot, pmask, pmres), (slot, mle, "mle")]):
                    b = 3 + (2 * j + blk) % 2
                    P.op("pe", lambda e, j=j, ks_=ks_, b=b: e.matmul(banks[b][:, :], lhsT=kT[ks_][:, j, :], rhs=qT[slot][:, 4 * j:4 * j + 4, :],
                                                                   start=True, stop=True), ["kT%d" % ks_, "qT%d" % slot], [bk(b)])
                    P.op("act", lambda e, j=j, blk=blk, b=b: e.activation(out=PT[:, blk, j, :], in_=banks[b][:, :], func=AF.Exp, scale=0.125),
                         [bk(b)], ["PT%d_%d" % (blk, j)])
                    eng = "pool" if blk == 0 else "dve"
                    P.op(eng, lambda e, j=j, blk=blk, msk=msk: e.tensor_tensor(
                        out=PT[:, blk, j, :].rearrange("p (g q) -> p g q", g=4), in0=PT[:, blk, j, :].rearrange("p (g q) -> p g q", g=4),
                        in1=msk[:].unsqueeze(1).broadcast_to([128, 4, 128]), op=ALU.mult), ["PT%d_%d" % (blk, j), mres], ["PT%d_%d" % (blk, j)])
            for h in range(16):
                j = h // 4; g = h % 4
                pb_ = 5 + h // 7
                ob = banks[pb_][:, (h % 7) * 65:(h % 7) * 65 + 65]
                P.op("pe", lambda e, j=j, g=g, ob=ob: e.matmul(ob, lhsT=PT[:, 0, j, g * 128:(g + 1) * 128], rhs=vaug[pslot][:, j, :], start=True, stop=False),
                     ["PT0_%d" % j, "vaug%d" % pslot], [bk(pb_)])
                P.op("pe", lambda e, j=j, g=g, ob=ob: e.matmul(ob, lhsT=PT[:, 1, j, g * 128:(g + 1) * 128], rhs=vaug[slot][:, j, :], start=False, stop=True),
                     ["PT1_%d" % j, "vaug%d" % slot], [bk(pb_)])
            hgroups = [(5, 0, 7), (6, 7, 7), (7, 14, 2)]
            for (pb_, h0, nh_) in hgroups:
                bv = banks[pb_][:, 0:nh_ * 65].rearrange("p (h c) -> p h c", c=65)
                P.op("dve", lambda e, bv=bv, h0=h0, nh_=nh_: e.tensor_tensor(out=dn[:, h0:h0 + nh_], in0=bv[:, :, 64], in1=esink[:, h0:h0 + nh_], op=ALU.add),
                     [bk(pb_), "esink"], ["dn"])
            P.op("dve", lambda e: e.reciprocal(out=dn, in_=dn), ["dn"], ["dn"])
            for (pb_, h0, nh_) in hgroups:
                bv = banks[pb_][:, 0:nh_ * 65].rearrange("p (h c) -> p h c", c=65)
                P.op("dve", lambda e, bv=bv, h0=h0, nh_=nh_: e.tensor_tensor(
                    out=on_s[:, h0 * 64:(h0 + nh_) * 64].rearrange("p (h d) -> p h d", d=64), in0=bv[:, :, 0:64],
                    in1=dn[:, h0:h0 + nh_].unsqueeze(2).broadcast_to([128, nh_, 64]), op=ALU.mult), [bk(pb_), "dn"], ["on_s"])
            if part == "front":
                return
            swa_B(t, "back")

        def swa_sample():
            t = NT
            tn = NS
            P.op("pool", lambda e: e.dma_start(out=WO2, in_=wo1.rearrange("(h p) n -> p h n", p=64)), [], ["WO2"], dma="w4")
            P.op("dve", lambda e: e.tensor_copy(out=selmat, in_=identf[:NS, 0:NS].unsqueeze(2).broadcast_to([NS, NS, 128])), ["identf"], ["selmat"])
            swa_A(t)
            P.op("sp", lambda e: e.dma_start(out=ks[:, 0:127, :], in_=ck[:, 1:128, :]), [], [], dma="co")
            P.op("sp", lambda e: e.dma_start(out=vs[:, 0:127, :], in_=cv[:, 1:128, :]), [], [], dma="co")
            P.op("sp", lambda e: e.dma_start(out=ks[:, 127, :], in_=qkf[:NS, 1024:1280]), ["qkf"], [], dma="co2")
            P.op("sp", lambda e: e.dma_start(out=vs[:, 127, :], in_=vfp[:NS, :]), ["vfp"], [], dma="co3")
            for n in range(NS):
                kc = Kc[n % 2]
                ld("sp", kc, ck[n], ["Kc%d" % (n % 2)], "kc%d" % (n % 2))
                for hf in range(2):
                    P.op("pe", lambda e, n=n, hf=hf: e.matmul(banks[hf][:, :], lhsT=selmat[:, n, :], rhs=qkb[:NS, hf * 512:(hf + 1) * 512], start=True, stop=True),
                         ["selmat", "qkb"], [bk(hf)])
                    P.op("dve", lambda e, hf=hf, kc=kc: e.tensor_tensor(
                        out=prod[:, hf * 8:(hf + 1) * 8, :].rearrange("p (j g) d -> p j g d", g=4),
                        in0=banks[hf][:, :].rearrange("p (j g d) -> p j g d", g=4, d=64),
                        in1=kc[:, hf * 128:(hf + 1) * 128].rearrange("p (j d) -> p j d", d=64).unsqueeze(2).broadcast_to([128, 2, 4, 64]),
                        op=ALU.mult), [bk(hf), "Kc%d" % (n % 2)], ["prod%d" % hf])
                P.op("dve", lambda e, n=n: e.tensor_reduce(out=sc_all[:, n, :], in_=prod, axis=mybir.AxisListType.X, op=ALU.add),
                     ["prod0", "prod1"], ["sc_all"])
            P.op("act", lambda e: e.activation(out=P_all, in_=sc_all, func=AF.Exp, scale=0.125), ["sc_all"], ["P_all"])
            qv4 = qkf[:NS, 0:1024].rearrange("p (j g d) -> p j g d", g=4, d=64)
            kv4 = qkf[:NS, 1024:1280].rearrange("p (j d) -> p j d", d=64).unsqueeze(2).broadcast_to([NS, 4, 4, 64])
            P.op("dve", lambda e: e.tensor_tensor(out=prod[:NS, :, :].rearrange("p (j g) d -> p j g d", g=4), in0=qv4, in1=kv4, op=ALU.mult),
                 ["qkf", "sc_all"], ["prod0", "prod1"])
            P.op("dve", lambda e: e.tensor_reduce(out=pnew, in_=prod[:NS, :, :], axis=mybir.AxisListType.X, op=ALU.add), ["prod0", "prod1"], ["pnew"])
            P.op("act", lambda e: e.activation(out=pnew, in_=pnew, func=AF.Exp, scale=0.125), ["pnew"], ["pnew"])
            P.op("dve", lambda e: e.tensor_tensor(out=Pnm, in0=pnew[:, :].unsqueeze(1).broadcast_to([NS, NS, 16]),
                                                  in1=identf[:NS, 0:NS].unsqueeze(2).broadcast_to([NS, NS, 16]), op=ALU.mult),
                 ["pnew", "identf"], ["Pnm"])
            OT = banks[2][0:64, 0:NS * 16]
            for n in range(NS):
                vc = Kc[n % 2]
                ld("sp", vc, cv[n], ["Kc%d" % (n % 2)], "kc%d" % (n % 2))
                for j in range(4):
                    oap = banks[2][0:64, n * 16 + 4 * j:n * 16 + 4 * j + 4]
                    P.op("pe", lambda e, n=n, j=j, vc=vc, oap=oap: e.matmul(oap, lhsT=vc[:, j * 64:(j + 1) * 64], rhs=P_all[:, n, 4 * j:4 * j + 4], start=True, stop=False),
                         ["Kc%d" % (n % 2), "P_all"], [bk(2)])
                    P.op("pe", lambda e, n=n, j=j, oap=oap: e.matmul(oap, lhsT=vfp[:NS, j * 64:(j + 1) * 64], rhs=Pnm[:, n, 4 * j:4 * j + 4], start=False, stop=True),
                         ["vfp", "Pnm"], [bk(2)])
            DEN = banks[3][0:64, 0:NS * 16]
            P.op("pe", lambda e: e.matmul(DEN, lhsT=onesf[:, 0:64], rhs=P_all[:, :, :].rearrange("p n h -> p (n h)"), start=True, stop=False), ["onesf", "P_all"], [bk(3)])
            P.op("pe", lambda e: e.matmul(DEN, lhsT=onesf[:NS, 0:64], rhs=Pnm[:, :, :].rearrange("p n h -> p (n h)"), start=False, stop=False), ["onesf", "Pnm"], [bk(3)])
            P.op("pe", lambda e: e.matmul(DEN, lhsT=onesf[0:1, 0:64], rhs=esr[:, :, :].rearrange("p n h -> p (n h)"), start=False, stop=True), ["onesf", "esr"], [bk(3)])
            P.op("dve", lambda e: e.reciprocal(out=rden, in_=DEN), [bk(3)], ["rden"])
            P.op("dve", lambda e: e.tensor_tensor(out=oTs, in0=OT.rearrange("p (n h) -> p h n", h=16), in1=rden[:, :].rearrange("p (n h) -> p h n", h=16), op=ALU.mult),
                 [bk(2), "rden"], ["oTs"])
            for hf in range(2):
                for h in range(16):
                    P.op("pe", lambda e, h=h, hf=hf: e.matmul(banks[hf][:NS, :], lhsT=oTs[:, h, :], rhs=WO2[:, h, hf * 512:(hf + 1) * 512], start=(h == 0), stop=False),
                         ["oTs", "WO2"], [bk(hf)])
                P.op("pe", lambda e, hf=hf: e.matmul(banks[hf][:NS, :], lhsT=onesb[0:1, :NS], rhs=bob[0:1, hf * 512:(hf + 1) * 512], start=False, stop=True),
                     ["onesb", "bob"], [bk(hf)])
            postnorm_residual(tn, [0, 1], 1, H[:tn, t, :], Hres(t), H[:tn, t, :], Hres(t), stmp, "stmp")

        swa_A(NT - 1, kv_only=True, slot=3)
        cut(2.1)
        P.op("sp", lambda e: e.dma_start(out=ag2_in.ap()[:, 0:256], in_=qkf[:, 1024:1280]), ["qkf"], ["ag2_in"], dma="ag2w0")
        P.op("sp", lambda e: e.dma_start(out=ag2_in.ap()[:, 256:512], in_=vfp[:, :]), ["vfp"], ["ag2_in"], dma="ag2w1")
        P.op("sp", lambda e: e.dma_start(out=kp, in_=qkf[:, 1024:1280]), ["qkf"], [], dma="kvo0")
        P.op("sp", lambda e: e.dma_start(out=vp, in_=vfp[:, :]), ["vfp"], [], dma="kvo1")
        P.op("pool", lambda e: e.collective_compute("AllGather", ALU.bypass, replica_groups=groups,
                                                    ins=[ag2_in.ap().opt()], outs=[ag2_out.ap().opt()]),
             ["ag2_in"], ["ag2_out"], dma="cc2", inc=1)
        P.op("sp", lambda e: e.dma_start(out=hg, in_=ag2_out.ap().rearrange("(r p) n -> p r n", p=128)), ["ag2_out"], ["hg"], dma="ag2r")
        cut(2.15)
        swa_A(0)
        if STAGE <= 2.2:
            P.barrier()
            for t in range(NT):
                P.op("sp", lambda e, t=t: e.dma_start(out=yp[t * 128:(t + 1) * 128, :], in_=H[:, t, :]), [Hres(t)], [], dma="dbg")
            P.op("sp", lambda e: e.dma_start(out=ys, in_=H[:NS, NT, :]), [Hres(NT)], [], dma="dbg")
            P.emit()
            return nc
        if NT > 1:
            swa_A(1)
        for t in range(1, NT):
            P.capture()
            swa_B(t)
            A_ = P.end_capture()
            B_ = []
            if t + 1 < NT:
                P.capture()
                swa_A(t + 1)
                B_ = P.end_capture()
            P.commit_merged(A_, B_)
        P.op("dve", lambda e: e.tensor_scalar(out=hk, in0=hg[:, 0, :], scalar1=selprev[:, 0:1], scalar2=None, op0=ALU.mult), ["hg", "selprev"], ["hk"])
        for j in range(1, 4):
            P.op("dve", lambda e, j=j: e.scalar_tensor_tensor(out=hk, in0=hg[:, j, :], scalar=selprev[:, j:j + 1], in1=hk, op0=ALU.mult, op1=ALU.add),
                 ["hg", "selprev", "hk"], ["hk"])
        P.op("pool", lambda e: e.tensor_copy(out=qkb[:, 1024:1280], in_=hk[:, 0:256]), ["hk"], ["qkb"])
        kv_finish(3, qkb[:, 1024:1280], "qkb", hk[:, 256:512], "hk", 128)
        swa_B(0)
        P.barrier()
        if STAGE <= 2.5:
            for t in range(NT):
                P.op("sp", lambda e, t=t: e.dma_start(out=yp[t * 128:(t + 1) * 128, :], in_=H[:, t, :]), [Hres(t)], [], dma="dbg")
            P.op("sp", lambda e: e.dma_start(out=ys, in_=H[:NS, NT, :]), [Hres(NT)], [], dma="dbg")
            P.emit()
            return nc
        swa_sample()

        if STAGE <= 3:
            for t in range(NT):
                P.op("sp", lambda e, t=t: e.dma_start(out=yp[t * 128:(t + 1) * 128, :], in_=H[:, t, :]), [Hres(t)], [], dma="dbg")
            P.op("sp", lambda e: e.dma_start(out=ys, in_=H[:NS, NT, :]), [Hres(NT)], [], dma="dbg")
            P.emit()
            return nc

        ffn_layer(1, True)
        P.emit()
    return nc


_CACHE = {}


def _consts(NT, c):
    r = np.arange(128)
    ident = np.eye(128, dtype=np.float32)
    triI = (r[:, None] <= r[None, :]).astype(np.float32)
    triS = (r[:, None] > r[None, :]).astype(np.float32)
    mge = (r[:, None] >= r[None, :]).astype(np.float32)
    pm0 = mge if (c % 4) != 0 else np.zeros((128, 128), np.float32)
    eye16 = np.broadcast_to(np.eye(16, dtype=np.float32), (128, 16, 16)).copy()
    half = 8
    inv = np.power(np.float32(500000.0), -np.arange(half, dtype=np.float32) * np.float32(2.0) / np.float32(16)).astype(np.float32)
    tok0 = (c % 4) * NT * 128
    pos = np.concatenate([tok0 + np.arange(NT * 128), np.full(128, PAST_LEN)]).astype(np.float32)
    ang = pos[:, None] * inv[None, :]
    cos = np.cos(ang).astype(np.float32)
    sin = np.sin(ang).astype(np.float32)
    cm = np.zeros((128, 4), np.float32)
    sp = np.zeros((128, 4), np.float32)
    for j in range(4):
        if j < (c % 4):
            cm[:, j] = 1.0
        if j == (c % 4) - 1:
            sp[:, j] = 1.0
    return dict(c_ident=ident, c_triI=triI, c_triS=triS, c_mge=mge, c_pm0=pm0, c_eye16=eye16, c_cos=cos, c_sin=sin,
                c_cmask=cm, c_selprev=sp)


def kernel(x_prompt, x_sample, state_gla, cache_swa_k, cache_swa_v,
           gla_w_in, gla_w_gate2, gla_b_gate, gla_g_head, gla_w_out,
           swa_w_qkv, swa_b_qkv, swa_sinks, swa_w_out, swa_b_out,
           norm_mix_pre, norm_mix_post, norm_ffn_pre, norm_ffn_post,
           ffn_w_up, ffn_w_down):
    f = lambda a: np.ascontiguousarray(np.asarray(a, dtype=np.float32))
    B, L, _ = x_prompt.shape
    NT = (B * L) // (NCORES * 128)
    key = (NT, STAGE)
    if key not in _CACHE:
        _CACHE[key] = build(NT)
    nc = _CACHE[key]
    xpf = f(x_prompt).reshape(B * L, D)
    xsf = f(x_sample).reshape(-1, D)
    shared = dict(
        w_in=f(gla_w_in[0]), wg2=f(gla_w_gate2[0]), bg=f(gla_b_gate[0]).reshape(1, 512), ghead=f(gla_g_head[0]),
        w_out0=f(gla_w_out[0]), wqkv=f(swa_w_qkv[0]), bqkv=f(swa_b_qkv[0]).reshape(1, 1536), sinks=f(swa_sinks[0]),
        wo1=f(swa_w_out[0]), bo1=f(swa_b_out[0]).reshape(1, D),
        nmp=f(norm_mix_pre), nmpost=f(norm_mix_post), nfp=f(norm_ffn_pre), nfpost=f(norm_ffn_post),
        wup=f(ffn_w_up), wdn=f(ffn_w_down))
    sg = f(state_gla[0]); ckf = f(cache_swa_k[0]).reshape(128, 128, 256); cvf = f(cache_swa_v[0]).reshape(128, 128, 256)
    in_maps = []
    for c in range(NCORES):
        m = dict(shared)
        m.update(_consts(NT, c))
        m["xp"] = xpf[c * NT * 128:(c + 1) * NT * 128]
        m["xs"] = xsf[c * NS:(c + 1) * NS]
        m["sgla"] = sg[c * NS:(c + 1) * NS]
        m["ck"] = ckf[c * NS:(c + 1) * NS]
        m["cv"] = cvf[c * NS:(c + 1) * NS]
        in_maps.append(m)
    res = run_bass_kernel_spmd(nc, in_maps, core_ids=list(range(NCORES)))
    R = res.results
    y_p = np.concatenate([R[c]["yp"] for c in range(NCORES)], 0).reshape(B, L, D)
    y_s = np.concatenate([R[c]["ys"] for c in range(NCORES)], 0).reshape(-1, 1, D)
    g_p = np.stack([R[3]["gp"], R[7]["gp"]], 0)[None]
    g_s = np.concatenate([R[c]["gs"] for c in range(NCORES)], 0)[None]
    k_p = np.stack([R[3]["kp"], R[7]["kp"]], 0).reshape(1, B, 128, 4, 64)
    v_p = np.stack([R[3]["vp"], R[7]["vp"]], 0).reshape(1, B, 128, 4, 64)
    k_s = np.concatenate([R[c]["ks"] for c in range(NCORES)], 0).reshape(1, -1, 128, 4, 64)
    v_s = np.concatenate([R[c]["vs"] for c in range(NCORES)], 0).reshape(1, -1, 128, 4, 64)
    return (y_p, y_s, g_p, g_s, k_p, v_p, k_s, v_s)
```

```python
import contextlib
import numpy as np
import concourse.bass as bass
import concourse.mybir as mybir
from concourse.bass_utils import run_bass_kernel_spmd

F32 = mybir.dt.float32
BF16 = mybir.dt.bfloat16
AF = mybir.ActivationFunctionType
ALU = mybir.AluOpType

NCORES = 8
D = 1024
SEQ = 8192
PAST_LEN = 8192
NS = 16
EPS = 1e-6
GIN = 3088
STAGE = 99
ENGS = ("pe", "act", "dve", "pool", "sp")


class Prog:
    def __init__(self, nc):
        self.nc = nc
        self.ops = []
        self.last_w = {}
        self.readers = {}
        self.dma_count = {}
        self.bar = set()

    def capture(self):
        self._cap = []

    def end_capture(self):
        lst = self._cap
        self._cap = None
        return lst

    def commit_merged(self, A, B):
        def banks_of(o):
            return {r for r in list(o[2]) + list(o[3]) if r.startswith("pb")}
        fut = [set() for _ in range(len(A) + 1)]
        for i in range(len(A) - 1, -1, -1):
            fut[i] = fut[i + 1] | banks_of(A[i])
        i = j = 0
        while i < len(A) or j < len(B):
            takeB = False
            if j < len(B):
                if i >= len(A):
                    takeB = True
                elif not (banks_of(B[j]) & fut[i]) and j * len(A) <= i * len(B):
                    takeB = True
            if takeB:
                self.op(*B[j]); j += 1
            else:
                self.op(*A[i]); i += 1

    def op(self, eng, fn, reads=(), writes=(), dma=None, inc=16, batch=False):
        if getattr(self, "_cap", None) is not None:
            self._cap.append((eng, fn, list(reads), list(writes), dma, inc, batch))
            return None
        idx = len(self.ops)
        deps = set(self.bar)
        pbr = [r for r in reads if r.startswith("pb")]
        if pbr:
            reads = [r for r in reads if not r.startswith("pb")]
            writes = list(writes) + pbr
        for r in reads:
            w = self.last_w.get(r)
            if w is not None:
                deps.add(w)
        for r in writes:
            w = self.last_w.get(r)
            if w is not None:
                deps.add(w)
            for rd in self.readers.get(r, ()):
                deps.add(rd)
        if eng == "pe" and dma is None:
            deps = {d for d in deps if not (self.ops[d]["eng"] == "pe" and self.ops[d]["dma"] is None)}
        if dma is not None and batch:
            deps = {d for d in deps if self.ops[d]["dma"] != dma}
        o = dict(eng=eng, fn=fn, deps=deps, dma=dma, signal=False, ev=None, inc=1)
        if dma is not None:
            n = self.dma_count.get(dma, 0) + 1
            self.dma_count[dma] = n
            o["ev"] = (("dma", dma), inc * n)
            o["signal"] = True
            o["inc"] = inc
            o["batch"] = batch
        self.ops.append(o)
        for r in reads:
            self.readers.setdefault(r, []).append(idx)
        for r in writes:
            self.last_w[r] = idx
            self.readers[r] = []
        return idx

    def barrier(self):
        allres = list(set(self.last_w) | set(self.readers))
        ids = []
        for e in ENGS:
            ids.append(self.op(e, lambda en: en.nop(), writes=allres + ["bar_" + e]))
        self.bar = set(ids)
        self.last_w = {}
        self.readers = {}

    def emit(self):
        nc = self.nc
        ops = self.ops
        for o in ops:
            for d in o["deps"]:
                ops[d]["signal"] = True
        for o in ops:
            if o["dma"] is not None and o.get("batch"):
                o["ev"] = (o["ev"][0], o["inc"] * self.dma_count[o["dma"]])
        cnt = {e: 0 for e in ENGS}
        for o in ops:
            if o["dma"] is None and o["signal"]:
                cnt[o["eng"]] += 1
                o["ev"] = (("eng", o["eng"]), cnt[o["eng"]])
        sem_keys = []
        for o in ops:
            if o["ev"] is not None and o["ev"][0] not in sem_keys:
                sem_keys.append(o["ev"][0])
        with contextlib.ExitStack() as st:
            sems = {}
            for k in sem_keys:
                sems[k] = st.enter_context(nc.semaphore("s_" + "_".join(str(x) for x in k)))
            block = st.enter_context(nc.Block())
            final = {}
            for o in ops:
                if o["dma"] is not None:
                    final[o["ev"][0]] = max(final.get(o["ev"][0], 0), o["ev"][1])

            def replay(engname, e):
                waited = {}
                for o in ops:
                    if o["eng"] != engname:
                        continue
                    need = {}
                    for d in o["deps"]:
                        k, v = ops[d]["ev"]
                        if v > need.get(k, 0):
                            need[k] = v
                    for k, v in need.items():
                        if waited.get(k, 0) < v:
                            e.wait_ge(sems[k], v)
                            waited[k] = v
                    ins = o["fn"](e)
                    if o["signal"]:
                        ins.then_inc(sems[o["ev"][0]], o["inc"])
                if engname == "sp":
                    for k, v in final.items():
                        if waited.get(k, 0) < v:
                            e.wait_ge(sems[k], v)
                            waited[k] = v

            @block.tensor
            def _(e):
                replay("pe", e)

            @block.scalar
            def _(e):
                replay("act", e)

            @block.vector
            def _(e):
                replay("dve", e)

            @block.gpsimd
            def _(e):
                replay("pool", e)

            @block.sync
            def _(e):
                replay("sp", e)


class _Done(Exception):
    pass


class Arena:
    def __init__(self, t, words):
        self.t = t
        self.words = words
        self.off = 0

    def reset(self):
        self.off = 0

    def alloc(self, shape, dtype, parts=None):
        parts = parts or shape[0]
        n = int(np.prod(shape[1:]))
        esz = 2 if dtype == BF16 else 4
        words = (n * esz + 3) // 4
        words = (words + 7) // 8 * 8
        assert self.off + words <= self.words, ("arena overflow", self.off, words, self.words)
        ap = self.t[0:parts, self.off:self.off + words]
        if dtype == BF16:
            ap = ap.bitcast(BF16)
        ap = ap[:, 0:n]
        self.off += words
        fs = shape[1:]
        if len(fs) == 2:
            ap = ap.rearrange("p (a b) -> p a b", a=fs[0])
        elif len(fs) == 3:
            ap = ap.rearrange("p (a b c) -> p a b c", a=fs[0], b=fs[1])
        return ap


def build(NT):
    box = {}
    try:
        _build(NT, box)
    except _Done:
        pass
    return box['nc']


def _build(NT, box):
    nc = bass.Bass("TRN2", target_bir_lowering=False)
    box["nc"] = nc
    NTT = NT + 1
    TOK = NT * 128

    def din(name, shape):
        return nc.dram_tensor(name, list(shape), F32, kind="ExternalInput").ap()

    def dout(name, shape):
        return nc.dram_tensor(name, list(shape), F32, kind="ExternalOutput").ap()

    xp = din("xp", [TOK, D]); xs = din("xs", [NS, D])
    sgla = din("sgla", [NS, 4, 128, 256]); ck = din("ck", [NS, 128, 256]); cv = din("cv", [NS, 128, 256])
    w_in = din("w_in", [D, GIN]); wg2 = din("wg2", [16, 512]); bg = din("bg", [1, 512]); ghead = din("ghead", [256])
    w_out0 = din("w_out0", [D, D]); wqkv = din("wqkv", [D, 1536]); bqkv = din("bqkv", [1, 1536])
    sinks = din("sinks", [16]); wo1 = din("wo1", [D, D]); bo1 = din("bo1", [1, D])
    nmp = din("nmp", [2, D]); nmpost = din("nmpost", [2, D]); nfp = din("nfp", [2, D]); nfpost = din("nfpost", [2, D])
    wup = din("wup", [2, D, 4096]); wdn = din("wdn", [2, 4096, D])
    c_ident = din("c_ident", [128, 128]); c_triI = din("c_triI", [128, 128]); c_triS = din("c_triS", [128, 128])
    c_mge = din("c_mge", [128, 128]); c_pm0 = din("c_pm0", [128, 128]); c_eye16 = din("c_eye16", [128, 16, 16])
    c_cos = din("c_cos", [NTT * 128, 8]); c_sin = din("c_sin", [NTT * 128, 8])
    c_cmask = din("c_cmask", [128, 4]); c_selprev = din("c_selprev", [128, 4])

    yp = dout("yp", [TOK, D]); ys = dout("ys", [NS, D])
    gp = dout("gp", [4, 128, 256]); gs = dout("gs", [NS, 4, 128, 256])
    kp = dout("kp", [128, 256]); vp = dout("vp", [128, 256])
    ks = dout("ks", [NS, 128, 256]); vs = dout("vs", [NS, 128, 256])

    ag1_in = nc.dram_tensor("ag1_in", [128, 1032], F32)
    ag1_out = nc.dram_tensor("ag1_out", [4 * 128, 1032], F32)
    ag2_in = nc.dram_tensor("ag2_in", [128, 512], F32)
    ag2_out = nc.dram_tensor("ag2_out", [4 * 128, 512], F32)
    groups = [[0, 1, 2, 3], [4, 5, 6, 7]]

    ARW = 31200
    with contextlib.ExitStack() as ctx:
        def sb(name, shape, dt):
            return ctx.enter_context(nc.sbuf_tensor(name, list(shape), dt))

        H = sb("H", [128, NTT, D], F32)
        G = sb("G", [128, 2, D], F32)
        identf = sb("identf", [128, 128], F32); identb = sb("identb", [128, 128], BF16)
        triI = sb("triI", [128, 128], F32); triS = sb("triS", [128, 128], F32)
        mle = sb("mle", [128, 128], BF16); mge = sb("mge", [128, 128], BF16); pm0 = sb("pm0", [128, 128], BF16)
        onesf = sb("onesf", [128, 128], F32); onesb = sb("onesb", [128, 128], BF16)
        eye16 = sb("eye16", [128, 16, 16], F32)
        cosT = sb("cosT", [128, NTT, 8], F32); sinT = sb("sinT", [128, NTT, 8], F32)
        cmask = sb("cmask", [128, 4], F32); selprev = sb("selprev", [128, 4], F32)
        stat = sb("stat", [128, 16], F32)
        junk = sb("junk", [128, D], F32)
        ctmp = sb("ctmp", [128, 128], F32)
        arena_t = sb("arena", [128, ARW], F32)
        AR = Arena(arena_t, ARW)
        banks = [ctx.enter_context(nc.psum_tensor("pb%d" % i, [128, 512], F32)) for i in range(8)]

        P = Prog(nc)
        uid = [0]

        def cut(v):
            if STAGE <= v:
                P.barrier()
                for t in range(NT):
                    P.op("sp", lambda e, t=t: e.dma_start(out=yp[t * 128:(t + 1) * 128, :], in_=H[:, t, :]), [], [], dma="dbg")
                P.op("sp", lambda e: e.dma_start(out=ys, in_=H[:NS, NT, :]), [], [], dma="dbg")
                P.emit()
                raise _Done()

        def bk(i):
            return "pb%d" % i

        def bf(i):
            return banks[i][:].bitcast(BF16)

        def ld(eng, out, in_, w, key, r=(), batch=False):
            P.op(eng, lambda e: e.dma_start(out=out, in_=in_), reads=r, writes=w, dma=key, batch=batch)

        ld("sp", identf[:], c_ident, ["identf"], "c0", batch=True)
        ld("sp", triI[:], c_triI, ["triI"], "c0", batch=True)
        ld("sp", triS[:], c_triS, ["triS"], "c0", batch=True)
        ld("sp", eye16[:], c_eye16, ["eye16"], "c0", batch=True)
        ld("sp", cosT[:], c_cos.rearrange("(t p) e -> p t e", p=128), ["cos"], "c0", batch=True)
        ld("sp", sinT[:], c_sin.rearrange("(t p) e -> p t e", p=128), ["sin"], "c0", batch=True)
        ld("sp", cmask[:], c_cmask, ["cmask"], "c0", batch=True)
        ld("sp", selprev[:], c_selprev, ["selprev"], "c0", batch=True)
        P.op("dve", lambda e: e.tensor_copy(out=identb[:], in_=identf[:]), ["identf"], ["identb"])
        P.op("dve", lambda e: e.tensor_copy(out=mle[:], in_=triI[:]), ["triI"], ["mle"])
        ld("sp", ctmp[:], c_mge, ["ctmp"], "c1")
        P.op("dve", lambda e: e.tensor_copy(out=mge[:], in_=ctmp[:]), ["ctmp"], ["mge"])
        ld("sp", ctmp[:], c_pm0, ["ctmp"], "c1")
        P.op("dve", lambda e: e.tensor_copy(out=pm0[:], in_=ctmp[:]), ["ctmp"], ["pm0"])
        P.op("pool", lambda e: e.memset(onesf[:], 1.0), [], ["onesf"])
        P.op("pool", lambda e: e.memset(onesb[:], 1.0), [], ["onesb"])

        def load_gain(slot, src_row):
            ld("sp", G[:, slot, :], src_row.partition_broadcast(128), ["G%d" % slot], "g%d" % slot)

        def rstd_from_ss(tn, ss_col, out_col, n, tag):
            P.op("act", lambda e: e.activation(out=stat[:tn, out_col:out_col + 1], in_=stat[:tn, ss_col:ss_col + 1],
                                               func=AF.Ln, scale=1.0 / n, bias=EPS),
                 ["st%d" % ss_col], ["st%d" % out_col])
            P.op("act", lambda e: e.activation(out=stat[:tn, out_col:out_col + 1], in_=stat[:tn, out_col:out_col + 1],
                                               func=AF.Exp, scale=-0.5),
                 ["st%d" % out_col], ["st%d" % out_col])

        def norm_stats(src, src_res, gslot, tn, xn, xn_res):
            P.op("act", lambda e: e.activation(out=junk[:tn, :], in_=src, func=AF.Square, accum_out=stat[:tn, 0:1]),
                 [src_res], ["junk", "st0"])
            rstd_from_ss(tn, 0, 1, D, "n")
            P.op("dve", lambda e: e.scalar_tensor_tensor(out=xn[:tn, :], in0=src, scalar=stat[:tn, 1:2],
                                                         in1=G[:tn, gslot, :], op0=ALU.mult, op1=ALU.mult),
                 [src_res, "st1", "G%d" % gslot], [xn_res])

        def norm_T(src, src_res, gslot, tn, xn, dstT, dst_res, tpbank):
            P.op("act", lambda e: e.activation(out=junk[:tn, :], in_=src, func=AF.Square, accum_out=stat[:tn, 0:1]),
                 [src_res], ["junk", "st0"])
            rstd_from_ss(tn, 0, 1, D, "n")
            P.op("dve", lambda e: e.scalar_tensor_tensor(out=xn[:tn, :], in0=src, scalar=stat[:tn, 1:2],
                                                         in1=G[:tn, gslot, :], op0=ALU.mult, op1=ALU.mult),
                 [src_res, "st1", "G%d" % gslot], ["xn"])
            transpose8(xn, "xn", tn, dstT, dst_res, tpbank)

        def transpose8(src, src_res, tn, dstT, dst_res, tpbank):
            tp = bf(tpbank)
            for k in range(8):
                P.op("pe", lambda e, k=k: e.transpose(out=tp[:, k * 128:k * 128 + tn], in_=src[:tn, k * 128:(k + 1) * 128],
                                                      identity=identb[:tn, :tn]),
                     [src_res, "identb"], [bk(tpbank)])
            P.op("act", lambda e: e.activation(out=dstT, in_=tp.rearrange("p (k t) -> p k t", k=8)[:, :, :tn], func=AF.Copy),
                 [bk(tpbank)], [dst_res])

        def proj(aT, aT_res, tn, W, W_res, c0, ncols, bank, col0=0, bias_row=None, bias_res=None):
            out = banks[bank][:tn, col0:col0 + ncols]
            for k in range(8):
                last = (k == 7 and bias_row is None)
                P.op("pe", lambda e, k=k, last=last: e.matmul(out, lhsT=aT[:, k, :tn], rhs=W[:, k, c0:c0 + ncols],
                                                                start=(k == 0), stop=last),
                     [aT_res, W_res], [bk(bank)])
            if bias_row is not None:
                P.op("pe", lambda e: e.matmul(out, lhsT=onesb[0:1, :tn], rhs=bias_row[0:1, c0:c0 + ncols],
                                              start=False, stop=True),
                     ["onesb", bias_res], [bk(bank)])

        def postnorm_residual(tn, mbanks, gslot, res_in, res_in_res, dst, dst_res, tmp, tmp_res):
            for i, b in enumerate(mbanks):
                P.op("act", lambda e, i=i, b=b: e.activation(out=junk[:tn, i * 512:(i + 1) * 512], in_=banks[b][:tn, :],
                                                             func=AF.Square, accum_out=stat[:tn, 2 + i:3 + i]),
                     [bk(b)], ["junk%d" % i, "st%d" % (2 + i)])
            P.op("dve", lambda e: e.tensor_tensor(out=stat[:tn, 4:5], in0=stat[:tn, 2:3], in1=stat[:tn, 3:4], op=ALU.add),
                 ["st2", "st3"], ["st4"])
            rstd_from_ss(tn, 4, 5, D, "p")
            for i, b in enumerate(mbanks):
                P.op("dve", lambda e, i=i, b=b: e.scalar_tensor_tensor(
                    out=tmp[:tn, i * 512:(i + 1) * 512], in0=banks[b][:tn, :], scalar=stat[:tn, 5:6],
                    in1=G[:tn, gslot, i * 512:(i + 1) * 512], op0=ALU.mult, op1=ALU.mult),
                     [bk(b), "st5", "G%d" % gslot], [tmp_res])
            P.op("pool", lambda e: e.tensor_tensor(out=dst, in0=res_in, in1=tmp[:tn, :], op=ALU.add),
                 [res_in_res, tmp_res], [dst_res])

        def Hres(t):
            return "H%d" % t

        def tile_rows(t):
            return 128 if t < NT else NS

        AR.reset()
        WIN = AR.alloc([128, 8, GIN], BF16)
        WO0 = AR.alloc([128, 8, D], BF16)
        wg2a = AR.alloc([17, 512], BF16, parts=17)
        gh = AR.alloc([128, D], F32)
        S = AR.alloc([128, D], F32)
        Sbf = AR.alloc([128, D], BF16)
        ltot = AR.alloc([128, 8], F32)
        xt = [AR.alloc([128, D], F32) for _ in range(2)]
        xn = AR.alloc([128, D], BF16)
        aT = AR.alloc([128, 8, 128], BF16)
        zTa = AR.alloc([17, 128], BF16, parts=17)
        ebuf = AR.alloc([128, 512], F32)
        _o = AR.off
        lbuf = AR.alloc([128, 512], F32)
        AR.off = _o
        wgtmp = AR.alloc([17, 512], F32, parts=17)
        AR.off = _o + 512
        sr2 = [AR.alloc([128, D], BF16) for _ in range(2)]
        sr = sr2[0]
        dcol = AR.alloc([128, 4], F32)
        E3 = ebuf
        U0 = AR.off
        E1 = AR.alloc([128, 512], F32); E2 = AR.alloc([128, 512], F32)
        qd = AR.alloc([128, 512], BF16); ki = AR.alloc([128, 512], BF16); ke = AR.alloc([128, 512], BF16)
        vb = AR.alloc([128, D], BF16)
        qkT = AR.alloc([128, 8, 128], BF16)
        attT = AR.alloc([128, 4, 128], BF16)
        on = AR.alloc([128, D], BF16)
        aT2 = [aT, AR.alloc([128, 8, 128], BF16)]
        ke2 = [ke, AR.alloc([128, 512], BF16)]
        qkT2 = [qkT, AR.alloc([128, 8, 128], BF16)]
        vb2 = [vb, AR.alloc([128, D], BF16)]
        dcol2 = [dcol, AR.alloc([128, 4], F32)]
        U1 = AR.off
        AR.off = U0
        cg = AR.alloc([128, 4, 1032], F32)
        U2 = AR.off
        AR.off = U0
        kf = AR.alloc([NS, 512], F32, parts=NS); vf = AR.alloc([NS, D], BF16, parts=NS)
        _oq = AR.off
        qf = AR.alloc([NS, 512], F32, parts=NS); af = AR.alloc([NS, 512], F32, parts=NS)
        _oe = AR.off
        AR.off = _oq
        S0_third = AR.alloc([128, D], F32)
        assert AR.off == _oe
        aqT = AR.alloc([128, 8, NS], F32)
        Qm = AR.alloc([128, 4, NS, NS], BF16)
        Km = AR.alloc([NS, 512], BF16, parts=NS)
        S0 = [AR.alloc([128, D], F32) for _ in range(2)] + [S0_third]
        Sn = S0
        Snb = [AR.alloc([128, D], BF16) for _ in range(2)]
        AR.off = max(AR.off, U1, U2)
        print("GLA arena words", AR.off)

        w_in_v = w_in.rearrange("(k p) n -> p k n", p=128)
        for (c0, c1, key, res) in [(512, 2048, "w0a", "WINa"), (3072, 3088, "w0z", "WINz"), (0, 512, "w0q", "WINq"), (2048, 3072, "w0r", "WINr")]:
            P.op("pool", lambda e, c0=c0, c1=c1: e.dma_start(out=WIN[:, :, c0:c1], in_=w_in_v[:, :, c0:c1]), [], [res], dma=key)
        P.op("pool", lambda e: e.dma_start(out=WO0, in_=w_out0.rearrange("(k p) n -> p k n", p=128)), [], ["WO0"], dma="w1")
        ld("sp", wgtmp[0:16, :], wg2, ["lbuf"], "c2", batch=True)
        ld("sp", wgtmp[16:17, :], bg, ["lbuf"], "c2", batch=True)
        P.op("dve", lambda e: e.tensor_copy(out=wg2a, in_=wgtmp), ["lbuf"], ["wg2a"])
        for h in range(4):
            ld("sp", gh[:, h * 256:(h + 1) * 256], ghead.partition_broadcast(128), ["gh"], "c2", batch=True)
        P.op("pool", lambda e: e.memset(zTa, 1.0), [], ["zTa"])
        P.op("pool", lambda e: e.memset(S, 0.0), [], ["S"])
        P.op("pool", lambda e: e.memset(ltot, 0.0), [], ["ltot"])
        load_gain(0, nmp[0])
        load_gain(1, nmpost[0])

        ZC = 3072

        def gla_front(t, pre):
            tn = tile_rows(t)
            sample = t >= NT
            p = t % 2
            xa = xt[p]
            aTp = aT2[p]; kep = ke2[p]; qkTp = qkT2[p]; vbp = vb2[p]; dcp = dcol2[p]
            RA = "aT%d" % p; RK = "ke%d" % p; RQ = "qkT%d" % p; RV = "vb%d" % p; RD = "dcol%d" % p
            src = xp[t * 128:(t + 1) * 128, :] if not sample else xs
            ld("sp", xa[:tn, :], src, ["xt%d" % p], "x%d" % p)
            xres = "xt%d" % p
            if pre:
                b4 = 4 * p
                TPB = b4; BZ = b4 + 1; BK = b4 + 2; BQ = b4 + 2; BV0 = b4; BV1 = b4 + 3; BD = [b4, b4 + 3]
            else:
                TPB = 7 if sample else 2
                BZ = 2; BK = 6; BQ = 5; BV0 = 3; BV1 = 4; BD = [0, 1]
            norm_T(xa[:tn, :], xres, 0, tn, xn, aTp[:, :, :tn], RA, TPB)
            zps = banks[BZ][0:16, 0:tn]
            for k in range(8):
                P.op("pe", lambda e, k=k: e.matmul(zps, lhsT=WIN[:, k, ZC:ZC + 16], rhs=aTp[:, k, :tn], start=(k == 0), stop=(k == 7)),
                     [RA, "WINz"], [bk(BZ)])
            P.op("act", lambda e: e.activation(out=zTa[0:16, :tn], in_=zps, func=AF.Copy), [bk(BZ)], ["zTa"])
            if not pre:
                proj(aTp, RA, tn, WIN, "WINq", 0, 512, BQ)
            proj(aTp, RA, tn, WIN, "WINa", 512, 512, BK)
            P.op("pe", lambda e: e.matmul(banks[BZ][:tn, :], lhsT=zTa[:, :tn], rhs=wg2a, start=True, stop=True),
                 ["zTa", "wg2a"], [bk(BZ)])
            P.op("act", lambda e: e.activation(out=ebuf[:tn, :], in_=banks[BZ][:tn, :], func=AF.Exp, scale=-1.0), [bk(BZ)], ["ebuf"])
            P.op("act", lambda e: e.activation(out=lbuf[:tn, :], in_=ebuf[:tn, :], func=AF.Ln, bias=1.0, scale=1.0), ["ebuf"], ["lbuf"])
            proj(aTp, RA, tn, WIN, "WINa", 1024, 512, BV0)
            proj(aTp, RA, tn, WIN, "WINa", 1536, 512, BV1)
            if not sample:
                P.op("dve", lambda e: e.tensor_copy(out=vbp[:tn, 0:512], in_=banks[BV0][:tn, :]), [bk(BV0)], [RV])
                P.op("dve", lambda e: e.tensor_copy(out=vbp[:tn, 512:1024], in_=banks[BV1][:tn, :]), [bk(BV1)], [RV])
            CB = 3
            RB = 4 if not pre else BV1
            if not sample:
                if not pre:
                    P.op("pe", lambda e: e.matmul(banks[CB][:tn, :], lhsT=triI[:tn, :tn], rhs=lbuf[:tn, :], start=True, stop=True),
                         ["triI", "lbuf"], [bk(CB)])
                P.op("pe", lambda e: e.matmul(banks[RB][:tn, :], lhsT=triS[:tn, :tn], rhs=lbuf[:tn, :], start=True, stop=True),
                     ["triS", "lbuf"], [bk(RB)])
                for h in range(4):
                    P.op("pe", lambda e, h=h: e.matmul(banks[BZ][:, 256 + h:257 + h], lhsT=lbuf[:tn, h * 128:(h + 1) * 128],
                                                       rhs=onesf[:tn, 0:1], start=True, stop=True),
                         ["lbuf", "onesf"], [bk(BZ)])
                P.op("act", lambda e: e.activation(out=dcp, in_=banks[BZ][:, 256:260], func=AF.Exp, scale=-1.0 / 16), [bk(BZ)], [RD])
                if pre:
                    P.op("dve", lambda e: e.tensor_tensor(out=ltot[:, 0:4], in0=ltot[:, 0:4], in1=banks[BZ][:, 256:260], op=ALU.add),
                         ["ltot", bk(BZ)], ["ltot"])
                P.op("act", lambda e: e.activation(out=E3[:tn, :], in_=banks[RB][:tn, :], func=AF.Exp, scale=-1.0 / 16), [bk(RB)], ["ebuf"])
                P.op("dve", lambda e: e.tensor_tensor(out=kep[:tn, :], in0=banks[BK][:tn, :], in1=E3[:tn, :], op=ALU.mult),
                     [bk(BK), "ebuf"], [RK])
                if not pre:
                    P.op("act", lambda e: e.activation(out=E1[:tn, :], in_=banks[CB][:tn, :], func=AF.Exp, scale=-1.0 / 16,
                                                       bias=float(np.log(128.0 ** -0.5))), [bk(CB)], ["E1"])
                    P.op("act", lambda e: e.activation(out=E2[:tn, :], in_=banks[CB][:tn, :], func=AF.Exp, scale=1.0 / 16), [bk(CB)], ["E2"])
                    P.op("dve", lambda e: e.tensor_tensor(out=qd[:tn, :], in0=banks[5][:tn, :], in1=E1[:tn, :], op=ALU.mult),
                         [bk(5), "E1"], ["qd"])
                    P.op("dve", lambda e: e.tensor_tensor(out=ki[:tn, :], in0=banks[6][:tn, :], in1=E2[:tn, :], op=ALU.mult),
                         [bk(6), "E2"], ["ki"])
            else:
                P.op("dve", lambda e: e.tensor_copy(out=vf[:, 0:512], in_=banks[3][:tn, :]), [bk(3)], ["vf"])
                P.op("dve", lambda e: e.tensor_copy(out=vf[:, 512:1024], in_=banks[4][:tn, :]), [bk(4)], ["vf"])
                P.op("act", lambda e: e.activation(out=af, in_=lbuf[:tn, :], func=AF.Exp, scale=-1.0 / 16), ["lbuf"], ["af"])
                P.op("act", lambda e: e.activation(out=qf, in_=banks[5][:tn, :], func=AF.Copy, scale=float(128.0 ** -0.5)), [bk(5)], ["qf"])
                P.op("dve", lambda e: e.tensor_copy(out=kf, in_=banks[6][:tn, :]), [bk(6)], ["kf"])
            if not pre:
                srp = sr2[p]; RS = "sr%d" % p
                proj(aTp, RA, tn, WIN, "WINr", 2048, 512, 3)
                proj(aTp, RA, tn, WIN, "WINr", 2560, 512, 4)
                if not sample:
                    tp = bf(TPB)
                    for h in range(4):
                        P.op("pe", lambda e, h=h: e.transpose(out=tp[:, h * 128:(h + 1) * 128], in_=qd[:tn, h * 128:(h + 1) * 128],
                                                              identity=identb[:tn, :tn]), ["qd", "identb"], [bk(TPB)])
                    for h in range(4):
                        P.op("pe", lambda e, h=h: e.transpose(out=tp[:, (4 + h) * 128:(5 + h) * 128], in_=ki[:tn, h * 128:(h + 1) * 128],
                                                              identity=identb[:tn, :tn]), ["ki", "identb"], [bk(TPB)])
                    P.op("act", lambda e: e.activation(out=qkTp, in_=tp.rearrange("p (k t) -> p k t", k=8), func=AF.Copy), [bk(TPB)], [RQ])
                P.op("act", lambda e: e.activation(out=srp[:tn, 0:512], in_=banks[3][:tn, :], func=AF.Silu), [bk(3)], [RS])
                P.op("act", lambda e: e.activation(out=srp[:tn, 512:1024], in_=banks[4][:tn, :], func=AF.Silu), [bk(4)], [RS])
                P.op("pool", lambda e: e.tensor_tensor(out=srp[:tn, :], in0=srp[:tn, :], in1=gh[:tn, :], op=ALU.mult), [RS, "gh"], [RS])
            if pre:
                for h in range(4):
                    P.op("pe", lambda e, h=h: e.matmul(banks[BD[h // 2]][:, (h % 2) * 256:(h % 2 + 1) * 256],
                                                       lhsT=kep[:tn, h * 128:(h + 1) * 128], rhs=vbp[:tn, h * 256:(h + 1) * 256],
                                                       start=True, stop=True), [RK, RV], [bk(BD[h // 2])])
                for h in range(4):
                    P.op("dve", lambda e, h=h: e.scalar_tensor_tensor(
                        out=S[:, h * 256:(h + 1) * 256], in0=S[:, h * 256:(h + 1) * 256], scalar=dcp[:, h:h + 1],
                        in1=banks[BD[h // 2]][:, (h % 2) * 256:(h % 2 + 1) * 256], op0=ALU.mult, op1=ALU.add),
                         ["S", RD, bk(BD[h // 2])], ["S"])

        def gla_back(t, part):
            tn = tile_rows(t)
            sample = t >= NT
            p = t % 2
            xa = xt[p]; xres = "xt%d" % p
            aTp = aT2[p]; kep = ke2[p]; qkTp = qkT2[p]; vbp = vb2[p]; dcp = dcol2[p]
            RA = "aT%d" % p; RK = "ke%d" % p; RQ = "qkT%d" % p; RV = "vb%d" % p; RD = "dcol%d" % p
            gsr = sr2[p]; RS = "sr%d" % p
            if part == 1:
                if not sample:
                    for h in range(4):
                        P.op("pe", lambda e, h=h: e.matmul(banks[2][:, h * 128:(h + 1) * 128], lhsT=qkTp[:, 4 + h, :], rhs=qkTp[:, h, :],
                                                           start=True, stop=True), [RQ], [bk(2)])
                    P.op("dve", lambda e: e.tensor_tensor(out=attT, in0=banks[2][:, :].rearrange("p (h t) -> p h t", h=4),
                                                          in1=mle[:].unsqueeze(1).broadcast_to([128, 4, 128]), op=ALU.mult),
                         [bk(2), "mle"], ["attT"])
                    for h in range(4):
                        ob = banks[h // 2][:, (h % 2) * 256:(h % 2 + 1) * 256]
                        P.op("pe", lambda e, h=h, ob=ob: e.matmul(ob, lhsT=attT[:, h, :], rhs=vbp[:, h * 256:(h + 1) * 256], start=True, stop=False),
                             ["attT", RV], [bk(h // 2)])
                        P.op("pe", lambda e, h=h, ob=ob: e.matmul(ob, lhsT=qkTp[:, h, :], rhs=Sbf[:, h * 256:(h + 1) * 256], start=False, stop=True),
                             [RQ, "Sbf"], [bk(h // 2)])
                    for h in range(4):
                        P.op("pe", lambda e, h=h: e.matmul(banks[3 + h // 2][:, (h % 2) * 256:(h % 2 + 1) * 256],
                                                           lhsT=kep[:, h * 128:(h + 1) * 128], rhs=vbp[:, h * 256:(h + 1) * 256],
                                                           start=True, stop=True), [RK, RV], [bk(3 + h // 2)])
                    for h in range(4):
                        P.op("dve", lambda e, h=h: e.scalar_tensor_tensor(
                            out=S[:, h * 256:(h + 1) * 256], in0=S[:, h * 256:(h + 1) * 256], scalar=dcp[:, h:h + 1],
                            in1=banks[3 + h // 2][:, (h % 2) * 256:(h % 2 + 1) * 256], op0=ALU.mult, op1=ALU.add),
                             ["S", RD, bk(3 + h // 2)], ["S"])
                    P.op("pool", lambda e: e.tensor_copy(out=Sbf, in_=S), ["S"], ["Sbf"])
                else:
                    gla_sample_state()
                for h in range(4):
                    P.op("act", lambda e, h=h: e.activation(out=junk[:tn, h * 256:(h + 1) * 256],
                                                            in_=banks[h // 2][:tn, (h % 2) * 256:(h % 2 + 1) * 256],
                                                            func=AF.Square, accum_out=stat[:tn, 8 + h:9 + h]),
                         [bk(h // 2)], ["junkh%d" % h, "st%d" % (8 + h)])
                P.op("act", lambda e: e.activation(out=stat[:tn, 12:16], in_=stat[:tn, 8:12], func=AF.Ln, scale=1.0 / 256, bias=EPS),
                     ["st8", "st9", "st10", "st11"], ["st12"])
                P.op("act", lambda e: e.activation(out=stat[:tn, 12:16], in_=stat[:tn, 12:16], func=AF.Exp, scale=-0.5), ["st12"], ["st12"])
                for h in range(4):
                    P.op("dve", lambda e, h=h: e.scalar_tensor_tensor(
                        out=on[:tn, h * 256:(h + 1) * 256], in0=banks[h // 2][:tn, (h % 2) * 256:(h % 2 + 1) * 256],
                        scalar=stat[:tn, 12 + h:13 + h], in1=gsr[:tn, h * 256:(h + 1) * 256], op0=ALU.mult, op1=ALU.mult),
                         [bk(h // 2), "st12", RS], ["on"] + (["S0_0", "S0_1", "S0_2"] if sample else []))
                return
            onT = aTp
            transpose8(on, "on", tn, onT[:, :, :tn], RA, 7)
            proj(onT, RA, tn, WO0, "WO0", 0, 512, 0)
            proj(onT, RA, tn, WO0, "WO0", 512, 512, 1)
            postnorm_residual(tn, [0, 1], 1, xa[:tn, :], xres, H[:tn, t, :], Hres(t), H[:, t, :], Hres(t))

        def gla_tile(t, pre):
            gla_front(t, pre)
            if not pre:
                gla_back(t, 1)
                gla_back(t, 2)

        def gla_sample_state():
            tn = NS
            pt = banks[2][:, 0:8 * NS].rearrange("p (k n) -> p k n", k=8)
            for h in range(4):
                P.op("pe", lambda e, h=h: e.transpose(out=pt[:, h, :], in_=af[:, h * 128:(h + 1) * 128], identity=identf[:tn, :tn]),
                     ["af", "identf"], [bk(2)])
            for h in range(4):
                P.op("pe", lambda e, h=h: e.transpose(out=pt[:, 4 + h, :], in_=qf[:, h * 128:(h + 1) * 128], identity=identf[:tn, :tn]),
                     ["qf", "identf"], [bk(2)])
            P.op("dve", lambda e: e.tensor_copy(out=aqT, in_=pt), [bk(2)], ["aqT"])
            P.op("dve", lambda e: e.tensor_tensor(out=Qm, in0=aqT[:, 4:8, :].unsqueeze(3).broadcast_to([128, 4, NS, NS]),
                                                  in1=eye16[:].unsqueeze(1).broadcast_to([128, 4, NS, NS]), op=ALU.mult),
                 ["aqT", "eye16"], ["Qm"])
            for n in range(NS):
                s0 = S0[n % 3]; sn = Sn[n % 3]
                ld("sp", s0.rearrange("p (h v) -> p h v", h=4), sgla[n].rearrange("h p v -> p h v"), ["S0_%d" % (n % 3)] + (["qf", "af"] if n % 3 == 2 else []), "s0_%d" % (n % 3))
                P.op("dve", lambda e, n=n: e.tensor_scalar(out=Km, in0=kf, scalar1=identf[:NS, n:n + 1],
                                                           scalar2=None, op0=ALU.mult), ["kf", "identf"], ["Km"])
                for h in range(4):
                    P.op("pe", lambda e, h=h: e.matmul(banks[3 + h // 2][:, (h % 2) * 256:(h % 2 + 1) * 256],
                                                       lhsT=Km[:, h * 128:(h + 1) * 128], rhs=vf[:, h * 256:(h + 1) * 256],
                                                       start=True, stop=True), ["Km", "vf"], [bk(3 + h // 2)])
                for h in range(4):
                    P.op("dve", lambda e, h=h, n=n, s0=s0, sn=sn: e.scalar_tensor_tensor(
                        out=sn[:, h * 256:(h + 1) * 256], in0=s0[:, h * 256:(h + 1) * 256], scalar=aqT[:, h, n:n + 1],
                        in1=banks[3 + h // 2][:, (h % 2) * 256:(h % 2 + 1) * 256], op0=ALU.mult, op1=ALU.add),
                         ["S0_%d" % (n % 3), "aqT", bk(3 + h // 2)], ["S0_%d" % (n % 3)])
                P.op("act", lambda e, n=n, sn=sn: e.dma_start(out=gs[n].rearrange("h p v -> p h v"), in_=sn.rearrange("p (h v) -> p h v", h=4)),
                     ["S0_%d" % (n % 3)], [], dma="so_%d" % (n % 3))
                snb = Snb[n % 2]
                P.op("act", lambda e, sn=sn, snb=snb: e.activation(out=snb, in_=sn, func=AF.Copy), ["S0_%d" % (n % 3)], ["Snb%d" % (n % 2)])
                for h in range(4):
                    P.op("pe", lambda e, h=h, n=n, sn=sn: e.matmul(banks[h // 2][:tn, (h % 2) * 256:(h % 2 + 1) * 256],
                                                                   lhsT=Qm[:, h, n, :], rhs=Snb[n % 2][:, h * 256:(h + 1) * 256],
                                                                   start=(n == 0 and h % 2 == 0), stop=(n == NS - 1),
                                                                   skip_group_check=True),
                         ["Qm", "Snb%d" % (n % 2)], [bk(h // 2)])

        def eyecol(n):
            return dcolsel[:NS, n:n + 1]

        dcolsel = identf

        pre_lists = []
        for t in range(NT):
            P.capture()
            gla_front(t, True)
            pre_lists.append(P.end_capture())
        ksp = [int(len(l) * 0.55) for l in pre_lists]
        for o_ in pre_lists[0][:ksp[0]]:
            P.op(*o_)
        for t in range(NT):
            A_ = pre_lists[t][ksp[t]:]
            B_ = pre_lists[t + 1][:ksp[t + 1]] if t + 1 < NT else []
            P.commit_merged(A_, B_)
        P.barrier()
        P.op("sp", lambda e: e.dma_start(out=ag1_in.ap()[:, 0:1024], in_=S), ["S"], ["ag1_in"], dma="ag1w")
        P.op("sp", lambda e: e.dma_start(out=ag1_in.ap()[:, 1024:1032], in_=ltot), ["ltot"], ["ag1_in"], dma="ag1w")
        P.op("pool", lambda e: e.collective_compute("AllGather", ALU.bypass, replica_groups=groups,
                                                    ins=[ag1_in.ap().opt()], outs=[ag1_out.ap().opt()]),
             ["ag1_in"], ["ag1_out"], dma="cc1", inc=1)
        gla_tile(NT, False)
        P.barrier()
        P.op("sp", lambda e: e.dma_start(out=cg, in_=ag1_out.ap().rearrange("(r p) n -> p r n", p=128)), ["ag1_out"], ["cg"], dma="ag1r")
        P.op("pool", lambda e: e.memset(S, 0.0), [], ["S"])
        for j in range(4):
            P.op("dve", lambda e, j=j: e.tensor_scalar(out=stat[:, 8:12], in0=cg[:, j, 1024:1028], scalar1=cmask[:, j:j + 1], scalar2=None,
                                                       op0=ALU.mult), ["cg", "cmask"], ["st8"])
            P.op("act", lambda e: e.activation(out=stat[:, 12:16], in_=stat[:, 8:12], func=AF.Exp, scale=-1.0 / 16), ["st8"], ["st12"])
            P.op("dve", lambda e, j=j: e.tensor_scalar(out=junk[:, :], in0=cg[:, j, 0:1024], scalar1=cmask[:, j:j + 1], scalar2=None,
                                                       op0=ALU.mult), ["cg", "cmask"], ["junk"])
            for h in range(4):
                P.op("dve", lambda e, h=h: e.scalar_tensor_tensor(
                    out=S[:, h * 256:(h + 1) * 256], in0=S[:, h * 256:(h + 1) * 256], scalar=stat[:, 12 + h:13 + h],
                    in1=junk[:, h * 256:(h + 1) * 256], op0=ALU.mult, op1=ALU.add), ["S", "st12", "junk"], ["S"])
        P.op("act", lambda e: e.activation(out=Sbf, in_=S, func=AF.Copy), ["S"], ["Sbf"])
        P.barrier()
        gla_front(0, False)
        for t in range(NT):
            P.capture()
            gla_back(t, 1)
            gla_back(t, 2)
            A_ = P.end_capture()
            B_ = []
            if t + 1 < NT:
                P.capture()
                gla_front(t + 1, False)
                B_ = P.end_capture()
            P.commit_merged(A_, B_)
        P.op("sp", lambda e: e.dma_start(out=gp.rearrange("h p v -> p h v"), in_=S.rearrange("p (h v) -> p h v", h=4)), ["S"], [], dma="gpo")

        if STAGE <= 1:
            for t in range(NT):
                P.op("sp", lambda e, t=t: e.dma_start(out=yp[t * 128:(t + 1) * 128, :], in_=H[:, t, :]), [Hres(t)], [], dma="dbg")
            P.op("sp", lambda e: e.dma_start(out=ys, in_=H[:NS, NT, :]), [Hres(NT)], [], dma="dbg")
            P.emit()
            return nc


        def postnorm_sb(tn, srcs, src_res, gslot, dst, dst_res, tmp, tmp_res):
            for i, sap in enumerate(srcs):
                P.op("act", lambda e, i=i, sap=sap: e.activation(out=junk[:tn, i * 512:(i + 1) * 512], in_=sap,
                                                                 func=AF.Square, accum_out=stat[:tn, 2 + i:3 + i]),
                     [src_res], ["junk%d" % i, "st%d" % (2 + i)])
            P.op("dve", lambda e: e.tensor_tensor(out=stat[:tn, 4:5], in0=stat[:tn, 2:3], in1=stat[:tn, 3:4], op=ALU.add),
                 ["st2", "st3"], ["st4"])
            rstd_from_ss(tn, 4, 5, D, "p")
            for i, sap in enumerate(srcs):
                P.op("dve", lambda e, i=i, sap=sap: e.scalar_tensor_tensor(
                    out=tmp[:tn, i * 512:(i + 1) * 512], in0=sap, scalar=stat[:tn, 5:6],
                    in1=G[:tn, gslot, i * 512:(i + 1) * 512], op0=ALU.mult, op1=ALU.mult),
                     [src_res, "st5", "G%d" % gslot], [tmp_res])
            P.op("dve", lambda e: e.tensor_tensor(out=dst, in0=dst, in1=tmp[:tn, :], op=ALU.add),
                 [dst_res, tmp_res], [dst_res])

        def ffn_layer(l, final):
            P.barrier()
            AR.reset()
            half = NT // 2
            passes = [list(range(0, half)), list(range(half, NTT))]
            NPT = max(len(p) for p in passes)
            acc = AR.alloc([128, NPT, D], F32)
            aTf = AR.alloc([128, 8, max(1, len(passes[0])) * 128], BF16)
            aTf2 = AR.alloc([128, 8, NPT * 128], BF16)
            xn_f2 = AR.alloc([128, D], BF16)
            wu = [AR.alloc([128, 8, 512], BF16) for _ in range(2)]
            wd = [AR.alloc([128, 4, D], BF16) for _ in range(2)]
            uT = [AR.alloc([128, 4, 512], BF16) for _ in range(2)]
            rT = [AR.alloc([128, 512], BF16) for _ in range(2)]
            xn_f = AR.alloc([128, D], BF16)
            ftmp = AR.alloc([128, D], F32)
            load_gain(0, nfp[l])
            load_gain(1, nfpost[l])
            wup_v = wup[l].rearrange("(k p) n -> p k n", p=128)
            wdn_v = wdn[l].rearrange("(f p) n -> p f n", p=128)
            ucount = [0]
            dcount = [0]
            passes = [p for p in passes if p]
            aTfs = [aTf, aTf2]
            offs_p = []
            groups_p = []
            for tiles in passes:
                offs = {}
                o = 0
                for t in tiles:
                    offs[t] = o
                    o += tile_rows(t)
                offs_p.append(offs)
                tot = sum(tile_rows(t) for t in tiles)
                n_g = -(-tot // 512)
                groups_ = []
                i0 = 0
                for gi in range(n_g):
                    cnt = len(tiles) // n_g + (1 if gi < len(tiles) % n_g else 0)
                    groups_.append(tiles[i0:i0 + cnt]); i0 += cnt
                groups_ = [g for g in groups_ if g]
                assert all(sum(tile_rows(t) for t in g) <= 512 for g in groups_)
                groups_p.append(groups_)
            items = [(p, blk, g) for p in range(len(passes)) for blk in range(8) for g in groups_p[p]]

            def emit_norm(p):
                for t in passes[p]:
                    tn = tile_rows(t)
                    o = offs_p[p][t]
                    norm_T(H[:tn, t, :], Hres(t), 0, tn, xn_f, aTfs[p % 2][:, :, o:o + tn], "aTf%d" % (p % 2), 7)

            def wslot(p, blk):
                return (p * 8 + blk) % 2

            def load_w(p, blk):
                slot = wslot(p, blk)
                P.op("pool", lambda e, slot=slot, blk=blk: e.dma_start(out=wu[slot], in_=wup_v[:, :, blk * 512:(blk + 1) * 512]),
                     [], ["wu%d" % slot], dma="wu%d" % slot)
                P.op("pool", lambda e, slot=slot, blk=blk: e.dma_start(out=wd[slot], in_=wdn_v[:, blk * 4:(blk + 1) * 4, :]),
                     [], ["wd%d" % slot], dma="wd%d" % slot)

            def emit_U(it, idx):
                p, blk, g = it
                slot = wslot(p, blk)
                us = idx % 2
                g0 = offs_p[p][g[0]]
                gw = sum(tile_rows(t) for t in g)
                aT_ = aTfs[p % 2]
                for f in range(4):
                    ub = ucount[0] % 3
                    ucount[0] += 1
                    for k in range(8):
                        P.op("pe", lambda e, k=k, f=f, ub=ub, slot=slot, g0=g0, gw=gw, aT_=aT_: e.matmul(
                            banks[ub][:, :gw], lhsT=wu[slot][:, k, f * 128:(f + 1) * 128], rhs=aT_[:, k, g0:g0 + gw],
                            start=(k == 0), stop=(k == 7)), ["wu%d" % slot, "aTf%d" % (p % 2)], [bk(ub)])
                    P.op("act", lambda e, f=f, ub=ub, gw=gw: e.activation(out=rT[f % 2][:, :gw], in_=banks[ub][:, :gw], func=AF.Relu),
                         [bk(ub)], ["rT%d" % (f % 2)])
                    P.op("dve", lambda e, f=f, us=us, gw=gw: e.tensor_tensor(out=uT[us][:, f, :gw], in0=rT[f % 2][:, :gw], in1=rT[f % 2][:, :gw], op=ALU.mult),
                         ["rT%d" % (f % 2)], ["uT%d_%d" % (us, f)])

            def emit_D(it, idx):
                p, blk, g = it
                slot = wslot(p, blk)
                us = idx % 2
                g0 = offs_p[p][g[0]]
                for t in g:
                    tn = tile_rows(t)
                    o_ = offs_p[p][t] - g0
                    j = passes[p].index(t)
                    for hf in range(2):
                        db = 3 + dcount[0] % 4
                        dcount[0] += 1
                        for f in range(4):
                            P.op("pe", lambda e, f=f, db=db, us=us, o_=o_, tn=tn, slot=slot, hf=hf: e.matmul(
                                banks[db][:tn, :], lhsT=uT[us][:, f, o_:o_ + tn], rhs=wd[slot][:, f, hf * 512:(hf + 1) * 512],
                                start=(f == 0), stop=(f == 3)), ["uT%d_%d" % (us, f), "wd%d" % slot], [bk(db)])
                        if blk == 0:
                            P.op("dve", lambda e, db=db, tn=tn, j=j, hf=hf: e.tensor_copy(out=acc[:tn, j, hf * 512:(hf + 1) * 512], in_=banks[db][:tn, :]),
                                 [bk(db)], ["acc%d" % j])
                        else:
                            P.op("dve", lambda e, db=db, tn=tn, j=j, hf=hf: e.tensor_tensor(out=acc[:tn, j, hf * 512:(hf + 1) * 512],
                                                                                        in0=acc[:tn, j, hf * 512:(hf + 1) * 512], in1=banks[db][:tn, :], op=ALU.add),
                                 [bk(db), "acc%d" % j], ["acc%d" % j])
                    if blk == 7:
                        pending_post.append((p, t))

            def emit_post(p, only=None):
                for j, t in enumerate(passes[p]):
                    if only is not None and t not in only:
                        continue
                    tn = tile_rows(t)
                    postnorm_sb(tn, [acc[:tn, j, 0:512], acc[:tn, j, 512:1024]], "acc%d" % j, 1, H[:tn, t, :], Hres(t), ftmp, "ftmp")
                    if final:
                        if t < NT:
                            P.op("sp", lambda e, t=t: e.dma_start(out=yp[t * 128:(t + 1) * 128, :], in_=H[:, t, :]), [Hres(t)], [], dma="yo")
                        else:
                            P.op("sp", lambda e: e.dma_start(out=ys, in_=H[:NS, NT, :]), [Hres(NT)], [], dma="yo")

            def emit_norm_group(p, g):
                for t in g:
                    tn = tile_rows(t)
                    o = offs_p[p][t]
                    norm_T(H[:tn, t, :], Hres(t), 0, tn, xn_f, aTfs[p % 2][:, :, o:o + tn], "aTf%d" % (p % 2), 7)
            pending_post = []
            emit_norm_group(0, groups_p[0][0])
            first_rest = list(groups_p[0][1:])
            loaded = set()
            normed = {0}
            n_items_p = [8 * len(groups_p[p]) for p in range(len(passes))]
            start_p = [sum(n_items_p[:p]) for p in range(len(passes))]
            for idx, it in enumerate(items):
                p = it[0]
                for la in (0, 1):
                    if idx + la < len(items):
                        key = items[idx + la][:2]
                        if key not in loaded:
                            load_w(*key); loaded.add(key)
                if idx == 0:
                    emit_U(it, idx)
                    for g_ in first_rest:
                        emit_norm_group(0, g_)
                if p + 1 < len(passes):
                    q_ = p + 1
                    i_ = idx - start_p[p] - 1
                    tl_ = passes[q_]
                    if 0 <= i_ - 1 < len(tl_):
                        t_ = tl_[i_ - 1]; tn_ = tile_rows(t_); o_ = offs_p[q_][t_]
                        xb = [xn_f, xn_f2][(i_ - 1) % 2]
                        transpose8(xb, "xnf%d" % ((i_ - 1) % 2), tn_, aTfs[q_ % 2][:, :, o_:o_ + tn_], "aTf%d" % (q_ % 2), 7)
                    if 0 <= i_ < len(tl_):
                        t_ = tl_[i_]; tn_ = tile_rows(t_)
                        norm_stats(H[:tn_, t_, :], Hres(t_), 0, tn_, [xn_f, xn_f2][i_ % 2], "xnf%d" % (i_ % 2))
                    assert len(tl_) + 2 < n_items_p[p]
                if idx + 1 < len(items):
                    emit_U(items[idx + 1], idx + 1)
                for (pp_, tt_) in pending_post:
                    emit_post(pp_, [tt_])
                del pending_post[:]
                emit_D(it, idx)
            for (pp_, tt_) in pending_post:
                emit_post(pp_, [tt_])
            del pending_post[:]

        ffn_layer(0, False)
        if STAGE <= 2:
            for t in range(NT):
                P.op("sp", lambda e, t=t: e.dma_start(out=yp[t * 128:(t + 1) * 128, :], in_=H[:, t, :]), [Hres(t)], [], dma="dbg")
            P.op("sp", lambda e: e.dma_start(out=ys, in_=H[:NS, NT, :]), [Hres(NT)], [], dma="dbg")
            P.emit()
            return nc


        P.barrier()
        AR.reset()
        WQ = AR.alloc([128, 8, 1536], BF16)
        bqb = AR.alloc([1, 1536], BF16, parts=1)
        bob = AR.alloc([1, D], BF16, parts=1)
        esink = AR.alloc([128, 16], F32)
        esr = AR.alloc([1, NS, 16], F32, parts=1)
        xn_s = AR.alloc([128, D], BF16)
        aTs = AR.alloc([128, 8, 128], BF16)
        qkf = AR.alloc([128, 1280], F32)
        vfp = AR.alloc([128, 256], F32)
        qkb = AR.alloc([128, 1280], BF16)
        rt = [AR.alloc([128, 20, 8], F32) for _ in range(4)]
        stmp = AR.alloc([128, D], F32)
        U0 = AR.off
        WO1 = AR.alloc([128, 8, D], BF16)
        qT = [AR.alloc([64, 16, 128], BF16, parts=64) for _ in range(3)]
        kT = [AR.alloc([64, 4, 128], BF16, parts=64) for _ in range(4)]
        vaug = [AR.alloc([128, 4, 65], BF16) for _ in range(4)]
        PT = AR.alloc([128, 2, 4, 512], BF16)
        dn = AR.alloc([128, 16], F32)
        on_s = AR.alloc([128, D], BF16)
        onTs = AR.alloc([128, 8, 128], BF16)
        hk = AR.alloc([128, 512], F32)
        _o = AR.off
        hg = AR.alloc([128, 4, 512], F32)
        AR.off = _o
        btmp = AR.alloc([1, 1536], F32, parts=1)
        AR.off = _o + 2048
        U1 = AR.off
        AR.off = U0
        WO2 = AR.alloc([64, 16, D], BF16, parts=64)
        selmat = AR.alloc([NS, NS, 128], BF16, parts=NS)
        Kc = [AR.alloc([128, 256], F32) for _ in range(2)]
        prod = AR.alloc([128, 16, 64], F32)
        sc_all = AR.alloc([128, NS, 16], F32)
        P_all = AR.alloc([128, NS, 16], F32)
        pnew = AR.alloc([NS, 16], F32, parts=NS)
        Pnm = AR.alloc([NS, NS, 16], F32, parts=NS)
        rden = AR.alloc([64, NS * 16], F32, parts=64)
        oTs = AR.alloc([64, 16, NS], BF16, parts=64)
        AR.off = max(AR.off, U1)
        print("SWA arena words", AR.off)

        wq_v = wqkv.rearrange("(k p) n -> p k n", p=128)
        P.op("pool", lambda e: e.dma_start(out=WQ[:, :, 1024:1536], in_=wq_v[:, :, 1024:1536]), [], ["WQkv"], dma="w2a")
        P.op("pool", lambda e: e.dma_start(out=WQ[:, :, 0:1024], in_=wq_v[:, :, 0:1024]), [], ["WQq"], dma="w2b")
        P.op("pool", lambda e: e.dma_start(out=WO1, in_=wo1.rearrange("(k p) n -> p k n", p=128)), [], ["WO1"], dma="w3")
        ld("sp", btmp[0:1, :], bqkv, ["btmp"], "c3")
        P.op("dve", lambda e: e.tensor_copy(out=bqb, in_=btmp), ["btmp"], ["bqb"])
        ld("sp", btmp[0:1, 0:D], bo1, ["btmp"], "c3")
        P.op("dve", lambda e: e.tensor_copy(out=bob, in_=btmp[0:1, 0:D]), ["btmp"], ["bob"])
        ld("sp", esink, sinks.partition_broadcast(128), ["esink"], "c4")
        P.op("act", lambda e: e.activation(out=esink, in_=esink, func=AF.Exp), ["esink"], ["esink"])
        P.op("dve", lambda e: e.tensor_copy(out=esr, in_=esink[0:1, :].unsqueeze(1).broadcast_to([1, NS, 16])), ["esink"], ["esr"])
        for i in range(4):
            P.op("pool", lambda e, i=i: e.memset(vaug[i], 1.0), [], ["vaug%d" % i])
        load_gain(0, nmp[1])
        load_gain(1, nmpost[1])
        cut(2.05)

        def slot_of(t):
            return 2 if t == 0 else t % 2

        def kv_finish(slot, ksrc, ksrc_res, vsrc, vsrc_res, tn):
            P.op("act", lambda e: e.activation(out=vaug[slot][:tn, :, 0:64], in_=vsrc.rearrange("p (j d) -> p j d", j=4), func=AF.Copy),
                 [vsrc_res], ["vaug%d" % slot])
            tp = bf(2)
            for j in range(4):
                P.op("pe", lambda e, j=j: e.transpose(out=tp[0:64, j * 128:j * 128 + tn], in_=ksrc[:tn, j * 64:(j + 1) * 64],
                                                      identity=identb[:tn, :tn]), [ksrc_res, "identb"], [bk(2)])
            P.op("act", lambda e: e.activation(out=kT[slot][:, :, :tn], in_=tp[0:64, 0:512].rearrange("p (j t) -> p j t", j=4)[:, :, :tn], func=AF.Copy),
                 [bk(2)], ["kT%d" % slot])

        def swa_A(t, kv_only=False, slot=None):
            tn = tile_rows(t)
            if slot is None:
                slot = slot_of(t)
            norm_T(H[:tn, t, :], Hres(t), 0, tn, xn_s, aTs[:, :, :tn], "aTs", 2)
            if not kv_only:
                proj(aTs, "aTs", tn, WQ, "WQq", 0, 512, 0, bias_row=bqb, bias_res="bqb")
                proj(aTs, "aTs", tn, WQ, "WQq", 512, 512, 1, bias_row=bqb, bias_res="bqb")
                P.op("act", lambda e: e.activation(out=qkf[:tn, 0:512], in_=banks[0][:tn, :], func=AF.Copy), [bk(0)], ["qkf"])
                P.op("act", lambda e: e.activation(out=qkf[:tn, 512:1024], in_=banks[1][:tn, :], func=AF.Copy), [bk(1)], ["qkf"])
            proj(aTs, "aTs", tn, WQ, "WQkv", 1024, 512, 2, bias_row=bqb, bias_res="bqb")
            P.op("act", lambda e: e.activation(out=qkf[:tn, 1024:1280], in_=banks[2][:tn, 0:256], func=AF.Copy), [bk(2)], ["qkf"])
            if kv_only:
                cut(2.06)
            need_v32 = kv_only or t >= NT - 1
            if need_v32:
                P.op("dve", lambda e: e.tensor_copy(out=vfp[:tn, :], in_=banks[2][:tn, 256:512]), [bk(2), "qkf"], ["vfp"])
            if kv_only:
                cut(2.062)
            h0 = 16 if kv_only else 0
            nh = 20 - h0
            qv = qkf[:tn, :].rearrange("p (h d) -> p h d", d=64)
            x1 = qv[:, h0:20, 0:8]; x2 = qv[:, h0:20, 8:16]
            cb = cosT[:tn, t, :].unsqueeze(1).broadcast_to([tn, nh, 8])
            sb_ = sinT[:tn, t, :].unsqueeze(1).broadcast_to([tn, nh, 8])
            P.op("dve", lambda e: e.tensor_tensor(out=rt[0][:tn, :nh, :], in0=x1, in1=cb, op=ALU.mult), ["qkf", "cos"], ["rt0"])
            P.op("dve", lambda e: e.tensor_tensor(out=rt[1][:tn, :nh, :], in0=x2, in1=sb_, op=ALU.mult), ["qkf", "sin"], ["rt1"])
            P.op("dve", lambda e: e.tensor_tensor(out=rt[2][:tn, :nh, :], in0=x2, in1=cb, op=ALU.mult), ["qkf", "cos"], ["rt2"])
            P.op("dve", lambda e: e.tensor_tensor(out=rt[3][:tn, :nh, :], in0=x1, in1=sb_, op=ALU.mult), ["qkf", "sin"], ["rt3"])
            if kv_only:
                cut(2.064)
            P.op("dve", lambda e: e.tensor_tensor(out=x1, in0=rt[0][:tn, :nh, :], in1=rt[1][:tn, :nh, :], op=ALU.subtract), ["rt0", "rt1", "qkf"], ["qkf"])
            P.op("dve", lambda e: e.tensor_tensor(out=x2, in0=rt[2][:tn, :nh, :], in1=rt[3][:tn, :nh, :], op=ALU.add), ["rt2", "rt3", "qkf"], ["qkf"])
            if kv_only:
                cut(2.07)
            c0 = 1024 if kv_only else 0
            P.op("act", lambda e: e.activation(out=qkb[:tn, c0:1280], in_=qkf[:tn, c0:1280], func=AF.Copy), ["qkf"], ["qkb"])
            if kv_only:
                cut(2.08)
            if t >= NT:
                return
            if not kv_only:
                tps = [bf(0), bf(1)]
                for h in range(16):
                    b = h // 8
                    P.op("pe", lambda e, h=h, b=b: e.transpose(out=tps[b][0:64, (h % 8) * 128:(h % 8) * 128 + tn], in_=qkb[:tn, h * 64:(h + 1) * 64],
                                                               identity=identb[:tn, :tn]), ["qkb", "identb"], [bk(b)])
                for b in range(2):
                    P.op("act", lambda e, b=b: e.activation(out=qT[slot][:, b * 8:(b + 1) * 8, :tn],
                                                            in_=tps[b][0:64, :].rearrange("p (h t) -> p h t", h=8)[:, :, :tn], func=AF.Copy),
                         [bk(b)], ["qT%d" % slot])
            kv_finish(slot, qkb[:, 1024:1280], "qkb", banks[2][:tn, 256:512], bk(2), tn)

        def swa_B(t, part=None):
            tn = 128
            if part == "back":
                transpose8(on_s, "on_s", tn, onTs[:, :, :tn], "onTs", 5)
                proj(onTs, "onTs", tn, WO1, "WO1", 0, 512, 3, bias_row=bob, bias_res="bob")
                proj(onTs, "onTs", tn, WO1, "WO1", 512, 512, 4, bias_row=bob, bias_res="bob")
                postnorm_residual(tn, [3, 4], 1, H[:tn, t, :], Hres(t), H[:tn, t, :], Hres(t), stmp, "stmp")
                return
            slot = slot_of(t)
            pslot = 3 if t == 0 else slot_of(t - 1)
            pmask = pm0 if t == 0 else mge
            pmres = "pm0" if t == 0 else "mge"
            for j in range(4):
                for blk, (ks_, msk, mres) in enumerate([(pslot, pmask, pmres), (slot, mle, "mle")]):
                    b = 3 + (2 * j + blk) % 2
                    P.op("pe", lambda e, j=j, ks_=ks_, b=b: e.matmul(banks[b][:, :], lhsT=kT[ks_][:, j, :], rhs=qT[slot][:, 4 * j:4 * j + 4, :],
                                                                   start=True, stop=True), ["kT%d" % ks_, "qT%d" % slot], [bk(b)])
                    P.op("act", lambda e, j=j, blk=blk, b=b: e.activation(out=PT[:, blk, j, :], in_=banks[b][:, :], func=AF.Exp, scale=0.125),
                         [bk(b)], ["PT%d_%d" % (blk, j)])
                    eng = "pool" if blk == 0 else "dve"
                    P.op(eng, lambda e, j=j, blk=blk, msk=msk: e.tensor_tensor(
                        out=PT[:, blk, j, :].rearrange("p (g q) -> p g q", g=4), in0=PT[:, blk, j, :].rearrange("p (g q) -> p g q", g=4),
                        in1=msk[:].unsqueeze(1).broadcast_to([128, 4, 128]), op=ALU.mult), ["PT%d_%d" % (blk, j), mres], ["PT%d_%d" % (blk, j)])
            for h in range(16):
                j = h // 4; g = h % 4
                pb_ = 5 + h // 7
                ob = banks[pb_][:, (h % 7) * 65:(h % 7) * 65 + 65]
                P.op("pe", lambda e, j=j, g=g, ob=ob: e.matmul(ob, lhsT=PT[:, 0, j, g * 128:(g + 1) * 128], rhs=vaug[pslot][:, j, :], start=True, stop=False),
                     ["PT0_%d" % j, "vaug%d" % pslot], [bk(pb_)])
                P.op("pe", lambda e, j=j, g=g, ob=ob: e.matmul(ob, lhsT=PT[:, 1, j, g * 128:(g + 1) * 128], rhs=vaug[slot][:, j, :], start=False, stop=True),
                     ["PT1_%d" % j, "vaug%d" % slot], [bk(pb_)])
            hgroups = [(5, 0, 7), (6, 7, 7), (7, 14, 2)]
            for (pb_, h0, nh_) in hgroups:
                bv = banks[pb_][:, 0:nh_ * 65].rearrange("p (h c) -> p h c", c=65)
                P.op("dve", lambda e, bv=bv, h0=h0, nh_=nh_: e.tensor_tensor(out=dn[:, h0:h0 + nh_], in0=bv[:, :, 64], in1=esink[:, h0:h0 + nh_], op=ALU.add),
                     [bk(pb_), "esink"], ["dn"])
            P.op("dve", lambda e: e.reciprocal(out=dn, in_=dn), ["dn"], ["dn"])
            for (pb_, h0, nh_) in hgroups:
                bv = banks[pb_][:, 0:nh_ * 65].rearrange("p (h c) -> p h c", c=65)
                P.op("dve", lambda e, bv=bv, h0=h0, nh_=nh_: e.tensor_tensor(
                    out=on_s[:, h0 * 64:(h0 + nh_) * 64].rearrange("p (h d) -> p h d", d=64), in0=bv[:, :, 0:64],
                    in1=dn[:, h0:h0 + nh_].unsqueeze(2).broadcast_to([128, nh_, 64]), op=ALU.mult), [bk(pb_), "dn"], ["on_s"])
            if part == "front":
                return
            swa_B(t, "back")

        def swa_sample():
            t = NT
            tn = NS
            P.op("pool", lambda e: e.dma_start(out=WO2, in_=wo1.rearrange("(h p) n -> p h n", p=64)), [], ["WO2"], dma="w4")
            P.op("dve", lambda e: e.tensor_copy(out=selmat, in_=identf[:NS, 0:NS].unsqueeze(2).broadcast_to([NS, NS, 128])), ["identf"], ["selmat"])
            swa_A(t)
            P.op("sp", lambda e: e.dma_start(out=ks[:, 0:127, :], in_=ck[:, 1:128, :]), [], [], dma="co")
            P.op("sp", lambda e: e.dma_start(out=vs[:, 0:127, :], in_=cv[:, 1:128, :]), [], [], dma="co")
            P.op("sp", lambda e: e.dma_start(out=ks[:, 127, :], in_=qkf[:NS, 1024:1280]), ["qkf"], [], dma="co2")
            P.op("sp", lambda e: e.dma_start(out=vs[:, 127, :], in_=vfp[:NS, :]), ["vfp"], [], dma="co3")
            for n in range(NS):
                kc = Kc[n % 2]
                ld("sp", kc, ck[n], ["Kc%d" % (n % 2)], "kc%d" % (n % 2))
                for hf in range(2):
                    P.op("pe", lambda e, n=n, hf=hf: e.matmul(banks[hf][:, :], lhsT=selmat[:, n, :], rhs=qkb[:NS, hf * 512:(hf + 1) * 512], start=True, stop=True),
                         ["selmat", "qkb"], [bk(hf)])
                    P.op("dve", lambda e, hf=hf, kc=kc: e.tensor_tensor(
                        out=prod[:, hf * 8:(hf + 1) * 8, :].rearrange("p (j g) d -> p j g d", g=4),
                        in0=banks[hf][:, :].rearrange("p (j g d) -> p j g d", g=4, d=64),
                        in1=kc[:, hf * 128:(hf + 1) * 128].rearrange("p (j d) -> p j d", d=64).unsqueeze(2).broadcast_to([128, 2, 4, 64]),
                        op=ALU.mult), [bk(hf), "Kc%d" % (n % 2)], ["prod%d" % hf])
                P.op("dve", lambda e, n=n: e.tensor_reduce(out=sc_all[:, n, :], in_=prod, axis=mybir.AxisListType.X, op=ALU.add),
                     ["prod0", "prod1"], ["sc_all"])
            P.op("act", lambda e: e.activation(out=P_all, in_=sc_all, func=AF.Exp, scale=0.125), ["sc_all"], ["P_all"])
            qv4 = qkf[:NS, 0:1024].rearrange("p (j g d) -> p j g d", g=4, d=64)
            kv4 = qkf[:NS, 1024:1280].rearrange("p (j d) -> p j d", d=64).unsqueeze(2).broadcast_to([NS, 4, 4, 64])
            P.op("dve", lambda e: e.tensor_tensor(out=prod[:NS, :, :].rearrange("p (j g) d -> p j g d", g=4), in0=qv4, in1=kv4, op=ALU.mult),
                 ["qkf", "sc_all"], ["prod0", "prod1"])
            P.op("dve", lambda e: e.tensor_reduce(out=pnew, in_=prod[:NS, :, :], axis=mybir.AxisListType.X, op=ALU.add), ["prod0", "prod1"], ["pnew"])
            P.op("act", lambda e: e.activation(out=pnew, in_=pnew, func=AF.Exp, scale=0.125), ["pnew"], ["pnew"])
            P.op("dve", lambda e: e.tensor_tensor(out=Pnm, in0=pnew[:, :].unsqueeze(1).broadcast_to([NS, NS, 16]),
                                                  in1=identf[:NS, 0:NS].unsqueeze(2).broadcast_to([NS, NS, 16]), op=ALU.mult),
                 ["pnew", "identf"], ["Pnm"])
            OT = banks[2][0:64, 0:NS * 16]
            for n in range(NS):
                vc = Kc[n % 2]
                ld("sp", vc, cv[n], ["Kc%d" % (n % 2)], "kc%d" % (n % 2))
                for j in range(4):
                    oap = banks[2][0:64, n * 16 + 4 * j:n * 16 + 4 * j + 4]
                    P.op("pe", lambda e, n=n, j=j, vc=vc, oap=oap: e.matmul(oap, lhsT=vc[:, j * 64:(j + 1) * 64], rhs=P_all[:, n, 4 * j:4 * j + 4], start=True, stop=False),
                         ["Kc%d" % (n % 2), "P_all"], [bk(2)])
                    P.op("pe", lambda e, n=n, j=j, oap=oap: e.matmul(oap, lhsT=vfp[:NS, j * 64:(j + 1) * 64], rhs=Pnm[:, n, 4 * j:4 * j + 4], start=False, stop=True),
                         ["vfp", "Pnm"], [bk(2)])
            DEN = banks[3][0:64, 0:NS * 16]
            P.op("pe", lambda e: e.matmul(DEN, lhsT=onesf[:, 0:64], rhs=P_all[:, :, :].rearrange("p n h -> p (n h)"), start=True, stop=False), ["onesf", "P_all"], [bk(3)])
            P.op("pe", lambda e: e.matmul(DEN, lhsT=onesf[:NS, 0:64], rhs=Pnm[:, :, :].rearrange("p n h -> p (n h)"), start=False, stop=False), ["onesf", "Pnm"], [bk(3)])
            P.op("pe", lambda e: e.matmul(DEN, lhsT=onesf[0:1, 0:64], rhs=esr[:, :, :].rearrange("p n h -> p (n h)"), start=False, stop=True), ["onesf", "esr"], [bk(3)])
            P.op("dve", lambda e: e.reciprocal(out=rden, in_=DEN), [bk(3)], ["rden"])
            P.op("dve", lambda e: e.tensor_tensor(out=oTs, in0=OT.rearrange("p (n h) -> p h n", h=16), in1=rden[:, :].rearrange("p (n h) -> p h n", h=16), op=ALU.mult),
                 [bk(2), "rden"], ["oTs"])
            for hf in range(2):
                for h in range(16):
                    P.op("pe", lambda e, h=h, hf=hf: e.matmul(banks[hf][:NS, :], lhsT=oTs[:, h, :], rhs=WO2[:, h, hf * 512:(hf + 1) * 512], start=(h == 0), stop=False),
                         ["oTs", "WO2"], [bk(hf)])
                P.op("pe", lambda e, hf=hf: e.matmul(banks[hf][:NS, :], lhsT=onesb[0:1, :NS], rhs=bob[0:1, hf * 512:(hf + 1) * 512], start=False, stop=True),
                     ["onesb", "bob"], [bk(hf)])
            postnorm_residual(tn, [0, 1], 1, H[:tn, t, :], Hres(t), H[:tn, t, :], Hres(t), stmp, "stmp")

        swa_A(NT - 1, kv_only=True, slot=3)
        cut(2.1)
        P.op("sp", lambda e: e.dma_start(out=ag2_in.ap()[:, 0:256], in_=qkf[:, 1024:1280]), ["qkf"], ["ag2_in"], dma="ag2w0")
        P.op("sp", lambda e: e.dma_start(out=ag2_in.ap()[:, 256:512], in_=vfp[:, :]), ["vfp"], ["ag2_in"], dma="ag2w1")
        P.op("sp", lambda e: e.dma_start(out=kp, in_=qkf[:, 1024:1280]), ["qkf"], [], dma="kvo0")
        P.op("sp", lambda e: e.dma_start(out=vp, in_=vfp[:, :]), ["vfp"], [], dma="kvo1")
        P.op("pool", lambda e: e.collective_compute("AllGather", ALU.bypass, replica_groups=groups,
                                                    ins=[ag2_in.ap().opt()], outs=[ag2_out.ap().opt()]),
             ["ag2_in"], ["ag2_out"], dma="cc2", inc=1)
        P.op("sp", lambda e: e.dma_start(out=hg, in_=ag2_out.ap().rearrange("(r p) n -> p r n", p=128)), ["ag2_out"], ["hg"], dma="ag2r")
        cut(2.15)
        swa_A(0)
        if STAGE <= 2.2:
            P.barrier()
            for t in range(NT):
                P.op("sp", lambda e, t=t: e.dma_start(out=yp[t * 128:(t + 1) * 128, :], in_=H[:, t, :]), [Hres(t)], [], dma="dbg")
            P.op("sp", lambda e: e.dma_start(out=ys, in_=H[:NS, NT, :]), [Hres(NT)], [], dma="dbg")
            P.emit()
            return nc
        if NT > 1:
            swa_A(1)
        for t in range(1, NT):
            P.capture()
            swa_B(t)
            A_ = P.end_capture()
            B_ = []
            if t + 1 < NT:
                P.capture()
                swa_A(t + 1)
                B_ = P.end_capture()
            P.commit_merged(A_, B_)
        P.op("dve", lambda e: e.tensor_scalar(out=hk, in0=hg[:, 0, :], scalar1=selprev[:, 0:1], scalar2=None, op0=ALU.mult), ["hg", "selprev"], ["hk"])
        for j in range(1, 4):
            P.op("dve", lambda e, j=j: e.scalar_tensor_tensor(out=hk, in0=hg[:, j, :], scalar=selprev[:, j:j + 1], in1=hk, op0=ALU.mult, op1=ALU.add),
                 ["hg", "selprev", "hk"], ["hk"])
        P.op("pool", lambda e: e.tensor_copy(out=qkb[:, 1024:1280], in_=hk[:, 0:256]), ["hk"], ["qkb"])
        kv_finish(3, qkb[:, 1024:1280], "qkb", hk[:, 256:512], "hk", 128)
        swa_B(0)
        P.barrier()
        if STAGE <= 2.5:
            for t in range(NT):
                P.op("sp", lambda e, t=t: e.dma_start(out=yp[t * 128:(t + 1) * 128, :], in_=H[:, t, :]), [Hres(t)], [], dma="dbg")
            P.op("sp", lambda e: e.dma_start(out=ys, in_=H[:NS, NT, :]), [Hres(NT)], [], dma="dbg")
            P.emit()
            return nc
        swa_sample()

        if STAGE <= 3:
            for t in range(NT):
                P.op("sp", lambda e, t=t: e.dma_start(out=yp[t * 128:(t + 1) * 128, :], in_=H[:, t, :]), [Hres(t)], [], dma="dbg")
            P.op("sp", lambda e: e.dma_start(out=ys, in_=H[:NS, NT, :]), [Hres(NT)], [], dma="dbg")
            P.emit()
            return nc

        ffn_layer(1, True)
        P.emit()
    return nc


_CACHE = {}


def _consts(NT, c):
    r = np.arange(128)
    ident = np.eye(128, dtype=np.float32)
    triI = (r[:, None] <= r[None, :]).astype(np.float32)
    triS = (r[:, None] > r[None, :]).astype(np.float32)
    mge = (r[:, None] >= r[None, :]).astype(np.float32)
    pm0 = mge if (c % 4) != 0 else np.zeros((128, 128), np.float32)
    eye16 = np.broadcast_to(np.eye(16, dtype=np.float32), (128, 16, 16)).copy()
    half = 8
    inv = np.power(np.float32(500000.0), -np.arange(half, dtype=np.float32) * np.float32(2.0) / np.float32(16)).astype(np.float32)
    tok0 = (c % 4) * NT * 128
    pos = np.concatenate([tok0 + np.arange(NT * 128), np.full(128, PAST_LEN)]).astype(np.float32)
    ang = pos[:, None] * inv[None, :]
    cos = np.cos(ang).astype(np.float32)
    sin = np.sin(ang).astype(np.float32)
    cm = np.zeros((128, 4), np.float32)
    sp = np.zeros((128, 4), np.float32)
    for j in range(4):
        if j < (c % 4):
            cm[:, j] = 1.0
        if j == (c % 4) - 1:
            sp[:, j] = 1.0
    return dict(c_ident=ident, c_triI=triI, c_triS=triS, c_mge=mge, c_pm0=pm0, c_eye16=eye16, c_cos=cos, c_sin=sin,
                c_cmask=cm, c_selprev=sp)


def kernel(x_prompt, x_sample, state_gla, cache_swa_k, cache_swa_v,
           gla_w_in, gla_w_gate2, gla_b_gate, gla_g_head, gla_w_out,
           swa_w_qkv, swa_b_qkv, swa_sinks, swa_w_out, swa_b_out,
           norm_mix_pre, norm_mix_post, norm_ffn_pre, norm_ffn_post,
           ffn_w_up, ffn_w_down):
    f = lambda a: np.ascontiguousarray(np.asarray(a, dtype=np.float32))
    B, L, _ = x_prompt.shape
    NT = (B * L) // (NCORES * 128)
    key = (NT, STAGE)
    if key not in _CACHE:
        _CACHE[key] = build(NT)
    nc = _CACHE[key]
    xpf = f(x_prompt).reshape(B * L, D)
    xsf = f(x_sample).reshape(-1, D)
    shared = dict(
        w_in=f(gla_w_in[0]), wg2=f(gla_w_gate2[0]), bg=f(gla_b_gate[0]).reshape(1, 512), ghead=f(gla_g_head[0]),
        w_out0=f(gla_w_out[0]), wqkv=f(swa_w_qkv[0]), bqkv=f(swa_b_qkv[0]).reshape(1, 1536), sinks=f(swa_sinks[0]),
        wo1=f(swa_w_out[0]), bo1=f(swa_b_out[0]).reshape(1, D),
        nmp=f(norm_mix_pre), nmpost=f(norm_mix_post), nfp=f(norm_ffn_pre), nfpost=f(norm_ffn_post),
        wup=f(ffn_w_up), wdn=f(ffn_w_down))
    sg = f(state_gla[0]); ckf = f(cache_swa_k[0]).reshape(128, 128, 256); cvf = f(cache_swa_v[0]).reshape(128, 128, 256)
    in_maps = []
    for c in range(NCORES):
        m = dict(shared)
        m.update(_consts(NT, c))
        m["xp"] = xpf[c * NT * 128:(c + 1) * NT * 128]
        m["xs"] = xsf[c * NS:(c + 1) * NS]
        m["sgla"] = sg[c * NS:(c + 1) * NS]
        m["ck"] = ckf[c * NS:(c + 1) * NS]
        m["cv"] = cvf[c * NS:(c + 1) * NS]
        in_maps.append(m)
    res = run_bass_kernel_spmd(nc, in_maps, core_ids=list(range(NCORES)))
    R = res.results
    y_p = np.concatenate([R[c]["yp"] for c in range(NCORES)], 0).reshape(B, L, D)
    y_s = np.concatenate([R[c]["ys"] for c in range(NCORES)], 0).reshape(-1, 1, D)
    g_p = np.stack([R[3]["gp"], R[7]["gp"]], 0)[None]
    g_s = np.concatenate([R[c]["gs"] for c in range(NCORES)], 0)[None]
    k_p = np.stack([R[3]["kp"], R[7]["kp"]], 0).reshape(1, B, 128, 4, 64)
    v_p = np.stack([R[3]["vp"], R[7]["vp"]], 0).reshape(1, B, 128, 4, 64)
    k_s = np.concatenate([R[c]["ks"] for c in range(NCORES)], 0).reshape(1, -1, 128, 4, 64)
    v_s = np.concatenate([R[c]["vs"] for c in range(NCORES)], 0).reshape(1, -1, 128, 4, 64)
    return (y_p, y_s, g_p, g_s, k_p, v_p, k_s, v_s)
```

```python
import contextlib
import numpy as np
import concourse.bass as bass
import concourse.mybir as mybir
from concourse.bass_utils import run_bass_kernel_spmd

F32 = mybir.dt.float32
BF16 = mybir.dt.bfloat16
AF = mybir.ActivationFunctionType
ALU = mybir.AluOpType

NCORES = 8
D = 1024
SEQ = 8192
PAST_LEN = 8192
NS = 16
EPS = 1e-6
GIN = 3088
STAGE = 99
ENGS = ("pe", "act", "dve", "pool", "sp")


class Prog:
    def __init__(self, nc):
        self.nc = nc
        self.ops = []
        self.last_w = {}
        self.readers = {}
        self.dma_count = {}
        self.bar = set()

    def capture(self):
        self._cap = []

    def end_capture(self):
        lst = self._cap
        self._cap = None
        return lst

    def commit_merged(self, A, B):
        def banks_of(o):
            return {r for r in list(o[2]) + list(o[3]) if r.startswith("pb")}
        fut = [set() for _ in range(len(A) + 1)]
        for i in range(len(A) - 1, -1, -1):
            fut[i] = fut[i + 1] | banks_of(A[i])
        i = j = 0
        while i < len(A) or j < len(B):
            takeB = False
            if j < len(B):
                if i >= len(A):
                    takeB = True
                elif not (banks_of(B[j]) & fut[i]) and j * len(A) <= i * len(B):
                    takeB = True
            if takeB:
                self.op(*B[j]); j += 1
            else:
                self.op(*A[i]); i += 1

    def op(self, eng, fn, reads=(), writes=(), dma=None, inc=16, batch=False):
        if getattr(self, "_cap", None) is not None:
            self._cap.append((eng, fn, list(reads), list(writes), dma, inc, batch))
            return None
        idx = len(self.ops)
        deps = set(self.bar)
        pbr = [r for r in reads if r.startswith("pb")]
        if pbr:
            reads = [r for r in reads if not r.startswith("pb")]
            writes = list(writes) + pbr
        for r in reads:
            w = self.last_w.get(r)
            if w is not None:
                deps.add(w)
        for r in writes:
            w = self.last_w.get(r)
            if w is not None:
                deps.add(w)
            for rd in self.readers.get(r, ()):
                deps.add(rd)
        if eng == "pe" and dma is None:
            deps = {d for d in deps if not (self.ops[d]["eng"] == "pe" and self.ops[d]["dma"] is None)}
        if dma is not None and batch:
            deps = {d for d in deps if self.ops[d]["dma"] != dma}
        o = dict(eng=eng, fn=fn, deps=deps, dma=dma, signal=False, ev=None, inc=1)
        if dma is not None:
            n = self.dma_count.get(dma, 0) + 1
            self.dma_count[dma] = n
            o["ev"] = (("dma", dma), inc * n)
            o["signal"] = True
            o["inc"] = inc
            o["batch"] = batch
        self.ops.append(o)
        for r in reads:
            self.readers.setdefault(r, []).append(idx)
        for r in writes:
            self.last_w[r] = idx
            self.readers[r] = []
        return idx

    def barrier(self):
        allres = list(set(self.last_w) | set(self.readers))
        ids = []
        for e in ENGS:
            ids.append(self.op(e, lambda en: en.nop(), writes=allres + ["bar_" + e]))
        self.bar = set(ids)
        self.last_w = {}
        self.readers = {}

    def emit(self):
        nc = self.nc
        ops = self.ops
        for o in ops:
            for d in o["deps"]:
                ops[d]["signal"] = True
        for o in ops:
            if o["dma"] is not None and o.get("batch"):
                o["ev"] = (o["ev"][0], o["inc"] * self.dma_count[o["dma"]])
        cnt = {e: 0 for e in ENGS}
        for o in ops:
            if o["dma"] is None and o["signal"]:
                cnt[o["eng"]] += 1
                o["ev"] = (("eng", o["eng"]), cnt[o["eng"]])
        sem_keys = []
        for o in ops:
            if o["ev"] is not None and o["ev"][0] not in sem_keys:
                sem_keys.append(o["ev"][0])
        with contextlib.ExitStack() as st:
            sems = {}
            for k in sem_keys:
                sems[k] = st.enter_context(nc.semaphore("s_" + "_".join(str(x) for x in k)))
            block = st.enter_context(nc.Block())
            final = {}
            for o in ops:
                if o["dma"] is not None:
                    final[o["ev"][0]] = max(final.get(o["ev"][0], 0), o["ev"][1])

            def replay(engname, e):
                waited = {}
                for o in ops:
                    if o["eng"] != engname:
                        continue
                    need = {}
                    for d in o["deps"]:
                        k, v = ops[d]["ev"]
                        if v > need.get(k, 0):
                            need[k] = v
                    for k, v in need.items():
                        if waited.get(k, 0) < v:
                            e.wait_ge(sems[k], v)
                            waited[k] = v
                    ins = o["fn"](e)
                    if o["signal"]:
                        ins.then_inc(sems[o["ev"][0]], o["inc"])
                if engname == "sp":
                    for k, v in final.items():
                        if waited.get(k, 0) < v:
                            e.wait_ge(sems[k], v)
                            waited[k] = v

            @block.tensor
            def _(e):
                replay("pe", e)

            @block.scalar
            def _(e):
                replay("act", e)

            @block.vector
            def _(e):
                replay("dve", e)

            @block.gpsimd
            def _(e):
                replay("pool", e)

            @block.sync
            def _(e):
                replay("sp", e)


class _Done(Exception):
    pass


class Arena:
    def __init__(self, t, words):
        self.t = t
        self.words = words
        self.off = 0

    def reset(self):
        self.off = 0

    def alloc(self, shape, dtype, parts=None):
        parts = parts or shape[0]
        n = int(np.prod(shape[1:]))
        esz = 2 if dtype == BF16 else 4
        words = (n * esz + 3) // 4
        words = (words + 7) // 8 * 8
        assert self.off + words <= self.words, ("arena overflow", self.off, words, self.words)
        ap = self.t[0:parts, self.off:self.off + words]
        if dtype == BF16:
            ap = ap.bitcast(BF16)
        ap = ap[:, 0:n]
        self.off += words
        fs = shape[1:]
        if len(fs) == 2:
            ap = ap.rearrange("p (a b) -> p a b", a=fs[0])
        elif len(fs) == 3:
            ap = ap.rearrange("p (a b c) -> p a b c", a=fs[0], b=fs[1])
        return ap


def build(NT):
    box = {}
    try:
        _build(NT, box)
    except _Done:
        pass
    return box['nc']


def _build(NT, box):
    nc = bass.Bass("TRN2", target_bir_lowering=False)
    box["nc"] = nc
    NTT = NT + 1
    TOK = NT * 128

    def din(name, shape):
        return nc.dram_tensor(name, list(shape), F32, kind="ExternalInput").ap()

    def dout(name, shape):
        return nc.dram_tensor(name, list(shape), F32, kind="ExternalOutput").ap()

    xp = din("xp", [TOK, D]); xs = din("xs", [NS, D])
    sgla = din("sgla", [NS, 4, 128, 256]); ck = din("ck", [NS, 128, 256]); cv = din("cv", [NS, 128, 256])
    w_in = din("w_in", [D, GIN]); wg2 = din("wg2", [16, 512]); bg = din("bg", [1, 512]); ghead = din("ghead", [256])
    w_out0 = din("w_out0", [D, D]); wqkv = din("wqkv", [D, 1536]); bqkv = din("bqkv", [1, 1536])
    sinks = din("sinks", [16]); wo1 = din("wo1", [D, D]); bo1 = din("bo1", [1, D])
    nmp = din("nmp", [2, D]); nmpost = din("nmpost", [2, D]); nfp = din("nfp", [2, D]); nfpost = din("nfpost", [2, D])
    wup = din("wup", [2, D, 4096]); wdn = din("wdn", [2, 4096, D])
    c_ident = din("c_ident", [128, 128]); c_triI = din("c_triI", [128, 128]); c_triS = din("c_triS", [128, 128])
    c_mge = din("c_mge", [128, 128]); c_pm0 = din("c_pm0", [128, 128]); c_eye16 = din("c_eye16", [128, 16, 16])
    c_cos = din("c_cos", [NTT * 128, 8]); c_sin = din("c_sin", [NTT * 128, 8])
    c_cmask = din("c_cmask", [128, 4]); c_selprev = din("c_selprev", [128, 4])

    yp = dout("yp", [TOK, D]); ys = dout("ys", [NS, D])
    gp = dout("gp", [4, 128, 256]); gs = dout("gs", [NS, 4, 128, 256])
    kp = dout("kp", [128, 256]); vp = dout("vp", [128, 256])
    ks = dout("ks", [NS, 128, 256]); vs = dout("vs", [NS, 128, 256])

    ag1_in = nc.dram_tensor("ag1_in", [128, 1032], F32)
    ag1_out = nc.dram_tensor("ag1_out", [4 * 128, 1032], F32)
    ag2_in = nc.dram_tensor("ag2_in", [128, 512], F32)
    ag2_out = nc.dram_tensor("ag2_out", [4 * 128, 512], F32)
    groups = [[0, 1, 2, 3], [4, 5, 6, 7]]

    ARW = 31200
    with contextlib.ExitStack() as ctx:
        def sb(name, shape, dt):
            return ctx.enter_context(nc.sbuf_tensor(name, list(shape), dt))

        H = sb("H", [128, NTT, D], F32)
        G = sb("G", [128, 2, D], F32)
        identf = sb("identf", [128, 128], F32); identb = sb("identb", [128, 128], BF16)
        triI = sb("triI", [128, 128], F32); triS = sb("triS", [128, 128], F32)
        mle = sb("mle", [128, 128], BF16); mge = sb("mge", [128, 128], BF16); pm0 = sb("pm0", [128, 128], BF16)
        onesf = sb("onesf", [128, 128], F32); onesb = sb("onesb", [128, 128], BF16)
        eye16 = sb("eye16", [128, 16, 16], F32)
        cosT = sb("cosT", [128, NTT, 8], F32); sinT = sb("sinT", [128, NTT, 8], F32)
        cmask = sb("cmask", [128, 4], F32); selprev = sb("selprev", [128, 4], F32)
        stat = sb("stat", [128, 16], F32)
        junk = sb("junk", [128, D], F32)
        ctmp = sb("ctmp", [128, 128], F32)
        arena_t = sb("arena", [128, ARW], F32)
        AR = Arena(arena_t, ARW)
        banks = [ctx.enter_context(nc.psum_tensor("pb%d" % i, [128, 512], F32)) for i in range(8)]

        P = Prog(nc)
        uid = [0]

        def cut(v):
            if STAGE <= v:
                P.barrier()
                for t in range(NT):
                    P.op("sp", lambda e, t=t: e.dma_start(out=yp[t * 128:(t + 1) * 128, :], in_=H[:, t, :]), [], [], dma="dbg")
                P.op("sp", lambda e: e.dma_start(out=ys, in_=H[:NS, NT, :]), [], [], dma="dbg")
                P.emit()
                raise _Done()

        def bk(i):
            return "pb%d" % i

        def bf(i):
            return banks[i][:].bitcast(BF16)

        def ld(eng, out, in_, w, key, r=(), batch=False):
            P.op(eng, lambda e: e.dma_start(out=out, in_=in_), reads=r, writes=w, dma=key, batch=batch)

        ld("sp", identf[:], c_ident, ["identf"], "c0", batch=True)
        ld("sp", triI[:], c_triI, ["triI"], "c0", batch=True)
        ld("sp", triS[:], c_triS, ["triS"], "c0", batch=True)
        ld("sp", eye16[:], c_eye16, ["eye16"], "c0", batch=True)
        ld("sp", cosT[:], c_cos.rearrange("(t p) e -> p t e", p=128), ["cos"], "c0", batch=True)
        ld("sp", sinT[:], c_sin.rearrange("(t p) e -> p t e", p=128), ["sin"], "c0", batch=True)
        ld("sp", cmask[:], c_cmask, ["cmask"], "c0", batch=True)
        ld("sp", selprev[:], c_selprev, ["selprev"], "c0", batch=True)
        P.op("dve", lambda e: e.tensor_copy(out=identb[:], in_=identf[:]), ["identf"], ["identb"])
        P.op("dve", lambda e: e.tensor_copy(out=mle[:], in_=triI[:]), ["triI"], ["mle"])
        ld("sp", ctmp[:], c_mge, ["ctmp"], "c1")
        P.op("dve", lambda e: e.tensor_copy(out=mge[:], in_=ctmp[:]), ["ctmp"], ["mge"])
        ld("sp", ctmp[:], c_pm0, ["ctmp"], "c1")
        P.op("dve", lambda e: e.tensor_copy(out=pm0[:], in_=ctmp[:]), ["ctmp"], ["pm0"])
        P.op("pool", lambda e: e.memset(onesf[:], 1.0), [], ["onesf"])
        P.op("pool", lambda e: e.memset(onesb[:], 1.0), [], ["onesb"])

        def load_gain(slot, src_row):
            ld("sp", G[:, slot, :], src_row.partition_broadcast(128), ["G%d" % slot], "g%d" % slot)

        def rstd_from_ss(tn, ss_col, out_col, n, tag):
            P.op("act", lambda e: e.activation(out=stat[:tn, out_col:out_col + 1], in_=stat[:tn, ss_col:ss_col + 1],
                                               func=AF.Ln, scale=1.0 / n, bias=EPS),
                 ["st%d" % ss_col], ["st%d" % out_col])
            P.op("act", lambda e: e.activation(out=stat[:tn, out_col:out_col + 1], in_=stat[:tn, out_col:out_col + 1],
                                               func=AF.Exp, scale=-0.5),
                 ["st%d" % out_col], ["st%d" % out_col])

        def norm_stats(src, src_res, gslot, tn, xn, xn_res):
            P.op("act", lambda e: e.activation(out=junk[:tn, :], in_=src, func=AF.Square, accum_out=stat[:tn, 0:1]),
                 [src_res], ["junk", "st0"])
            rstd_from_ss(tn, 0, 1, D, "n")
            P.op("dve", lambda e: e.scalar_tensor_tensor(out=xn[:tn, :], in0=src, scalar=stat[:tn, 1:2],
                                                         in1=G[:tn, gslot, :], op0=ALU.mult, op1=ALU.mult),
                 [src_res, "st1", "G%d" % gslot], [xn_res])

        def norm_T(src, src_res, gslot, tn, xn, dstT, dst_res, tpbank):
            P.op("act", lambda e: e.activation(out=junk[:tn, :], in_=src, func=AF.Square, accum_out=stat[:tn, 0:1]),
                 [src_res], ["junk", "st0"])
            rstd_from_ss(tn, 0, 1, D, "n")
            P.op("dve", lambda e: e.scalar_tensor_tensor(out=xn[:tn, :], in0=src, scalar=stat[:tn, 1:2],
                                                         in1=G[:tn, gslot, :], op0=ALU.mult, op1=ALU.mult),
                 [src_res, "st1", "G%d" % gslot], ["xn"])
            transpose8(xn, "xn", tn, dstT, dst_res, tpbank)

        def transpose8(src, src_res, tn, dstT, dst_res, tpbank):
            tp = bf(tpbank)
            for k in range(8):
                P.op("pe", lambda e, k=k: e.transpose(out=tp[:, k * 128:k * 128 + tn], in_=src[:tn, k * 128:(k + 1) * 128],
                                                      identity=identb[:tn, :tn]),
                     [src_res, "identb"], [bk(tpbank)])
            P.op("act", lambda e: e.activation(out=dstT, in_=tp.rearrange("p (k t) -> p k t", k=8)[:, :, :tn], func=AF.Copy),
                 [bk(tpbank)], [dst_res])

        def proj(aT, aT_res, tn, W, W_res, c0, ncols, bank, col0=0, bias_row=None, bias_res=None):
            out = banks[bank][:tn, col0:col0 + ncols]
            for k in range(8):
                last = (k == 7 and bias_row is None)
                P.op("pe", lambda e, k=k, last=last: e.matmul(out, lhsT=aT[:, k, :tn], rhs=W[:, k, c0:c0 + ncols],
                                                                start=(k == 0), stop=last),
                     [aT_res, W_res], [bk(bank)])
            if bias_row is not None:
                P.op("pe", lambda e: e.matmul(out, lhsT=onesb[0:1, :tn], rhs=bias_row[0:1, c0:c0 + ncols],
                                              start=False, stop=True),
                     ["onesb", bias_res], [bk(bank)])

        def postnorm_residual(tn, mbanks, gslot, res_in, res_in_res, dst, dst_res, tmp, tmp_res):
            for i, b in enumerate(mbanks):
                P.op("act", lambda e, i=i, b=b: e.activation(out=junk[:tn, i * 512:(i + 1) * 512], in_=banks[b][:tn, :],
                                                             func=AF.Square, accum_out=stat[:tn, 2 + i:3 + i]),
                     [bk(b)], ["junk%d" % i, "st%d" % (2 + i)])
            P.op("dve", lambda e: e.tensor_tensor(out=stat[:tn, 4:5], in0=stat[:tn, 2:3], in1=stat[:tn, 3:4], op=ALU.add),
                 ["st2", "st3"], ["st4"])
            rstd_from_ss(tn, 4, 5, D, "p")
            for i, b in enumerate(mbanks):
                P.op("dve", lambda e, i=i, b=b: e.scalar_tensor_tensor(
                    out=tmp[:tn, i * 512:(i + 1) * 512], in0=banks[b][:tn, :], scalar=stat[:tn, 5:6],
                    in1=G[:tn, gslot, i * 512:(i + 1) * 512], op0=ALU.mult, op1=ALU.mult),
                     [bk(b), "st5", "G%d" % gslot], [tmp_res])
            P.op("pool", lambda e: e.tensor_tensor(out=dst, in0=res_in, in1=tmp[:tn, :], op=ALU.add),
                 [res_in_res, tmp_res], [dst_res])

        def Hres(t):
            return "H%d" % t

        def tile_rows(t):
            return 128 if t < NT else NS

        AR.reset()
        WIN = AR.alloc([128, 8, GIN], BF16)
        WO0 = AR.alloc([128, 8, D], BF16)
        wg2a = AR.alloc([17, 512], BF16, parts=17)
        gh = AR.alloc([128, D], F32)
        S = AR.alloc([128, D], F32)
        Sbf = AR.alloc([128, D], BF16)
        ltot = AR.alloc([128, 8], F32)
        xt = [AR.alloc([128, D], F32) for _ in range(2)]
        xn = AR.alloc([128, D], BF16)
        aT = AR.alloc([128, 8, 128], BF16)
        zTa = AR.alloc([17, 128], BF16, parts=17)
        ebuf = AR.alloc([128, 512], F32)
        _o = AR.off
        lbuf = AR.alloc([128, 512], F32)
        AR.off = _o
        wgtmp = AR.alloc([17, 512], F32, parts=17)
        AR.off = _o + 512
        sr2 = [AR.alloc([128, D], BF16) for _ in range(2)]
        sr = sr2[0]
        dcol = AR.alloc([128, 4], F32)
        E3 = ebuf
        U0 = AR.off
        E1 = AR.alloc([128, 512], F32); E2 = AR.alloc([128, 512], F32)
        qd = AR.alloc([128, 512], BF16); ki = AR.alloc([128, 512], BF16); ke = AR.alloc([128, 512], BF16)
        vb = AR.alloc([128, D], BF16)
        qkT = AR.alloc([128, 8, 128], BF16)
        attT = AR.alloc([128, 4, 128], BF16)
        on = AR.alloc([128, D], BF16)
        aT2 = [aT, AR.alloc([128, 8, 128], BF16)]
        ke2 = [ke, AR.alloc([128, 512], BF16)]
        qkT2 = [qkT, AR.alloc([128, 8, 128], BF16)]
        vb2 = [vb, AR.alloc([128, D], BF16)]
        dcol2 = [dcol, AR.alloc([128, 4], F32)]
        U1 = AR.off
        AR.off = U0
        cg = AR.alloc([128, 4, 1032], F32)
        U2 = AR.off
        AR.off = U0
        kf = AR.alloc([NS, 512], F32, parts=NS); vf = AR.alloc([NS, D], BF16, parts=NS)
        _oq = AR.off
        qf = AR.alloc([NS, 512], F32, parts=NS); af = AR.alloc([NS, 512], F32, parts=NS)
        _oe = AR.off
        AR.off = _oq
        S0_third = AR.alloc([128, D], F32)
        assert AR.off == _oe
        aqT = AR.alloc([128, 8, NS], F32)
        Qm = AR.alloc([128, 4, NS, NS], BF16)
        Km = AR.alloc([NS, 512], BF16, parts=NS)
        S0 = [AR.alloc([128, D], F32) for _ in range(2)] + [S0_third]
        Sn = S0
        Snb = [AR.alloc([128, D], BF16) for _ in range(2)]
        AR.off = max(AR.off, U1, U2)
        print("GLA arena words", AR.off)

        w_in_v = w_in.rearrange("(k p) n -> p k n", p=128)
        for (c0, c1, key, res) in [(512, 2048, "w0a", "WINa"), (3072, 3088, "w0z", "WINz"), (0, 512, "w0q", "WINq"), (2048, 3072, "w0r", "WINr")]:
            P.op("pool", lambda e, c0=c0, c1=c1: e.dma_start(out=WIN[:, :, c0:c1], in_=w_in_v[:, :, c0:c1]), [], [res], dma=key)
        P.op("pool", lambda e: e.dma_start(out=WO0, in_=w_out0.rearrange("(k p) n -> p k n", p=128)), [], ["WO0"], dma="w1")
        ld("sp", wgtmp[0:16, :], wg2, ["lbuf"], "c2", batch=True)
        ld("sp", wgtmp[16:17, :], bg, ["lbuf"], "c2", batch=True)
        P.op("dve", lambda e: e.tensor_copy(out=wg2a, in_=wgtmp), ["lbuf"], ["wg2a"])
        for h in range(4):
            ld("sp", gh[:, h * 256:(h + 1) * 256], ghead.partition_broadcast(128), ["gh"], "c2", batch=True)
        P.op("pool", lambda e: e.memset(zTa, 1.0), [], ["zTa"])
        P.op("pool", lambda e: e.memset(S, 0.0), [], ["S"])
        P.op("pool", lambda e: e.memset(ltot, 0.0), [], ["ltot"])
        load_gain(0, nmp[0])
        load_gain(1, nmpost[0])

        ZC = 3072

        def gla_front(t, pre):
            tn = tile_rows(t)
            sample = t >= NT
            p = t % 2
            xa = xt[p]
            aTp = aT2[p]; kep = ke2[p]; qkTp = qkT2[p]; vbp = vb2[p]; dcp = dcol2[p]
            RA = "aT%d" % p; RK = "ke%d" % p; RQ = "qkT%d" % p; RV = "vb%d" % p; RD = "dcol%d" % p
            src = xp[t * 128:(t + 1) * 128, :] if not sample else xs
            ld("sp", xa[:tn, :], src, ["xt%d" % p], "x%d" % p)
            xres = "xt%d" % p
            if pre:
                b4 = 4 * p
                TPB = b4; BZ = b4 + 1; BK = b4 + 2; BQ = b4 + 2; BV0 = b4; BV1 = b4 + 3; BD = [b4, b4 + 3]
            else:
                TPB = 7 if sample else 2
                BZ = 2; BK = 6; BQ = 5; BV0 = 3; BV1 = 4; BD = [0, 1]
            norm_T(xa[:tn, :], xres, 0, tn, xn, aTp[:, :, :tn], RA, TPB)
            zps = banks[BZ][0:16, 0:tn]
            for k in range(8):
                P.op("pe", lambda e, k=k: e.matmul(zps, lhsT=WIN[:, k, ZC:ZC + 16], rhs=aTp[:, k, :tn], start=(k == 0), stop=(k == 7)),
                     [RA, "WINz"], [bk(BZ)])
            P.op("act", lambda e: e.activation(out=zTa[0:16, :tn], in_=zps, func=AF.Copy), [bk(BZ)], ["zTa"])
            if not pre:
                proj(aTp, RA, tn, WIN, "WINq", 0, 512, BQ)
            proj(aTp, RA, tn, WIN, "WINa", 512, 512, BK)
            P.op("pe", lambda e: e.matmul(banks[BZ][:tn, :], lhsT=zTa[:, :tn], rhs=wg2a, start=True, stop=True),
                 ["zTa", "wg2a"], [bk(BZ)])
            P.op("act", lambda e: e.activation(out=ebuf[:tn, :], in_=banks[BZ][:tn, :], func=AF.Exp, scale=-1.0), [bk(BZ)], ["ebuf"])
            P.op("act", lambda e: e.activation(out=lbuf[:tn, :], in_=ebuf[:tn, :], func=AF.Ln, bias=1.0, scale=1.0), ["ebuf"], ["lbuf"])
            proj(aTp, RA, tn, WIN, "WINa", 1024, 512, BV0)
            proj(aTp, RA, tn, WIN, "WINa", 1536, 512, BV1)
            if not sample:
                P.op("dve", lambda e: e.tensor_copy(out=vbp[:tn, 0:512], in_=banks[BV0][:tn, :]), [bk(BV0)], [RV])
                P.op("dve", lambda e: e.tensor_copy(out=vbp[:tn, 512:1024], in_=banks[BV1][:tn, :]), [bk(BV1)], [RV])
            CB = 3
            RB = 4 if not pre else BV1
            if not sample:
                if not pre:
                    P.op("pe", lambda e: e.matmul(banks[CB][:tn, :], lhsT=triI[:tn, :tn], rhs=lbuf[:tn, :], start=True, stop=True),
                         ["triI", "lbuf"], [bk(CB)])
                P.op("pe", lambda e: e.matmul(banks[RB][:tn, :], lhsT=triS[:tn, :tn], rhs=lbuf[:tn, :], start=True, stop=True),
                     ["triS", "lbuf"], [bk(RB)])
                for h in range(4):
                    P.op("pe", lambda e, h=h: e.matmul(banks[BZ][:, 256 + h:257 + h], lhsT=lbuf[:tn, h * 128:(h + 1) * 128],
                                                       rhs=onesf[:tn, 0:1], start=True, stop=True),
                         ["lbuf", "onesf"], [bk(BZ)])
                P.op("act", lambda e: e.activation(out=dcp, in_=banks[BZ][:, 256:260], func=AF.Exp, scale=-1.0 / 16), [bk(BZ)], [RD])
                if pre:
                    P.op("dve", lambda e: e.tensor_tensor(out=ltot[:, 0:4], in0=ltot[:, 0:4], in1=banks[BZ][:, 256:260], op=ALU.add),
                         ["ltot", bk(BZ)], ["ltot"])
                P.op("act", lambda e: e.activation(out=E3[:tn, :], in_=banks[RB][:tn, :], func=AF.Exp, scale=-1.0 / 16), [bk(RB)], ["ebuf"])
                P.op("dve", lambda e: e.tensor_tensor(out=kep[:tn, :], in0=banks[BK][:tn, :], in1=E3[:tn, :], op=ALU.mult),
                     [bk(BK), "ebuf"], [RK])
                if not pre:
                    P.op("act", lambda e: e.activation(out=E1[:tn, :], in_=banks[CB][:tn, :], func=AF.Exp, scale=-1.0 / 16,
                                                       bias=float(np.log(128.0 ** -0.5))), [bk(CB)], ["E1"])
                    P.op("act", lambda e: e.activation(out=E2[:tn, :], in_=banks[CB][:tn, :], func=AF.Exp, scale=1.0 / 16), [bk(CB)], ["E2"])
                    P.op("dve", lambda e: e.tensor_tensor(out=qd[:tn, :], in0=banks[5][:tn, :], in1=E1[:tn, :], op=ALU.mult),
                         [bk(5), "E1"], ["qd"])
                    P.op("dve", lambda e: e.tensor_tensor(out=ki[:tn, :], in0=banks[6][:tn, :], in1=E2[:tn, :], op=ALU.mult),
                         [bk(6), "E2"], ["ki"])
            else:
                P.op("dve", lambda e: e.tensor_copy(out=vf[:, 0:512], in_=banks[3][:tn, :]), [bk(3)], ["vf"])
                P.op("dve", lambda e: e.tensor_copy(out=vf[:, 512:1024], in_=banks[4][:tn, :]), [bk(4)], ["vf"])
                P.op("act", lambda e: e.activation(out=af, in_=lbuf[:tn, :], func=AF.Exp, scale=-1.0 / 16), ["lbuf"], ["af"])
                P.op("act", lambda e: e.activation(out=qf, in_=banks[5][:tn, :], func=AF.Copy, scale=float(128.0 ** -0.5)), [bk(5)], ["qf"])
                P.op("dve", lambda e: e.tensor_copy(out=kf, in_=banks[6][:tn, :]), [bk(6)], ["kf"])
            if not pre:
                srp = sr2[p]; RS = "sr%d" % p
                proj(aTp, RA, tn, WIN, "WINr", 2048, 512, 3)
                proj(aTp, RA, tn, WIN, "WINr", 2560, 512, 4)
                if not sample:
                    tp = bf(TPB)
                    for h in range(4):
                        P.op("pe", lambda e, h=h: e.transpose(out=tp[:, h * 128:(h + 1) * 128], in_=qd[:tn, h * 128:(h + 1) * 128],
                                                              identity=identb[:tn, :tn]), ["qd", "identb"], [bk(TPB)])
                    for h in range(4):
                        P.op("pe", lambda e, h=h: e.transpose(out=tp[:, (4 + h) * 128:(5 + h) * 128], in_=ki[:tn, h * 128:(h + 1) * 128],
                                                              identity=identb[:tn, :tn]), ["ki", "identb"], [bk(TPB)])
                    P.op("act", lambda e: e.activation(out=qkTp, in_=tp.rearrange("p (k t) -> p k t", k=8), func=AF.Copy), [bk(TPB)], [RQ])
                P.op("act", lambda e: e.activation(out=srp[:tn, 0:512], in_=banks[3][:tn, :], func=AF.Silu), [bk(3)], [RS])
                P.op("act", lambda e: e.activation(out=srp[:tn, 512:1024], in_=banks[4][:tn, :], func=AF.Silu), [bk(4)], [RS])
                P.op("pool", lambda e: e.tensor_tensor(out=srp[:tn, :], in0=srp[:tn, :], in1=gh[:tn, :], op=ALU.mult), [RS, "gh"], [RS])
            if pre:
                for h in range(4):
                    P.op("pe", lambda e, h=h: e.matmul(banks[BD[h // 2]][:, (h % 2) * 256:(h % 2 + 1) * 256],
                                                       lhsT=kep[:tn, h * 128:(h + 1) * 128], rhs=vbp[:tn, h * 256:(h + 1) * 256],
                                                       start=True, stop=True), [RK, RV], [bk(BD[h // 2])])
                for h in range(4):
                    P.op("dve", lambda e, h=h: e.scalar_tensor_tensor(
                        out=S[:, h * 256:(h + 1) * 256], in0=S[:, h * 256:(h + 1) * 256], scalar=dcp[:, h:h + 1],
                        in1=banks[BD[h // 2]][:, (h % 2) * 256:(h % 2 + 1) * 256], op0=ALU.mult, op1=ALU.add),
                         ["S", RD, bk(BD[h // 2])], ["S"])

        def gla_back(t, part):
            tn = tile_rows(t)
            sample = t >= NT
            p = t % 2
            xa = xt[p]; xres = "xt%d" % p
            aTp = aT2[p]; kep = ke2[p]; qkTp = qkT2[p]; vbp = vb2[p]; dcp = dcol2[p]
            RA = "aT%d" % p; RK = "ke%d" % p; RQ = "qkT%d" % p; RV = "vb%d" % p; RD = "dcol%d" % p
            gsr = sr2[p]; RS = "sr%d" % p
            if part == 1:
                if not sample:
                    for h in range(4):
                        P.op("pe", lambda e, h=h: e.matmul(banks[2][:, h * 128:(h + 1) * 128], lhsT=qkTp[:, 4 + h, :], rhs=qkTp[:, h, :],
                                                           start=True, stop=True), [RQ], [bk(2)])
                    P.op("dve", lambda e: e.tensor_tensor(out=attT, in0=banks[2][:, :].rearrange("p (h t) -> p h t", h=4),
                                                          in1=mle[:].unsqueeze(1).broadcast_to([128, 4, 128]), op=ALU.mult),
                         [bk(2), "mle"], ["attT"])
                    for h in range(4):
                        ob = banks[h // 2][:, (h % 2) * 256:(h % 2 + 1) * 256]
                        P.op("pe", lambda e, h=h, ob=ob: e.matmul(ob, lhsT=attT[:, h, :], rhs=vbp[:, h * 256:(h + 1) * 256], start=True, stop=False),
                             ["attT", RV], [bk(h // 2)])
                        P.op("pe", lambda e, h=h, ob=ob: e.matmul(ob, lhsT=qkTp[:, h, :], rhs=Sbf[:, h * 256:(h + 1) * 256], start=False, stop=True),
                             [RQ, "Sbf"], [bk(h // 2)])
                    for h in range(4):
                        P.op("pe", lambda e, h=h: e.matmul(banks[3 + h // 2][:, (h % 2) * 256:(h % 2 + 1) * 256],
                                                           lhsT=kep[:, h * 128:(h + 1) * 128], rhs=vbp[:, h * 256:(h + 1) * 256],
                                                           start=True, stop=True), [RK, RV], [bk(3 + h // 2)])
                    for h in range(4):
                        P.op("dve", lambda e, h=h: e.scalar_tensor_tensor(
                            out=S[:, h * 256:(h + 1) * 256], in0=S[:, h * 256:(h + 1) * 256], scalar=dcp[:, h:h + 1],
                            in1=banks[3 + h // 2][:, (h % 2) * 256:(h % 2 + 1) * 256], op0=ALU.mult, op1=ALU.add),
                             ["S", RD, bk(3 + h // 2)], ["S"])
                    P.op("pool", lambda e: e.tensor_copy(out=Sbf, in_=S), ["S"], ["Sbf"])
                else:
                    gla_sample_state()
                for h in range(4):
                    P.op("act", lambda e, h=h: e.activation(out=junk[:tn, h * 256:(h + 1) * 256],
                                                            in_=banks[h // 2][:tn, (h % 2) * 256:(h % 2 + 1) * 256],
                                                            func=AF.Square, accum_out=stat[:tn, 8 + h:9 + h]),
                         [bk(h // 2)], ["junkh%d" % h, "st%d" % (8 + h)])
                P.op("act", lambda e: e.activation(out=stat[:tn, 12:16], in_=stat[:tn, 8:12], func=AF.Ln, scale=1.0 / 256, bias=EPS),
                     ["st8", "st9", "st10", "st11"], ["st12"])
                P.op("act", lambda e: e.activation(out=stat[:tn, 12:16], in_=stat[:tn, 12:16], func=AF.Exp, scale=-0.5), ["st12"], ["st12"])
                for h in range(4):
                    P.op("dve", lambda e, h=h: e.scalar_tensor_tensor(
                        out=on[:tn, h * 256:(h + 1) * 256], in0=banks[h // 2][:tn, (h % 2) * 256:(h % 2 + 1) * 256],
                        scalar=stat[:tn, 12 + h:13 + h], in1=gsr[:tn, h * 256:(h + 1) * 256], op0=ALU.mult, op1=ALU.mult),
                         [bk(h // 2), "st12", RS], ["on"] + (["S0_0", "S0_1", "S0_2"] if sample else []))
                return
            onT = aTp
            transpose8(on, "on", tn, onT[:, :, :tn], RA, 7)
            proj(onT, RA, tn, WO0, "WO0", 0, 512, 0)
            proj(onT, RA, tn, WO0, "WO0", 512, 512, 1)
            postnorm_residual(tn, [0, 1], 1, xa[:tn, :], xres, H[:tn, t, :], Hres(t), H[:, t, :], Hres(t))

        def gla_tile(t, pre):
            gla_front(t, pre)
            if not pre:
                gla_back(t, 1)
                gla_back(t, 2)

        def gla_sample_state():
            tn = NS
            pt = banks[2][:, 0:8 * NS].rearrange("p (k n) -> p k n", k=8)
            for h in range(4):
                P.op("pe", lambda e, h=h: e.transpose(out=pt[:, h, :], in_=af[:, h * 128:(h + 1) * 128], identity=identf[:tn, :tn]),
                     ["af", "identf"], [bk(2)])
            for h in range(4):
                P.op("pe", lambda e, h=h: e.transpose(out=pt[:, 4 + h, :], in_=qf[:, h * 128:(h + 1) * 128], identity=identf[:tn, :tn]),
                     ["qf", "identf"], [bk(2)])
            P.op("dve", lambda e: e.tensor_copy(out=aqT, in_=pt), [bk(2)], ["aqT"])
            P.op("dve", lambda e: e.tensor_tensor(out=Qm, in0=aqT[:, 4:8, :].unsqueeze(3).broadcast_to([128, 4, NS, NS]),
                                                  in1=eye16[:].unsqueeze(1).broadcast_to([128, 4, NS, NS]), op=ALU.mult),
                 ["aqT", "eye16"], ["Qm"])
            for n in range(NS):
                s0 = S0[n % 3]; sn = Sn[n % 3]
                ld("sp", s0.rearrange("p (h v) -> p h v", h=4), sgla[n].rearrange("h p v -> p h v"), ["S0_%d" % (n % 3)] + (["qf", "af"] if n % 3 == 2 else []), "s0_%d" % (n % 3))
                P.op("dve", lambda e, n=n: e.tensor_scalar(out=Km, in0=kf, scalar1=identf[:NS, n:n + 1],
                                                           scalar2=None, op0=ALU.mult), ["kf", "identf"], ["Km"])
                for h in range(4):
                    P.op("pe", lambda e, h=h: e.matmul(banks[3 + h // 2][:, (h % 2) * 256:(h % 2 + 1) * 256],
                                                       lhsT=Km[:, h * 128:(h + 1) * 128], rhs=vf[:, h * 256:(h + 1) * 256],
                                                       start=True, stop=True), ["Km", "vf"], [bk(3 + h // 2)])
                for h in range(4):
                    P.op("dve", lambda e, h=h, n=n, s0=s0, sn=sn: e.scalar_tensor_tensor(
                        out=sn[:, h * 256:(h + 1) * 256], in0=s0[:, h * 256:(h + 1) * 256], scalar=aqT[:, h, n:n + 1],
                        in1=banks[3 + h // 2][:, (h % 2) * 256:(h % 2 + 1) * 256], op0=ALU.mult, op1=ALU.add),
                         ["S0_%d" % (n % 3), "aqT", bk(3 + h // 2)], ["S0_%d" % (n % 3)])
                P.op("act", lambda e, n=n, sn=sn: e.dma_start(out=gs[n].rearrange("h p v -> p h v"), in_=sn.rearrange("p (h v) -> p h v", h=4)),
                     ["S0_%d" % (n % 3)], [], dma="so_%d" % (n % 3))
                snb = Snb[n % 2]
                P.op("act", lambda e, sn=sn, snb=snb: e.activation(out=snb, in_=sn, func=AF.Copy), ["S0_%d" % (n % 3)], ["Snb%d" % (n % 2)])
                for h in range(4):
                    P.op("pe", lambda e, h=h, n=n, sn=sn: e.matmul(banks[h // 2][:tn, (h % 2) * 256:(h % 2 + 1) * 256],
                                                                   lhsT=Qm[:, h, n, :], rhs=Snb[n % 2][:, h * 256:(h + 1) * 256],
                                                                   start=(n == 0 and h % 2 == 0), stop=(n == NS - 1),
                                                                   skip_group_check=True),
                         ["Qm", "Snb%d" % (n % 2)], [bk(h // 2)])

        def eyecol(n):
            return dcolsel[:NS, n:n + 1]

        dcolsel = identf

        pre_lists = []
        for t in range(NT):
            P.capture()
            gla_front(t, True)
            pre_lists.append(P.end_capture())
        ksp = [int(len(l) * 0.55) for l in pre_lists]
        for o_ in pre_lists[0][:ksp[0]]:
            P.op(*o_)
        for t in range(NT):
            A_ = pre_lists[t][ksp[t]:]
            B_ = pre_lists[t + 1][:ksp[t + 1]] if t + 1 < NT else []
            P.commit_merged(A_, B_)
        P.barrier()
        P.op("sp", lambda e: e.dma_start(out=ag1_in.ap()[:, 0:1024], in_=S), ["S"], ["ag1_in"], dma="ag1w")
        P.op("sp", lambda e: e.dma_start(out=ag1_in.ap()[:, 1024:1032], in_=ltot), ["ltot"], ["ag1_in"], dma="ag1w")
        P.op("pool", lambda e: e.collective_compute("AllGather", ALU.bypass, replica_groups=groups,
                                                    ins=[ag1_in.ap().opt()], outs=[ag1_out.ap().opt()]),
             ["ag1_in"], ["ag1_out"], dma="cc1", inc=1)
        gla_tile(NT, False)
        P.barrier()
        P.op("sp", lambda e: e.dma_start(out=cg, in_=ag1_out.ap().rearrange("(r p) n -> p r n", p=128)), ["ag1_out"], ["cg"], dma="ag1r")
        P.op("pool", lambda e: e.memset(S, 0.0), [], ["S"])
        for j in range(4):
            P.op("dve", lambda e, j=j: e.tensor_scalar(out=stat[:, 8:12], in0=cg[:, j, 1024:1028], scalar1=cmask[:, j:j + 1], scalar2=None,
                                                       op0=ALU.mult), ["cg", "cmask"], ["st8"])
            P.op("act", lambda e: e.activation(out=stat[:, 12:16], in_=stat[:, 8:12], func=AF.Exp, scale=-1.0 / 16), ["st8"], ["st12"])
            P.op("dve", lambda e, j=j: e.tensor_scalar(out=junk[:, :], in0=cg[:, j, 0:1024], scalar1=cmask[:, j:j + 1], scalar2=None,
                                                       op0=ALU.mult), ["cg", "cmask"], ["junk"])
            for h in range(4):
                P.op("dve", lambda e, h=h: e.scalar_tensor_tensor(
                    out=S[:, h * 256:(h + 1) * 256], in0=S[:, h * 256:(h + 1) * 256], scalar=stat[:, 12 + h:13 + h],
                    in1=junk[:, h * 256:(h + 1) * 256], op0=ALU.mult, op1=ALU.add), ["S", "st12", "junk"], ["S"])
        P.op("act", lambda e: e.activation(out=Sbf, in_=S, func=AF.Copy), ["S"], ["Sbf"])
        P.barrier()
        gla_front(0, False)
        for t in range(NT):
            P.capture()
            gla_back(t, 1)
            gla_back(t, 2)
            A_ = P.end_capture()
            B_ = []
            if t + 1 < NT:
                P.capture()
                gla_front(t + 1, False)
                B_ = P.end_capture()
            P.commit_merged(A_, B_)
        P.op("sp", lambda e: e.dma_start(out=gp.rearrange("h p v -> p h v"), in_=S.rearrange("p (h v) -> p h v", h=4)), ["S"], [], dma="gpo")

        if STAGE <= 1:
            for t in range(NT):
                P.op("sp", lambda e, t=t: e.dma_start(out=yp[t * 128:(t + 1) * 128, :], in_=H[:, t, :]), [Hres(t)], [], dma="dbg")
            P.op("sp", lambda e: e.dma_start(out=ys, in_=H[:NS, NT, :]), [Hres(NT)], [], dma="dbg")
            P.emit()
            return nc


        def postnorm_sb(tn, srcs, src_res, gslot, dst, dst_res, tmp, tmp_res):
            for i, sap in enumerate(srcs):
                P.op("act", lambda e, i=i, sap=sap: e.activation(out=junk[:tn, i * 512:(i + 1) * 512], in_=sap,
                                                                 func=AF.Square, accum_out=stat[:tn, 2 + i:3 + i]),
                     [src_res], ["junk%d" % i, "st%d" % (2 + i)])
            P.op("dve", lambda e: e.tensor_tensor(out=stat[:tn, 4:5], in0=stat[:tn, 2:3], in1=stat[:tn, 3:4], op=ALU.add),
                 ["st2", "st3"], ["st4"])
            rstd_from_ss(tn, 4, 5, D, "p")
            for i, sap in enumerate(srcs):
                P.op("dve", lambda e, i=i, sap=sap: e.scalar_tensor_tensor(
                    out=tmp[:tn, i * 512:(i + 1) * 512], in0=sap, scalar=stat[:tn, 5:6],
                    in1=G[:tn, gslot, i * 512:(i + 1) * 512], op0=ALU.mult, op1=ALU.mult),
                     [src_res, "st5", "G%d" % gslot], [tmp_res])
            P.op("dve", lambda e: e.tensor_tensor(out=dst, in0=dst, in1=tmp[:tn, :], op=ALU.add),
                 [dst_res, tmp_res], [dst_res])

        def ffn_layer(l, final):
            P.barrier()
            AR.reset()
            half = NT // 2
            passes = [list(range(0, half)), list(range(half, NTT))]
            NPT = max(len(p) for p in passes)
            acc = AR.alloc([128, NPT, D], F32)
            aTf = AR.alloc([128, 8, max(1, len(passes[0])) * 128], BF16)
            aTf2 = AR.alloc([128, 8, NPT * 128], BF16)
            xn_f2 = AR.alloc([128, D], BF16)
            wu = [AR.alloc([128, 8, 512], BF16) for _ in range(2)]
            wd = [AR.alloc([128, 4, D], BF16) for _ in range(2)]
            uT = [AR.alloc([128, 4, 512], BF16) for _ in range(2)]
            rT = [AR.alloc([128, 512], BF16) for _ in range(2)]
            xn_f = AR.alloc([128, D], BF16)
            ftmp = AR.alloc([128, D], F32)
            load_gain(0, nfp[l])
            load_gain(1, nfpost[l])
            wup_v = wup[l].rearrange("(k p) n -> p k n", p=128)
            wdn_v = wdn[l].rearrange("(f p) n -> p f n", p=128)
            ucount = [0]
            dcount = [0]
            passes = [p for p in passes if p]
            aTfs = [aTf, aTf2]
            offs_p = []
            groups_p = []
            for tiles in passes:
                offs = {}
                o = 0
                for t in tiles:
                    offs[t] = o
                    o += tile_rows(t)
                offs_p.append(offs)
                tot = sum(tile_rows(t) for t in tiles)
                n_g = -(-tot // 512)
                groups_ = []
                i0 = 0
                for gi in range(n_g):
                    cnt = len(tiles) // n_g + (1 if gi < len(tiles) % n_g else 0)
                    groups_.append(tiles[i0:i0 + cnt]); i0 += cnt
                groups_ = [g for g in groups_ if g]
                assert all(sum(tile_rows(t) for t in g) <= 512 for g in groups_)
                groups_p.append(groups_)
            items = [(p, blk, g) for p in range(len(passes)) for blk in range(8) for g in groups_p[p]]

            def emit_norm(p):
                for t in passes[p]:
                    tn = tile_rows(t)
                    o = offs_p[p][t]
                    norm_T(H[:tn, t, :], Hres(t), 0, tn, xn_f, aTfs[p % 2][:, :, o:o + tn], "aTf%d" % (p % 2), 7)

            def wslot(p, blk):
                return (p * 8 + blk) % 2

            def load_w(p, blk):
                slot = wslot(p, blk)
                P.op("pool", lambda e, slot=slot, blk=blk: e.dma_start(out=wu[slot], in_=wup_v[:, :, blk * 512:(blk + 1) * 512]),
                     [], ["wu%d" % slot], dma="wu%d" % slot)
                P.op("pool", lambda e, slot=slot, blk=blk: e.dma_start(out=wd[slot], in_=wdn_v[:, blk * 4:(blk + 1) * 4, :]),
                     [], ["wd%d" % slot], dma="wd%d" % slot)

            def emit_U(it, idx):
                p, blk, g = it
                slot = wslot(p, blk)
                us = idx % 2
                g0 = offs_p[p][g[0]]
                gw = sum(tile_rows(t) for t in g)
                aT_ = aTfs[p % 2]
                for f in range(4):
                    ub = ucount[0] % 3
                    ucount[0] += 1
                    for k in range(8):
                        P.op("pe", lambda e, k=k, f=f, ub=ub, slot=slot, g0=g0, gw=gw, aT_=aT_: e.matmul(
                            banks[ub][:, :gw], lhsT=wu[slot][:, k, f * 128:(f + 1) * 128], rhs=aT_[:, k, g0:g0 + gw],
                            start=(k == 0), stop=(k == 7)), ["wu%d" % slot, "aTf%d" % (p % 2)], [bk(ub)])
                    P.op("act", lambda e, f=f, ub=ub, gw=gw: e.activation(out=rT[f % 2][:, :gw], in_=banks[ub][:, :gw], func=AF.Relu),
                         [bk(ub)], ["rT%d" % (f % 2)])
                    P.op("dve", lambda e, f=f, us=us, gw=gw: e.tensor_tensor(out=uT[us][:, f, :gw], in0=rT[f % 2][:, :gw], in1=rT[f % 2][:, :gw], op=ALU.mult),
                         ["rT%d" % (f % 2)], ["uT%d_%d" % (us, f)])

            def emit_D(it, idx):
                p, blk, g = it
                slot = wslot(p, blk)
                us = idx % 2
                g0 = offs_p[p][g[0]]
                for t in g:
                    tn = tile_rows(t)
                    o_ = offs_p[p][t] - g0
                    j = passes[p].index(t)
                    for hf in range(2):
                        db = 3 + dcount[0] % 4
                        dcount[0] += 1
                        for f in range(4):
                            P.op("pe", lambda e, f=f, db=db, us=us, o_=o_, tn=tn, slot=slot, hf=hf: e.matmul(
                                banks[db][:tn, :], lhsT=uT[us][:, f, o_:o_ + tn], rhs=wd[slot][:, f, hf * 512:(hf + 1) * 512],
                                start=(f == 0), stop=(f == 3)), ["uT%d_%d" % (us, f), "wd%d" % slot], [bk(db)])
                        if blk == 0:
                            P.op("dve", lambda e, db=db, tn=tn, j=j, hf=hf: e.tensor_copy(out=acc[:tn, j, hf * 512:(hf + 1) * 512], in_=banks[db][:tn, :]),
                                 [bk(db)], ["acc%d" % j])
                        else:
                            P.op("dve", lambda e, db=db, tn=tn, j=j, hf=hf: e.tensor_tensor(out=acc[:tn, j, hf * 512:(hf + 1) * 512],
                                                                                        in0=acc[:tn, j, hf * 512:(hf + 1) * 512], in1=banks[db][:tn, :], op=ALU.add),
                                 [bk(db), "acc%d" % j], ["acc%d" % j])
                    if blk == 7:
                        post_new.append((p, t))
                    if post_ready:
                        emit_post(*[(a_, [b_]) for (a_, b_) in [post_ready.pop(0)]][0])

            def emit_post(p, only=None):
                for j, t in enumerate(passes[p]):
                    if only is not None and t not in only:
                        continue
                    tn = tile_rows(t)
                    postnorm_sb(tn, [acc[:tn, j, 0:512], acc[:tn, j, 512:1024]], "acc%d" % j, 1, H[:tn, t, :], Hres(t), ftmp, "ftmp")
                    if final:
                        if t < NT:
                            P.op("sp", lambda e, t=t: e.dma_start(out=yp[t * 128:(t + 1) * 128, :], in_=H[:, t, :]), [Hres(t)], [], dma="yo")
                        else:
                            P.op("sp", lambda e: e.dma_start(out=ys, in_=H[:NS, NT, :]), [Hres(NT)], [], dma="yo")

            def emit_norm_group(p, g):
                for t in g:
                    tn = tile_rows(t)
                    o = offs_p[p][t]
                    norm_T(H[:tn, t, :], Hres(t), 0, tn, xn_f, aTfs[p % 2][:, :, o:o + tn], "aTf%d" % (p % 2), 7)
            post_ready = []
            post_new = []
            emit_norm_group(0, groups_p[0][0])
            first_rest = list(groups_p[0][1:])
            loaded = set()
            normed = {0}
            n_items_p = [8 * len(groups_p[p]) for p in range(len(passes))]
            start_p = [sum(n_items_p[:p]) for p in range(len(passes))]
            for idx, it in enumerate(items):
                p = it[0]
                for la in (0, 1):
                    if idx + la < len(items):
                        key = items[idx + la][:2]
                        if key not in loaded:
                            load_w(*key); loaded.add(key)
                if idx == 0:
                    emit_U(it, idx)
                    for g_ in first_rest:
                        emit_norm_group(0, g_)
                if p + 1 < len(passes):
                    q_ = p + 1
                    i_ = idx - start_p[p] - 1
                    tl_ = passes[q_]
                    if 0 <= i_ - 1 < len(tl_):
                        t_ = tl_[i_ - 1]; tn_ = tile_rows(t_); o_ = offs_p[q_][t_]
                        xb = [xn_f, xn_f2][(i_ - 1) % 2]
                        transpose8(xb, "xnf%d" % ((i_ - 1) % 2), tn_, aTfs[q_ % 2][:, :, o_:o_ + tn_], "aTf%d" % (q_ % 2), 7)
                    if 0 <= i_ < len(tl_):
                        t_ = tl_[i_]; tn_ = tile_rows(t_)
                        norm_stats(H[:tn_, t_, :], Hres(t_), 0, tn_, [xn_f, xn_f2][i_ % 2], "xnf%d" % (i_ % 2))
                    assert len(tl_) + 2 < n_items_p[p]
                if idx + 1 < len(items):
                    emit_U(items[idx + 1], idx + 1)
                post_ready.extend(post_new)
                del post_new[:]
                if idx > 0 and items[idx - 1][0] != it[0]:
                    for (pp_, tt_) in post_ready:
                        emit_post(pp_, [tt_])
                    del post_ready[:]
                emit_D(it, idx)
            for (pp_, tt_) in post_ready + post_new:
                emit_post(pp_, [tt_])
            del post_ready[:]
            del post_new[:]

        ffn_layer(0, False)
        if STAGE <= 2:
            for t in range(NT):
                P.op("sp", lambda e, t=t: e.dma_start(out=yp[t * 128:(t + 1) * 128, :], in_=H[:, t, :]), [Hres(t)], [], dma="dbg")
            P.op("sp", lambda e: e.dma_start(out=ys, in_=H[:NS, NT, :]), [Hres(NT)], [], dma="dbg")
            P.emit()
            return nc


        P.barrier()
        AR.reset()
        WQ = AR.alloc([128, 8, 1536], BF16)
        bqb = AR.alloc([1, 1536], BF16, parts=1)
        bob = AR.alloc([1, D], BF16, parts=1)
        esink = AR.alloc([128, 16], F32)
        esr = AR.alloc([1, NS, 16], F32, parts=1)
        xn_s = AR.alloc([128, D], BF16)
        aTs = AR.alloc([128, 8, 128], BF16)
        qkf = AR.alloc([128, 1280], F32)
        vfp = AR.alloc([128, 256], F32)
        qkb = AR.alloc([128, 1280], BF16)
        rt = [AR.alloc([128, 20, 8], F32) for _ in range(4)]
        stmp = AR.alloc([128, D], F32)
        U0 = AR.off
        WO1 = AR.alloc([128, 8, D], BF16)
        qT = [AR.alloc([64, 16, 128], BF16, parts=64) for _ in range(3)]
        kT = [AR.alloc([64, 4, 128], BF16, parts=64) for _ in range(4)]
        vaug = [AR.alloc([128, 4, 65], BF16) for _ in range(4)]
        PT = AR.alloc([128, 2, 4, 512], BF16)
        dn = AR.alloc([128, 16], F32)
        on_s = AR.alloc([128, D], BF16)
        onTs = AR.alloc([128, 8, 128], BF16)
        hk = AR.alloc([128, 512], F32)
        _o = AR.off
        hg = AR.alloc([128, 4, 512], F32)
        AR.off = _o
        btmp = AR.alloc([1, 1536], F32, parts=1)
        AR.off = _o + 2048
        U1 = AR.off
        AR.off = U0
        WO2 = AR.alloc([64, 16, D], BF16, parts=64)
        selmat = AR.alloc([NS, NS, 128], BF16, parts=NS)
        Kc = [AR.alloc([128, 256], F32) for _ in range(2)]
        prod = AR.alloc([128, 16, 64], F32)
        sc_all = AR.alloc([128, NS, 16], F32)
        P_all = AR.alloc([128, NS, 16], F32)
        pnew = AR.alloc([NS, 16], F32, parts=NS)
        Pnm = AR.alloc([NS, NS, 16], F32, parts=NS)
        rden = AR.alloc([64, NS * 16], F32, parts=64)
        oTs = AR.alloc([64, 16, NS], BF16, parts=64)
        AR.off = max(AR.off, U1)
        print("SWA arena words", AR.off)

        wq_v = wqkv.rearrange("(k p) n -> p k n", p=128)
        P.op("pool", lambda e: e.dma_start(out=WQ[:, :, 1024:1536], in_=wq_v[:, :, 1024:1536]), [], ["WQkv"], dma="w2a")
        P.op("pool", lambda e: e.dma_start(out=WQ[:, :, 0:1024], in_=wq_v[:, :, 0:1024]), [], ["WQq"], dma="w2b")
        P.op("pool", lambda e: e.dma_start(out=WO1, in_=wo1.rearrange("(k p) n -> p k n", p=128)), [], ["WO1"], dma="w3")
        ld("sp", btmp[0:1, :], bqkv, ["btmp"], "c3")
        P.op("dve", lambda e: e.tensor_copy(out=bqb, in_=btmp), ["btmp"], ["bqb"])
        ld("sp", btmp[0:1, 0:D], bo1, ["btmp"], "c3")
        P.op("dve", lambda e: e.tensor_copy(out=bob, in_=btmp[0:1, 0:D]), ["btmp"], ["bob"])
        ld("sp", esink, sinks.partition_broadcast(128), ["esink"], "c4")
        P.op("act", lambda e: e.activation(out=esink, in_=esink, func=AF.Exp), ["esink"], ["esink"])
        P.op("dve", lambda e: e.tensor_copy(out=esr, in_=esink[0:1, :].unsqueeze(1).broadcast_to([1, NS, 16])), ["esink"], ["esr"])
        for i in range(4):
            P.op("pool", lambda e, i=i: e.memset(vaug[i], 1.0), [], ["vaug%d" % i])
        load_gain(0, nmp[1])
        load_gain(1, nmpost[1])
        cut(2.05)

        def slot_of(t):
            return 2 if t == 0 else t % 2

        def kv_finish(slot, ksrc, ksrc_res, vsrc, vsrc_res, tn):
            P.op("act", lambda e: e.activation(out=vaug[slot][:tn, :, 0:64], in_=vsrc.rearrange("p (j d) -> p j d", j=4), func=AF.Copy),
                 [vsrc_res], ["vaug%d" % slot])
            tp = bf(2)
            for j in range(4):
                P.op("pe", lambda e, j=j: e.transpose(out=tp[0:64, j * 128:j * 128 + tn], in_=ksrc[:tn, j * 64:(j + 1) * 64],
                                                      identity=identb[:tn, :tn]), [ksrc_res, "identb"], [bk(2)])
            P.op("act", lambda e: e.activation(out=kT[slot][:, :, :tn], in_=tp[0:64, 0:512].rearrange("p (j t) -> p j t", j=4)[:, :, :tn], func=AF.Copy),
                 [bk(2)], ["kT%d" % slot])

        def swa_A(t, kv_only=False, slot=None):
            tn = tile_rows(t)
            if slot is None:
                slot = slot_of(t)
            norm_T(H[:tn, t, :], Hres(t), 0, tn, xn_s, aTs[:, :, :tn], "aTs", 2)
            if not kv_only:
                proj(aTs, "aTs", tn, WQ, "WQq", 0, 512, 0, bias_row=bqb, bias_res="bqb")
                proj(aTs, "aTs", tn, WQ, "WQq", 512, 512, 1, bias_row=bqb, bias_res="bqb")
                P.op("act", lambda e: e.activation(out=qkf[:tn, 0:512], in_=banks[0][:tn, :], func=AF.Copy), [bk(0)], ["qkf"])
                P.op("act", lambda e: e.activation(out=qkf[:tn, 512:1024], in_=banks[1][:tn, :], func=AF.Copy), [bk(1)], ["qkf"])
            proj(aTs, "aTs", tn, WQ, "WQkv", 1024, 512, 2, bias_row=bqb, bias_res="bqb")
            P.op("act", lambda e: e.activation(out=qkf[:tn, 1024:1280], in_=banks[2][:tn, 0:256], func=AF.Copy), [bk(2)], ["qkf"])
            if kv_only:
                cut(2.06)
            need_v32 = kv_only or t >= NT - 1
            if need_v32:
                P.op("dve", lambda e: e.tensor_copy(out=vfp[:tn, :], in_=banks[2][:tn, 256:512]), [bk(2), "qkf"], ["vfp"])
            if kv_only:
                cut(2.062)
            h0 = 16 if kv_only else 0
            nh = 20 - h0
            qv = qkf[:tn, :].rearrange("p (h d) -> p h d", d=64)
            x1 = qv[:, h0:20, 0:8]; x2 = qv[:, h0:20, 8:16]
            cb = cosT[:tn, t, :].unsqueeze(1).broadcast_to([tn, nh, 8])
            sb_ = sinT[:tn, t, :].unsqueeze(1).broadcast_to([tn, nh, 8])
            P.op("dve", lambda e: e.tensor_tensor(out=rt[0][:tn, :nh, :], in0=x1, in1=cb, op=ALU.mult), ["qkf", "cos"], ["rt0"])
            P.op("dve", lambda e: e.tensor_tensor(out=rt[1][:tn, :nh, :], in0=x2, in1=sb_, op=ALU.mult), ["qkf", "sin"], ["rt1"])
            P.op("dve", lambda e: e.tensor_tensor(out=rt[2][:tn, :nh, :], in0=x2, in1=cb, op=ALU.mult), ["qkf", "cos"], ["rt2"])
            P.op("dve", lambda e: e.tensor_tensor(out=rt[3][:tn, :nh, :], in0=x1, in1=sb_, op=ALU.mult), ["qkf", "sin"], ["rt3"])
            if kv_only:
                cut(2.064)
            P.op("dve", lambda e: e.tensor_tensor(out=x1, in0=rt[0][:tn, :nh, :], in1=rt[1][:tn, :nh, :], op=ALU.subtract), ["rt0", "rt1", "qkf"], ["qkf"])
            P.op("dve", lambda e: e.tensor_tensor(out=x2, in0=rt[2][:tn, :nh, :], in1=rt[3][:tn, :nh, :], op=ALU.add), ["rt2", "rt3", "qkf"], ["qkf"])
            if kv_only:
                cut(2.07)
            c0 = 1024 if kv_only else 0
            P.op("act", lambda e: e.activation(out=qkb[:tn, c0:1280], in_=qkf[:tn, c0:1280], func=AF.Copy), ["qkf"], ["qkb"])
            if kv_only:
                cut(2.08)
            if t >= NT:
                return
            if not kv_only:
                tps = [bf(0), bf(1)]
                for h in range(16):
                    b = h // 8
                    P.op("pe", lambda e, h=h, b=b: e.transpose(out=tps[b][0:64, (h % 8) * 128:(h % 8) * 128 + tn], in_=qkb[:tn, h * 64:(h + 1) * 64],
                                                               identity=identb[:tn, :tn]), ["qkb", "identb"], [bk(b)])
                for b in range(2):
                    P.op("act", lambda e, b=b: e.activation(out=qT[slot][:, b * 8:(b + 1) * 8, :tn],
                                                            in_=tps[b][0:64, :].rearrange("p (h t) -> p h t", h=8)[:, :, :tn], func=AF.Copy),
                         [bk(b)], ["qT%d" % slot])
            kv_finish(slot, qkb[:, 1024:1280], "qkb", banks[2][:tn, 256:512], bk(2), tn)

        def swa_B(t, part=None):
            tn = 128
            if part == "back":
                transpose8(on_s, "on_s", tn, onTs[:, :, :tn], "onTs", 5)
                proj(onTs, "onTs", tn, WO1, "WO1", 0, 512, 3, bias_row=bob, bias_res="bob")
                proj(onTs, "onTs", tn, WO1, "WO1", 512, 512, 4, bias_row=bob, bias_res="bob")
                postnorm_residual(tn, [3, 4], 1, H[:tn, t, :], Hres(t), H[:tn, t, :], Hres(t), stmp, "stmp")
                return
            slot = slot_of(t)
            pslot = 3 if t == 0 else slot_of(t - 1)
            pmask = pm0 if t == 0 else mge
            pmres = "pm0" if t == 0 else "mge"
            for j in range(4):
                for blk, (ks_, msk, mres) in enumerate([(pslot, pmask, pmres), (slot, mle, "mle")]):
                    b = 3 + (2 * j + blk) % 2
                    P.op("pe", lambda e, j=j, ks_=ks_, b=b: e.matmul(banks[b][:, :], lhsT=kT[ks_][:, j, :], rhs=qT[slot][:, 4 * j:4 * j + 4, :],
                                                                   start=True, stop=True), ["kT%d" % ks_, "qT%d" % slot], [bk(b)])
                    P.op("act", lambda e, j=j, blk=blk, b=b: e.activation(out=PT[:, blk, j, :], in_=banks[b][:, :], func=AF.Exp, scale=0.125),
                         [bk(b)], ["PT%d_%d" % (blk, j)])
                    eng = "pool" if blk == 0 else "dve"
                    P.op(eng, lambda e, j=j, blk=blk, msk=msk: e.tensor_tensor(
                        out=PT[:, blk, j, :].rearrange("p (g q) -> p g q", g=4), in0=PT[:, blk, j, :].rearrange("p (g q) -> p g q", g=4),
                        in1=msk[:].unsqueeze(1).broadcast_to([128, 4, 128]), op=ALU.mult), ["PT%d_%d" % (blk, j), mres], ["PT%d_%d" % (blk, j)])
            for h in range(16):
                j = h // 4; g = h % 4
                pb_ = 5 + h // 7
                ob = banks[pb_][:, (h % 7) * 65:(h % 7) * 65 + 65]
                P.op("pe", lambda e, j=j, g=g, ob=ob: e.matmul(ob, lhsT=PT[:, 0, j, g * 128:(g + 1) * 128], rhs=vaug[pslot][:, j, :], start=True, stop=False),
                     ["PT0_%d" % j, "vaug%d" % pslot], [bk(pb_)])
                P.op("pe", lambda e, j=j, g=g, ob=ob: e.matmul(ob, lhsT=PT[:, 1, j, g * 128:(g + 1) * 128], rhs=vaug[slot][:, j, :], start=False, stop=True),
                     ["PT1_%d" % j, "vaug%d" % slot], [bk(pb_)])
            hgroups = [(5, 0, 7), (6, 7, 7), (7, 14, 2)]
            for (pb_, h0, nh_) in hgroups:
                bv = banks[pb_][:, 0:nh_ * 65].rearrange("p (h c) -> p h c", c=65)
                P.op("dve", lambda e, bv=bv, h0=h0, nh_=nh_: e.tensor_tensor(out=dn[:, h0:h0 + nh_], in0=bv[:, :, 64], in1=esink[:, h0:h0 + nh_], op=ALU.add),
                     [bk(pb_), "esink"], ["dn"])
            P.op("dve", lambda e: e.reciprocal(out=dn, in_=dn), ["dn"], ["dn"])
            for (pb_, h0, nh_) in hgroups:
                bv = banks[pb_][:, 0:nh_ * 65].rearrange("p (h c) -> p h c", c=65)
                P.op("dve", lambda e, bv=bv, h0=h0, nh_=nh_: e.tensor_tensor(
                    out=on_s[:, h0 * 64:(h0 + nh_) * 64].rearrange("p (h d) -> p h d", d=64), in0=bv[:, :, 0:64],
                    in1=dn[:, h0:h0 + nh_].unsqueeze(2).broadcast_to([128, nh_, 64]), op=ALU.mult), [bk(pb_), "dn"], ["on_s"])
            if part == "front":
                return
            swa_B(t, "back")

        def swa_sample():
            t = NT
            tn = NS
            P.op("pool", lambda e: e.dma_start(out=WO2, in_=wo1.rearrange("(h p) n -> p h n", p=64)), [], ["WO2"], dma="w4")
            P.op("dve", lambda e: e.tensor_copy(out=selmat, in_=identf[:NS, 0:NS].unsqueeze(2).broadcast_to([NS, NS, 128])), ["identf"], ["selmat"])
            swa_A(t)
            P.op("sp", lambda e: e.dma_start(out=ks[:, 0:127, :], in_=ck[:, 1:128, :]), [], [], dma="co")
            P.op("sp", lambda e: e.dma_start(out=vs[:, 0:127, :], in_=cv[:, 1:128, :]), [], [], dma="co")
            P.op("sp", lambda e: e.dma_start(out=ks[:, 127, :], in_=qkf[:NS, 1024:1280]), ["qkf"], [], dma="co2")
            P.op("sp", lambda e: e.dma_start(out=vs[:, 127, :], in_=vfp[:NS, :]), ["vfp"], [], dma="co3")
            for n in range(NS):
                kc = Kc[n % 2]
                ld("sp", kc, ck[n], ["Kc%d" % (n % 2)], "kc%d" % (n % 2))
                for hf in range(2):
                    P.op("pe", lambda e, n=n, hf=hf: e.matmul(banks[hf][:, :], lhsT=selmat[:, n, :], rhs=qkb[:NS, hf * 512:(hf + 1) * 512], start=True, stop=True),
                         ["selmat", "qkb"], [bk(hf)])
                    P.op("dve", lambda e, hf=hf, kc=kc: e.tensor_tensor(
                        out=prod[:, hf * 8:(hf + 1) * 8, :].rearrange("p (j g) d -> p j g d", g=4),
                        in0=banks[hf][:, :].rearrange("p (j g d) -> p j g d", g=4, d=64),
                        in1=kc[:, hf * 128:(hf + 1) * 128].rearrange("p (j d) -> p j d", d=64).unsqueeze(2).broadcast_to([128, 2, 4, 64]),
                        op=ALU.mult), [bk(hf), "Kc%d" % (n % 2)], ["prod%d" % hf])
                P.op("dve", lambda e, n=n: e.tensor_reduce(out=sc_all[:, n, :], in_=prod, axis=mybir.AxisListType.X, op=ALU.add),
                     ["prod0", "prod1"], ["sc_all"])
            P.op("act", lambda e: e.activation(out=P_all, in_=sc_all, func=AF.Exp, scale=0.125), ["sc_all"], ["P_all"])
            qv4 = qkf[:NS, 0:1024].rearrange("p (j g d) -> p j g d", g=4, d=64)
            kv4 = qkf[:NS, 1024:1280].rearrange("p (j d) -> p j d", d=64).unsqueeze(2).broadcast_to([NS, 4, 4, 64])
            P.op("dve", lambda e: e.tensor_tensor(out=prod[:NS, :, :].rearrange("p (j g) d -> p j g d", g=4), in0=qv4, in1=kv4, op=ALU.mult),
                 ["qkf", "sc_all"], ["prod0", "prod1"])
            P.op("dve", lambda e: e.tensor_reduce(out=pnew, in_=prod[:NS, :, :], axis=mybir.AxisListType.X, op=ALU.add), ["prod0", "prod1"], ["pnew"])
            P.op("act", lambda e: e.activation(out=pnew, in_=pnew, func=AF.Exp, scale=0.125), ["pnew"], ["pnew"])
            P.op("dve", lambda e: e.tensor_tensor(out=Pnm, in0=pnew[:, :].unsqueeze(1).broadcast_to([NS, NS, 16]),
                                                  in1=identf[:NS, 0:NS].unsqueeze(2).broadcast_to([NS, NS, 16]), op=ALU.mult),
                 ["pnew", "identf"], ["Pnm"])
            OT = banks[2][0:64, 0:NS * 16]
            for n in range(NS):
                vc = Kc[n % 2]
                ld("sp", vc, cv[n], ["Kc%d" % (n % 2)], "kc%d" % (n % 2))
                for j in range(4):
                    oap = banks[2][0:64, n * 16 + 4 * j:n * 16 + 4 * j + 4]
                    P.op("pe", lambda e, n=n, j=j, vc=vc, oap=oap: e.matmul(oap, lhsT=vc[:, j * 64:(j + 1) * 64], rhs=P_all[:, n, 4 * j:4 * j + 4], start=True, stop=False),
                         ["Kc%d" % (n % 2), "P_all"], [bk(2)])
                    P.op("pe", lambda e, n=n, j=j, oap=oap: e.matmul(oap, lhsT=vfp[:NS, j * 64:(j + 1) * 64], rhs=Pnm[:, n, 4 * j:4 * j + 4], start=False, stop=True),
                         ["vfp", "Pnm"], [bk(2)])
            DEN = banks[3][0:64, 0:NS * 16]
            P.op("pe", lambda e: e.matmul(DEN, lhsT=onesf[:, 0:64], rhs=P_all[:, :, :].rearrange("p n h -> p (n h)"), start=True, stop=False), ["onesf", "P_all"], [bk(3)])
            P.op("pe", lambda e: e.matmul(DEN, lhsT=onesf[:NS, 0:64], rhs=Pnm[:, :, :].rearrange("p n h -> p (n h)"), start=False, stop=False), ["onesf", "Pnm"], [bk(3)])
            P.op("pe", lambda e: e.matmul(DEN, lhsT=onesf[0:1, 0:64], rhs=esr[:, :, :].rearrange("p n h -> p (n h)"), start=False, stop=True), ["onesf", "esr"], [bk(3)])
            P.op("dve", lambda e: e.reciprocal(out=rden, in_=DEN), [bk(3)], ["rden"])
            P.op("dve", lambda e: e.tensor_tensor(out=oTs, in0=OT.rearrange("p (n h) -> p h n", h=16), in1=rden[:, :].rearrange("p (n h) -> p h n", h=16), op=ALU.mult),
                 [bk(2), "rden"], ["oTs"])
            for hf in range(2):
                for h in range(16):
                    P.op("pe", lambda e, h=h, hf=hf: e.matmul(banks[hf][:NS, :], lhsT=oTs[:, h, :], rhs=WO2[:, h, hf * 512:(hf + 1) * 512], start=(h == 0), stop=False),
                         ["oTs", "WO2"], [bk(hf)])
                P.op("pe", lambda e, hf=hf: e.matmul(banks[hf][:NS, :], lhsT=onesb[0:1, :NS], rhs=bob[0:1, hf * 512:(hf + 1) * 512], start=False, stop=True),
                     ["onesb", "bob"], [bk(hf)])
            postnorm_residual(tn, [0, 1], 1, H[:tn, t, :], Hres(t), H[:tn, t, :], Hres(t), stmp, "stmp")

        swa_A(NT - 1, kv_only=True, slot=3)
        cut(2.1)
        P.op("sp", lambda e: e.dma_start(out=ag2_in.ap()[:, 0:256], in_=qkf[:, 1024:1280]), ["qkf"], ["ag2_in"], dma="ag2w0")
        P.op("sp", lambda e: e.dma_start(out=ag2_in.ap()[:, 256:512], in_=vfp[:, :]), ["vfp"], ["ag2_in"], dma="ag2w1")
        P.op("sp", lambda e: e.dma_start(out=kp, in_=qkf[:, 1024:1280]), ["qkf"], [], dma="kvo0")
        P.op("sp", lambda e: e.dma_start(out=vp, in_=vfp[:, :]), ["vfp"], [], dma="kvo1")
        P.op("pool", lambda e: e.collective_compute("AllGather", ALU.bypass, replica_groups=groups,
                                                    ins=[ag2_in.ap().opt()], outs=[ag2_out.ap().opt()]),
             ["ag2_in"], ["ag2_out"], dma="cc2", inc=1)
        P.op("sp", lambda e: e.dma_start(out=hg, in_=ag2_out.ap().rearrange("(r p) n -> p r n", p=128)), ["ag2_out"], ["hg"], dma="ag2r")
        cut(2.15)
        swa_A(0)
        if STAGE <= 2.2:
            P.barrier()
            for t in range(NT):
                P.op("sp", lambda e, t=t: e.dma_start(out=yp[t * 128:(t + 1) * 128, :], in_=H[:, t, :]), [Hres(t)], [], dma="dbg")
            P.op("sp", lambda e: e.dma_start(out=ys, in_=H[:NS, NT, :]), [Hres(NT)], [], dma="dbg")
            P.emit()
            return nc
        if NT > 1:
            swa_A(1)
        for t in range(1, NT):
            P.capture()
            swa_B(t)
            A_ = P.end_capture()
            B_ = []
            if t + 1 < NT:
                P.capture()
                swa_A(t + 1)
                B_ = P.end_capture()
            P.commit_merged(A_, B_)
        P.op("dve", lambda e: e.tensor_scalar(out=hk, in0=hg[:, 0, :], scalar1=selprev[:, 0:1], scalar2=None, op0=ALU.mult), ["hg", "selprev"], ["hk"])
        for j in range(1, 4):
            P.op("dve", lambda e, j=j: e.scalar_tensor_tensor(out=hk, in0=hg[:, j, :], scalar=selprev[:, j:j + 1], in1=hk, op0=ALU.mult, op1=ALU.add),
                 ["hg", "selprev", "hk"], ["hk"])
        P.op("pool", lambda e: e.tensor_copy(out=qkb[:, 1024:1280], in_=hk[:, 0:256]), ["hk"], ["qkb"])
        kv_finish(3, qkb[:, 1024:1280], "qkb", hk[:, 256:512], "hk", 128)
        swa_B(0)
        P.barrier()
        if STAGE <= 2.5:
            for t in range(NT):
                P.op("sp", lambda e, t=t: e.dma_start(out=yp[t * 128:(t + 1) * 128, :], in_=H[:, t, :]), [Hres(t)], [], dma="dbg")
            P.op("sp", lambda e: e.dma_start(out=ys, in_=H[:NS, NT, :]), [Hres(NT)], [], dma="dbg")
            P.emit()
            return nc
        swa_sample()

        if STAGE <= 3:
            for t in range(NT):
                P.op("sp", lambda e, t=t: e.dma_start(out=yp[t * 128:(t + 1) * 128, :], in_=H[:, t, :]), [Hres(t)], [], dma="dbg")
            P.op("sp", lambda e: e.dma_start(out=ys, in_=H[:NS, NT, :]), [Hres(NT)], [], dma="dbg")
            P.emit()
            return nc

        ffn_layer(1, True)
        P.emit()
    return nc


_CACHE = {}


def _consts(NT, c):
    r = np.arange(128)
    ident = np.eye(128, dtype=np.float32)
    triI = (r[:, None] <= r[None, :]).astype(np.float32)
    triS = (r[:, None] > r[None, :]).astype(np.float32)
    mge = (r[:, None] >= r[None, :]).astype(np.float32)
    pm0 = mge if (c % 4) != 0 else np.zeros((128, 128), np.float32)
    eye16 = np.broadcast_to(np.eye(16, dtype=np.float32), (128, 16, 16)).copy()
    half = 8
    inv = np.power(np.float32(500000.0), -np.arange(half, dtype=np.float32) * np.float32(2.0) / np.float32(16)).astype(np.float32)
    tok0 = (c % 4) * NT * 128
    pos = np.concatenate([tok0 + np.arange(NT * 128), np.full(128, PAST_LEN)]).astype(np.float32)
    ang = pos[:, None] * inv[None, :]
    cos = np.cos(ang).astype(np.float32)
    sin = np.sin(ang).astype(np.float32)
    cm = np.zeros((128, 4), np.float32)
    sp = np.zeros((128, 4), np.float32)
    for j in range(4):
        if j < (c % 4):
            cm[:, j] = 1.0
        if j == (c % 4) - 1:
            sp[:, j] = 1.0
    return dict(c_ident=ident, c_triI=triI, c_triS=triS, c_mge=mge, c_pm0=pm0, c_eye16=eye16, c_cos=cos, c_sin=sin,
                c_cmask=cm, c_selprev=sp)


def kernel(x_prompt, x_sample, state_gla, cache_swa_k, cache_swa_v,
           gla_w_in, gla_w_gate2, gla_b_gate, gla_g_head, gla_w_out,
           swa_w_qkv, swa_b_qkv, swa_sinks, swa_w_out, swa_b_out,
           norm_mix_pre, norm_mix_post, norm_ffn_pre, norm_ffn_post,
           ffn_w_up, ffn_w_down):
    f = lambda a: np.ascontiguousarray(np.asarray(a, dtype=np.float32))
    B, L, _ = x_prompt.shape
    NT = (B * L) // (NCORES * 128)
    key = (NT, STAGE)
    if key not in _CACHE:
        _CACHE[key] = build(NT)
    nc = _CACHE[key]
    xpf = f(x_prompt).reshape(B * L, D)
    xsf = f(x_sample).reshape(-1, D)
    shared = dict(
        w_in=f(gla_w_in[0]), wg2=f(gla_w_gate2[0]), bg=f(gla_b_gate[0]).reshape(1, 512), ghead=f(gla_g_head[0]),
        w_out0=f(gla_w_out[0]), wqkv=f(swa_w_qkv[0]), bqkv=f(swa_b_qkv[0]).reshape(1, 1536), sinks=f(swa_sinks[0]),
        wo1=f(swa_w_out[0]), bo1=f(swa_b_out[0]).reshape(1, D),
        nmp=f(norm_mix_pre), nmpost=f(norm_mix_post), nfp=f(norm_ffn_pre), nfpost=f(norm_ffn_post),
        wup=f(ffn_w_up), wdn=f(ffn_w_down))
    sg = f(state_gla[0]); ckf = f(cache_swa_k[0]).reshape(128, 128, 256); cvf = f(cache_swa_v[0]).reshape(128, 128, 256)
    in_maps = []
    for c in range(NCORES):
        m = dict(shared)
        m.update(_consts(NT, c))
        m["xp"] = xpf[c * NT * 128:(c + 1) * NT * 128]
        m["xs"] = xsf[c * NS:(c + 1) * NS]
        m["sgla"] = sg[c * NS:(c + 1) * NS]
        m["ck"] = ckf[c * NS:(c + 1) * NS]
        m["cv"] = cvf[c * NS:(c + 1) * NS]
        in_maps.append(m)
    res = run_bass_kernel_spmd(nc, in_maps, core_ids=list(range(NCORES)))
    R = res.results
    y_p = np.concatenate([R[c]["yp"] for c in range(NCORES)], 0).reshape(B, L, D)
    y_s = np.concatenate([R[c]["ys"] for c in range(NCORES)], 0).reshape(-1, 1, D)
    g_p = np.stack([R[3]["gp"], R[7]["gp"]], 0)[None]
    g_s = np.concatenate([R[c]["gs"] for c in range(NCORES)], 0)[None]
    k_p = np.stack([R[3]["kp"], R[7]["kp"]], 0).reshape(1, B, 128, 4, 64)
    v_p = np.stack([R[3]["vp"], R[7]["vp"]], 0).reshape(1, B, 128, 4, 64)
    k_s = np.concatenate([R[c]["ks"] for c in range(NCORES)], 0).reshape(1, -1, 128, 4, 64)
    v_s = np.concatenate([R[c]["vs"] for c in range(NCORES)], 0).reshape(1, -1, 128, 4, 64)
    return (y_p, y_s, g_p, g_s, k_p, v_p, k_s, v_s)
```
